# Optimizing a Trainium2 kernel written in Bass

```python
import jax, jax.numpy as jnp
from jax import lax
import numpy as np

D_MODEL = 2048
BATCH = 8
SEQ = 2048
DEPTH = 1
DEC_BATCH = 8
DEC_SEQ = 64
PAST_LEN = 1024

CHUNK = 64
Q_BLOCK = 128
FOX_HEADS = 12
FOX_HEAD_DIM = 128
FOX_WIDTH = FOX_HEADS * FOX_HEAD_DIM
CONV_WIDTH = 1536
CONV_K = 3
MEM_TOKENS = 256
MEM_HEADS = 4
MEM_HEAD_DIM = 256
MEM_WIDTH = MEM_HEADS * MEM_HEAD_DIM
N_BRANCH = 3
LN_EPS = 1e-5
NEG_INF = -1e30
DN_ALPHA = (2 * DEPTH) ** 0.25
DN_BETA = (8 * DEPTH) ** -0.25
IN_SIZES = (FOX_WIDTH, FOX_WIDTH, FOX_WIDTH, FOX_HEADS, FOX_WIDTH,
            CONV_WIDTH, CONV_WIDTH, CONV_WIDTH, CONV_WIDTH, MEM_WIDTH, MEM_WIDTH)
IN_WIDTH = 4 * FOX_WIDTH + FOX_HEADS + 4 * CONV_WIDTH + 2 * MEM_WIDTH

kernel_name = "fox_shortconv_memxattn_gated_hybrid_step"


def _layernorm(x, g, b):
    xf = x.astype(jnp.float32)
    mu = jnp.mean(xf, axis=-1, keepdims=True)
    var = jnp.mean(jnp.square(xf - mu), axis=-1, keepdims=True)
    y = (xf - mu) * lax.rsqrt(var + LN_EPS) * g.astype(jnp.float32) + b.astype(jnp.float32)
    return y.astype(x.dtype)


def _project(x, w_in, fox_bf):
    B, T, _ = x.shape
    z = jnp.einsum('btd,dn->btn', x, w_in)
    offs = tuple(int(o) for o in np.cumsum(IN_SIZES)[:-1])
    q, k, v, f, g_fox, b_gate, c_gate, h_in, g_conv, mq, g_mem = jnp.split(z, offs, axis=-1)
    q = q.reshape(B, T, FOX_HEADS, FOX_HEAD_DIM)
    k = k.reshape(B, T, FOX_HEADS, FOX_HEAD_DIM)
    v = v.reshape(B, T, FOX_HEADS, FOX_HEAD_DIM)
    logf = jax.nn.log_sigmoid(f.astype(jnp.float32) + fox_bf.astype(jnp.float32))
    u = c_gate * h_in
    mq = mq.reshape(B, T, MEM_HEADS, MEM_HEAD_DIM)
    return q, k, v, logf, g_fox, b_gate, u, g_conv, mq, g_mem


def _fox_prompt(q, k, v, logf):
    B, S, H, Dh = q.shape
    nb = S // Q_BLOCK
    scale = Dh ** -0.5
    cum = jnp.transpose(jnp.cumsum(logf, axis=1), (0, 2, 1))
    k_pos = jnp.arange(S)
    qb = jnp.moveaxis(q.reshape(B, nb, Q_BLOCK, H, Dh), 1, 0)
    cb = jnp.moveaxis(cum.reshape(B, H, nb, Q_BLOCK), 2, 0)
    pb = k_pos.reshape(nb, Q_BLOCK)

    def block(args):
        qi, ci, pi = args
        s = jnp.einsum('bqhd,bkhd->bhqk', qi, k).astype(jnp.float32) * scale
        s = s + ci[..., :, None] - cum[:, :, None, :]
        s = jnp.where(k_pos[None, :] <= pi[:, None], s, NEG_INF)
        p = jax.nn.softmax(s, axis=-1).astype(v.dtype)
        return jnp.einsum('bhqk,bkhd->bqhd', p, v)

    o = lax.map(block, (qb, cb, pb))
    return jnp.moveaxis(o, 0, 1).reshape(B, S, H * Dh)


def _fox_sample(q, k_new, v_new, logf_new, k_cache, v_cache, logf_cache):
    B, T, H, Dh = q.shape
    P = k_cache.shape[1]
    scale = Dh ** -0.5
    k = jnp.concatenate([k_cache.astype(k_new.dtype), k_new], axis=1)
    v = jnp.concatenate([v_cache.astype(v_new.dtype), v_new], axis=1)
    logf = jnp.concatenate([logf_cache.astype(jnp.float32), logf_new], axis=1)
    cum = jnp.transpose(jnp.cumsum(logf, axis=1), (0, 2, 1))
    s = jnp.einsum('bqhd,bkhd->bhqk', q, k).astype(jnp.float32) * scale
    s = s + cum[:, :, P:, None] - cum[:, :, None, :]
    mask = jnp.arange(P + T)[None, :] <= (P + jnp.arange(T))[:, None]
    s = jnp.where(mask, s, NEG_INF)
    p = jax.nn.softmax(s, axis=-1).astype(v.dtype)
    return jnp.einsum('bhqk,bkhd->bqhd', p, v).reshape(B, T, H * Dh)


def _short_conv(u, prev, w, b):
    T = u.shape[1]
    up = jnp.concatenate([prev.astype(u.dtype), u], axis=1)
    y = sum(w[j] * up[:, j:j + T] for j in range(CONV_K)) + b
    return y, up[:, T:]


def _mem_kv(mem, w_mem_kv):
    B, M, _ = mem.shape
    kv = jnp.einsum('bmd,dn->bmn', mem, w_mem_kv)
    mk, mv = jnp.split(kv, 2, axis=-1)
    return mk.reshape(B, M, MEM_HEADS, MEM_HEAD_DIM), mv.reshape(B, M, MEM_HEADS, MEM_HEAD_DIM)


def _mem_attn(q, mk, mv):
    B, T, _, _ = q.shape
    s = jnp.einsum('bthd,bmhd->bhtm', q, mk.astype(q.dtype)).astype(jnp.float32) * (MEM_HEAD_DIM ** -0.5)
    p = jax.nn.softmax(s, axis=-1).astype(q.dtype)
    return jnp.einsum('bhtm,bmhd->bthd', p, mv.astype(q.dtype)).reshape(B, T, MEM_WIDTH)


def _combine(x, o_fox, g_fox, o_conv, g_conv, o_mem, g_mem,
             w_fox_out, w_conv_out, w_mem_out, w_merge, b_merge, w_o, ln_g, ln_b):
    y_fox = jnp.einsum('btc,cd->btd', o_fox * jax.nn.silu(g_fox), w_fox_out)
    y_conv = jnp.einsum('btc,cd->btd', o_conv * jax.nn.silu(g_conv), w_conv_out)
    y_mem = jnp.einsum('btc,cd->btd', o_mem * jax.nn.silu(g_mem), w_mem_out)
    gates = jax.nn.sigmoid(jnp.einsum('btd,dn->btn', x, w_merge) + b_merge)
    g1, g2, g3 = jnp.split(gates, N_BRANCH, axis=-1)
    m = g1 * y_fox + g2 * y_conv + g3 * y_mem
    h = jnp.einsum('btd,de->bte', m, w_o)
    return _layernorm(DN_ALPHA * x + h, ln_g, ln_b)


def setup_inputs(seed: int = 0) -> dict:
    key = jax.random.key(seed)
    ks = jax.random.split(key, 24)
    f32 = jnp.float32
    L = DEPTH

    def nrm(k, shape, scale):
        return jax.random.normal(k, shape, f32) * scale

    return {
        "x_prompt": nrm(ks[0], (BATCH, SEQ, D_MODEL), 1.0),
        "x_sample": nrm(ks[1], (DEC_BATCH, DEC_SEQ, D_MODEL), 1.0),
        "mem_prompt": nrm(ks[2], (BATCH, MEM_TOKENS, D_MODEL), 1.0),
        "cache_fox_k": nrm(ks[3], (L, DEC_BATCH, PAST_LEN, FOX_HEADS, FOX_HEAD_DIM), 1.0),
        "cache_fox_v": nrm(ks[4], (L, DEC_BATCH, PAST_LEN, FOX_HEADS, FOX_HEAD_DIM), 1.0),
        "cache_fox_logf": jax.nn.log_sigmoid(nrm(ks[5], (L, DEC_BATCH, PAST_LEN, FOX_HEADS), 1.0) + 3.0),
        "state_conv": nrm(ks[6], (L, DEC_BATCH, CONV_K - 1, CONV_WIDTH), 1.0),
        "cache_mem_k": nrm(ks[7], (L, DEC_BATCH, MEM_TOKENS, MEM_HEADS, MEM_HEAD_DIM), 1.0),
        "cache_mem_v": nrm(ks[8], (L, DEC_BATCH, MEM_TOKENS, MEM_HEADS, MEM_HEAD_DIM), 1.0),
        "w_in": nrm(ks[9], (L, D_MODEL, IN_WIDTH), D_MODEL ** -0.5),
        "fox_bf": jnp.linspace(1.0, 4.0, FOX_HEADS)[None, :] + nrm(ks[10], (L, FOX_HEADS), 0.1),
        "conv_w": nrm(ks[11], (L, CONV_K, CONV_WIDTH), CONV_K ** -0.5),
        "conv_b": nrm(ks[12], (L, CONV_WIDTH), 0.01),
        "w_mem_kv": nrm(ks[13], (L, D_MODEL, 2 * MEM_WIDTH), D_MODEL ** -0.5),
        "w_fox_out": nrm(ks[14], (L, FOX_WIDTH, D_MODEL), FOX_WIDTH ** -0.5 * DN_BETA),
        "w_conv_out": nrm(ks[15], (L, CONV_WIDTH, D_MODEL), CONV_WIDTH ** -0.5 * DN_BETA),
        "w_mem_out": nrm(ks[16], (L, MEM_WIDTH, D_MODEL), MEM_WIDTH ** -0.5 * DN_BETA),
        "w_merge": nrm(ks[17], (L, D_MODEL, N_BRANCH * D_MODEL), D_MODEL ** -0.5),
        "b_merge": nrm(ks[18], (L, N_BRANCH * D_MODEL), 0.01),
        "w_o": nrm(ks[19], (L, D_MODEL, D_MODEL), D_MODEL ** -0.5 * DN_BETA),
        "ln_g": 1.0 + nrm(ks[20], (L, D_MODEL), 0.01),
        "ln_b": nrm(ks[21], (L, D_MODEL), 0.01),
    }


def reference(x_prompt, x_sample, mem_prompt, cache_fox_k, cache_fox_v, cache_fox_logf, state_conv,
              cache_mem_k, cache_mem_v, w_in, fox_bf, conv_w, conv_b, w_mem_kv, w_fox_out, w_conv_out,
              w_mem_out, w_merge, b_merge, w_o, ln_g, ln_b):
    hp, hs = x_prompt, x_sample
    fk_p, fv_p, fl_p, cv_p, mk_p, mv_p = [], [], [], [], [], []
    fk_s, fv_s, fl_s, cv_s = [], [], [], []
    for l in range(DEPTH):
        q, k, v, logf, g_fox, b_gate, u, g_conv, mq, g_mem = _project(hp, w_in[l], fox_bf[l])
        o_fox = _fox_prompt(q, k, v, logf)
        prev0 = jnp.zeros((hp.shape[0], CONV_K - 1, CONV_WIDTH), hp.dtype)
        c, tail = _short_conv(u, prev0, conv_w[l], conv_b[l])
        mk, mv = _mem_kv(mem_prompt, w_mem_kv[l])
        o_mem = _mem_attn(mq, mk, mv)
        hp_next = _combine(hp, o_fox, g_fox, b_gate * c, g_conv, o_mem, g_mem,
                           w_fox_out[l], w_conv_out[l], w_mem_out[l], w_merge[l], b_merge[l], w_o[l],
                           ln_g[l], ln_b[l])
        fk_p.append(k); fv_p.append(v); fl_p.append(logf); cv_p.append(tail)
        mk_p.append(mk); mv_p.append(mv)
        hp = hp_next
        q, k, v, logf, g_fox, b_gate, u, g_conv, mq, g_mem = _project(hs, w_in[l], fox_bf[l])
        o_fox = _fox_sample(q, k, v, logf, cache_fox_k[l], cache_fox_v[l], cache_fox_logf[l])
        c, tail = _short_conv(u, state_conv[l], conv_w[l], conv_b[l])
        o_mem = _mem_attn(mq, cache_mem_k[l], cache_mem_v[l])
        hs_next = _combine(hs, o_fox, g_fox, b_gate * c, g_conv, o_mem, g_mem,
                           w_fox_out[l], w_conv_out[l], w_mem_out[l], w_merge[l], b_merge[l], w_o[l],
                           ln_g[l], ln_b[l])
        fk_s.append(k); fv_s.append(v); fl_s.append(logf); cv_s.append(tail)
        hs = hs_next
    return (hp, hs,
            jnp.stack(fk_p), jnp.stack(fv_p), jnp.stack(fl_p), jnp.stack(cv_p),
            jnp.stack(mk_p), jnp.stack(mv_p),
            jnp.stack(fk_s), jnp.stack(fv_s), jnp.stack(fl_s), jnp.stack(cv_s))
```

```python
import numpy as np
import ml_dtypes
import concourse.bass as bass
import concourse.mybir as mybir
from concourse.bass_utils import run_bass_kernel_spmd

F32 = mybir.dt.float32
BF16 = mybir.dt.bfloat16
ALU = mybir.AluOpType
AF = mybir.ActivationFunctionType

PE, ACT, DVE, POOL, SP = "pe", "act", "dve", "pool", "sp"
ENGS = [PE, ACT, DVE, POOL, SP]


class Res:
    __slots__ = ("name", "w", "r")

    def __init__(self, name=""):
        self.name = name
        self.w = None
        self.r = {}


class DmaSem:
    __slots__ = ("sem", "count", "name")

    def __init__(self, name):
        self.name = name
        self.sem = None
        self.count = 0


class Op:
    __slots__ = ("eng", "fn", "deps", "dmasem", "marked", "val")

    def __init__(self, eng, fn):
        self.eng = eng
        self.fn = fn
        self.deps = []
        self.dmasem = None
        self.marked = False
        self.val = 0


class Sched:
    def __init__(self, nc, sync_same_engine=True):
        self.nc = nc
        self.ops = {e: [] for e in ENGS}
        self.dmasems = []
        self.sync_same = sync_same_engine

    def dmasem(self, name):
        d = DmaSem(name)
        self.dmasems.append(d)
        return d

    def _dep(self, op, prod):
        if prod is None or prod is op:
            return
        if prod.dmasem is not None:
            if op.dmasem is prod.dmasem:
                return
            op.deps.append((prod, prod.dmasem.count))
            return
        if prod.eng == op.eng and op.dmasem is None:
            if prod.eng == PE or not self.sync_same:
                return
        op.deps.append((prod, None))
        prod.marked = True

    def add(self, eng, fn, reads=(), writes=(), dmasem=None):
        op = Op(eng, fn)
        op.dmasem = dmasem
        for r in reads:
            self._dep(op, r.w)
        for w in writes:
            self._dep(op, w.w)
            for rr in w.r.values():
                self._dep(op, rr)
        if dmasem is not None:
            dmasem.count += 16
        key = dmasem if dmasem is not None else eng
        for r in reads:
            r.r[key] = op
        for w in writes:
            w.w = op
            w.r = {}
        self.ops[eng].append(op)
        return op

    def emit(self, block, es):
        nc = self.nc
        for e in ENGS:
            c = 0
            for op in self.ops[e]:
                if op.dmasem is None and op.marked:
                    c += 1
                    op.val = c
        esem = {e: es.enter_context(nc.semaphore("prog_" + e)) for e in [PE, ACT, DVE, POOL]}
        for d in self.dmasems:
            d.sem = es.enter_context(nc.semaphore("dma_" + d.name))

        def run(e, engobj):
            waited = {}
            for op in self.ops[e]:
                need = {}
                for (p, dv) in op.deps:
                    if p.dmasem is not None:
                        key, v = p.dmasem.sem, dv
                    else:
                        key, v = esem[p.eng], p.val
                    if v > need.get(key, 0):
                        need[key] = v
                for key, v in need.items():
                    if waited.get(key, 0) < v:
                        engobj.wait_ge(key, v)
                        waited[key] = v
                ins = op.fn(engobj)
                if op.dmasem is not None:
                    ins.then_inc(op.dmasem.sem, 16)
                elif op.marked:
                    ins.then_inc(esem[e], 1)
            if e == SP:
                for d in self.dmasems:
                    if d.count > 0:
                        engobj.wait_ge(d.sem, d.count)

        @block.tensor
        def _(eng):
            run(PE, eng)

        @block.scalar
        def _(eng):
            run(ACT, eng)

        @block.vector
        def _(eng):
            run(DVE, eng)

        @block.gpsimd
        def _(eng):
            run(POOL, eng)

        @block.sync
        def _(eng):
            run(SP, eng)


def MM(out, lhsT, rhs, start, stop):
    return lambda e: e.matmul(out, lhsT=lhsT, rhs=rhs, start=start, stop=stop)


def TR(out, in_, ident):
    return lambda e: e.transpose(out=out, in_=in_, identity=ident)


def DMA(out, in_, **kw):
    return lambda e: e.dma_start(out=out, in_=in_, **kw)


def ACTF(out, in_, func, bias=None, scale=None):
    kw = {}
    if bias is not None:
        kw["bias"] = bias
    if scale is not None:
        kw["scale"] = scale
    return lambda e: e.activation(out=out, in_=in_, func=func, **kw)


def COPY(out, in_):
    def f(e):
        if hasattr(e, "activation"):
            return e.activation(out=out, in_=in_, func=AF.Copy)
        return e.tensor_copy(out=out, in_=in_)
    return f


def TT(out, in0, in1, op):
    return lambda e: e.tensor_tensor(out=out, in0=in0, in1=in1, op=op)


def TS(out, in0, s1, op0, s2=None, op1=None):
    if op1 is None:
        return lambda e: e.tensor_scalar(out=out, in0=in0, scalar1=s1, scalar2=None, op0=op0)
    return lambda e: e.tensor_scalar(out=out, in0=in0, scalar1=s1, scalar2=s2, op0=op0, op1=op1)


def STT(out, in0, scalar, in1, op0, op1):
    return lambda e: e.scalar_tensor_tensor(out=out, in0=in0, scalar=scalar, in1=in1, op0=op0, op1=op1)


D = 2048
NP = 2048
NS = 64
NT = NP + NS
PAST = 1024
H = 12
DH = 128
FW = 1536
MW = 1024
MT = 256
INW = 14348
OFF_Q, OFF_K, OFF_V, OFF_F, OFF_G = 0, 1536, 3072, 4608, 4620
OFF_CB, OFF_CC, OFF_CH, OFF_CG = 6156, 7692, 9228, 10764
OFF_MQ, OFF_MG = 12300, 13324
TBLK = [(0, 512), (512, 512), (1024, 512), (1536, 512), (2048, 64)]
TCH = [(i * 128, 128) for i in range(16)] + [(2048, 64)]
SCALE = DH ** -0.5
MSCALE = 256 ** -0.5
DN_ALPHA = 2.0 ** 0.25
LN_EPS = 1e-5
NEG = -1.0e30

OUT_SPECS = [
    ("y_p", [NP, D]), ("y_s", [NS, D]), ("fk_p", [NP, FW]), ("fv_p", [NP, FW]), ("fl_p", [NP, H]),
    ("cv_p", [2, FW]), ("mk_p", [MT, MW]), ("mv_p", [MT, MW]), ("fk_s", [NS, FW]), ("fv_s", [NS, FW]),
    ("fl_s", [NS, H]), ("cv_s", [2, FW]),
]
IN_SPECS = [
    ("xp", [NP, D], F32), ("xs", [NS, D], F32), ("mem", [MT, D], F32), ("ck", [PAST, FW], F32),
    ("cv", [PAST, FW], F32), ("clf", [PAST, H], F32), ("sconv", [2, FW], F32), ("cmk", [MT, MW], F32),
    ("cmv", [MT, MW], F32), ("w_in", [D, INW], F32), ("fox_bf", [H, 1], F32), ("conv_w", [3, FW], F32),
    ("conv_b", [FW, 1], F32), ("w_mem_kv", [D, 2 * MW], F32), ("w_fox_out", [FW, D], F32),
    ("w_conv_out", [FW, D], F32), ("w_mem_out", [MW, D], F32), ("w_merge", [D, 3 * D], F32),
    ("b_merge", [3 * D, 1], F32), ("w_o", [D, D], F32), ("ln_g", [1, D], F32), ("ln_b", [1, D], F32),
    ("identb", [128, 128], BF16), ("identf", [128, 128], F32), ("onesb", [128, 128], BF16),
    ("maskneg", [128, 128], F32), ("masknegb", [128, 128], BF16),
]


class SbAlloc:
    def __init__(self, nc, lo=16512, hi=229376 - 2048):
        self.nc = nc
        self.lo = lo
        self.hi = hi
        self.top = lo
        self.n = 0
        self.live = []

    def mark(self):
        return self.top

    def mark_hi(self):
        return self.hi

    def release_hi(self, m):
        self.hi = m

    def release(self, mark):
        self.top = mark

    def alloc(self, name, shape, dt, res=(), high=False):
        nb = int(np.prod(shape[1:])) * mybir.dt.size(dt)
        nb = (nb + 63) // 64 * 64
        if high:
            assert self.hi - nb >= self.top, f"SBUF overflow allocating {name} (high)"
            self.hi -= nb
            off = self.hi
        else:
            off = self.top
            assert off + nb <= self.hi, f"SBUF overflow allocating {name}: {off + nb} > {self.hi}"
            self.top += nb
        self.n += 1
        t = self.nc.alloc_sbuf_tensor_at(f"{name}_{self.n}", list(shape), dt, offset=off)
        inherited = {}
        k = 0
        for (o, e, rl) in self.live:
            if o < off + nb and off < e:
                for r0 in rl:
                    if r0.w is not None:
                        inherited[("w", k)] = r0.w
                        k += 1
                    for rr in r0.r.values():
                        inherited[("r", k)] = rr
                        k += 1
        for r1 in res:
            r1.r.update(inherited)
        self.live = [(o, e, rl) for (o, e, rl) in self.live if not (o >= off and e <= off + nb)]
        self.live.append((off, off + nb, list(res)))
        return t


def build(debug=None, upto=99):
    nc = bass.Bass("TRN2", target_bir_lowering=False)
    I = {n: nc.dram_tensor(n, s, dt, kind="ExternalInput").ap() for (n, s, dt) in IN_SPECS}
    O = {n: nc.dram_tensor(n, s, F32, kind="ExternalOutput").ap() for (n, s) in OUT_SPECS}
    cum_scr = nc.dram_tensor("cum_scr", [H, NP + PAST + NS], F32).ap()
    m_scr = nc.dram_tensor("m_scr", [16, 128, NT], F32).ap()
    m_scrb = nc.dram_tensor("m_scrb", [16, 128, NT], BF16).ap()
    dbg = None
    if debug is not None:
        dbg = nc.dram_tensor("dbg", list(debug), F32, kind="ExternalOutput").ap()

    nc.alloc_sbuf_tensor("arena", [128, 229376 - 16512 - 64], mybir.dt.uint8)
    A = SbAlloc(nc)
    S = Sched(nc)
    banks = [nc.alloc_psum_tensor(f"bank{i}", [128, 512], F32) for i in range(8)]
    bankr = [Res(f"bank{i}") for i in range(8)]

    r_const = Res("const")
    identb = A.alloc("identb", [128, 128], BF16, [r_const])
    identf = A.alloc("identf", [128, 128], F32, [r_const])
    onesb = A.alloc("onesb", [128, 128], BF16, [r_const])
    maskneg = A.alloc("maskneg", [128, 128], F32, [r_const])
    masknegb = A.alloc("masknegb", [128, 128], BF16, [r_const])
    ones1 = A.alloc("ones1", [128, 1], F32, [r_const])
    nbf = A.alloc("nbf", [H, 1], F32, [r_const])
    dconst = S.dmasem("const")
    for t, n in ((identb, "identb"), (identf, "identf"), (onesb, "onesb"), (maskneg, "maskneg"), (masknegb, "masknegb")):
        S.add(SP, DMA(t[:], I[n]), writes=[r_const], dmasem=dconst)
    S.add(SP, DMA(nbf[:], I["fox_bf"]), writes=[r_const], dmasem=dconst)
    S.add(DVE, lambda e: e.memset(ones1[:], 1.0), writes=[r_const])
    S.add(DVE, TS(nbf[:], nbf[:], -1.0, ALU.mult), reads=[r_const], writes=[r_const])

    bm = A.alloc("bm", [128, 48], F32, [r_const])
    epsb = A.alloc("epsb", [128, 1], F32, [r_const])
    S.add(SP, DMA(bm[:], I["b_merge"].rearrange("(c p) o -> p (c o)", p=128), allow_slow_non_contiguous=True),
          writes=[r_const], dmasem=dconst)
    S.add(DVE, lambda e: e.memset(epsb[:], LN_EPS), writes=[r_const])
    r_ncc = Res("ncc")
    ncc = A.alloc("ncc", [128, 25, H], F32, [r_ncc])
    r_ut = Res("utail")
    utail = A.alloc("utail", [128, 2, 12], F32, [r_ut])
    ustail = A.alloc("ustail", [128, 2, 12], F32, [r_ut])
    mkX = A.mark()
    r_xT = [Res(f"xT{i}") for i in range(17)]
    xT = A.alloc("xT", [128, 16, NT], BF16, r_xT)
    def xT_res(t0, n):
        return [r_xT[i] for i, (c0, cn) in enumerate(TCH) if c0 < t0 + n and t0 < c0 + cn]

    mk0 = A.mark()
    r_xb = [Res("xb0"), Res("xb1")]
    xb = [A.alloc("xb0", [128, D], BF16, [r_xb[0]]), A.alloc("xb1", [128, D], BF16, [r_xb[1]])]
    d_xb = [S.dmasem("xb0"), S.dmasem("xb1")]
    for tc, (t0, tn) in enumerate(TCH):
        s = tc % 2
        src = I["xp"][t0:t0 + tn, :] if tc < 16 else I["xs"][:, :]
        S.add(POOL, DMA(xb[s][0:tn, :].rearrange("p (a b) -> p a b", b=1024),
                        src.rearrange("p (a b) -> p a b", b=1024)),
              writes=[r_xb[s]], dmasem=d_xb[s])
        for g in range(4):
            bk = (tc * 4 + g) % 2
            bv = banks[bk].bitcast(BF16)
            for j in range(4):
                c = g * 4 + j
                S.add(PE, TR(bv[:, j * 128:j * 128 + tn], xb[s][0:tn, c * 128:(c + 1) * 128], identb[0:tn, 0:tn]),
                      reads=[r_xb[s], r_const], writes=[bankr[bk]])
            eng = ACT if (tc * 4 + g) % 2 == 0 else DVE
            S.add(eng, COPY(xT[:, g * 4:g * 4 + 4, t0:t0 + tn],
                            bv[:, 0:512].rearrange("p (a b) -> p a b", b=128)[:, :, 0:tn]),
                  reads=[bankr[bk]], writes=[r_xT[tc]])
    A.release(mk0)

    mk1 = A.mark()
    r_pro = Res("pro")
    wfg = A.alloc("wfg", [128, 16, H], BF16, [r_pro])
    r_lf = Res("logfT")
    logfT = A.alloc("logfT", [H, NT], F32, [r_lf])
    r_lc = Res("lcT")
    lcT = A.alloc("lcT", [H, PAST + NS], F32, [r_lc])
    r_cum = Res("cum")
    cumP = A.alloc("cumP", [H, NP], F32, [r_cum])
    cumS = A.alloc("cumS", [H, PAST + NS], F32, [r_cum])
    r_et = [Res("et0"), Res("et1")]
    etmp = [A.alloc("et0", [H, 512], F32, [r_et[0]]), A.alloc("et1", [H, 512], F32, [r_et[1]])]
    r_clf = Res("clf")
    clf_tm = A.alloc("clf_tm", [128, 8, H], F32, [r_clf])
    r_lfcol = Res("lfcol")
    lfcol = A.alloc("lfcol", [128, 17, H], F32, [r_lfcol])
    d_pro = S.dmasem("pro")
    S.add(POOL, DMA(wfg[:], I["w_in"][:, OFF_F:OFF_F + H].rearrange("(c p) n -> p c n", p=128)),
          writes=[r_pro], dmasem=d_pro)
    d_clf = S.dmasem("clf")
    S.add(SP, DMA(clf_tm[:], I["clf"].rearrange("(c p) h -> p c h", p=128)), writes=[r_clf], dmasem=d_clf)
    for tb, (t0, tn) in enumerate(TBLK):
        bk = tb % 2
        for c in range(16):
            S.add(PE, MM(banks[bk][0:H, 0:tn], wfg[:, c, :], xT[:, c, t0:t0 + tn], c == 0, c == 15),
                  reads=[r_pro] + xT_res(t0, tn), writes=[bankr[bk]])
        S.add(ACT, ACTF(etmp[bk][:, 0:tn], banks[bk][0:H, 0:tn], AF.Exp, bias=nbf[:, 0:1], scale=-1.0),
              reads=[bankr[bk], r_const], writes=[r_et[bk]])
        S.add(ACT, ACTF(etmp[bk][:, 0:tn], etmp[bk][:, 0:tn], AF.Ln, bias=1.0),
              reads=[r_et[bk]], writes=[r_et[bk]])
        S.add(DVE, TS(logfT[:, t0:t0 + tn], etmp[bk][:, 0:tn], -1.0, ALU.mult),
              reads=[r_et[bk]], writes=[r_lf])
    for c in range(8):
        bk = 2 + c // 4
        S.add(PE, TR(banks[bk][0:H, (c % 4) * 128:(c % 4 + 1) * 128], clf_tm[:, c, :], identf[:, :]),
              reads=[r_clf, r_const], writes=[bankr[bk]])
        if c % 4 == 3:
            S.add(ACT, COPY(lcT[:, (c // 4) * 512:(c // 4 + 1) * 512], banks[bk][0:H, :]),
                  reads=[bankr[bk]], writes=[r_lc])
    S.add(DVE, COPY(lcT[:, PAST:PAST + NS], logfT[:, NP:NT]), reads=[r_lf], writes=[r_lc])

    def SCAN(out, data1, n):
        return lambda e: e.tensor_tensor_scan(out=out, data0=ones1[0:H, 0:1].broadcast_to([H, n]), data1=data1,
                                               initial=0.0, op0=ALU.mult, op1=ALU.add)
    S.add(DVE, SCAN(cumP[:], logfT[:, 0:NP], NP), reads=[r_lf, r_const], writes=[r_cum])
    S.add(DVE, SCAN(cumS[:], lcT[:], PAST + NS), reads=[r_lc, r_const], writes=[r_cum])
    r_cscr = Res("cum_scr")
    d_cscr = S.dmasem("cscr")
    S.add(SP, DMA(cum_scr[:, 0:NP], cumP[:]), reads=[r_cum], writes=[r_cscr], dmasem=d_cscr)
    S.add(SP, DMA(cum_scr[:, NP:NP + PAST + NS], cumS[:]), reads=[r_cum], writes=[r_cscr], dmasem=d_cscr)
    for tc, (t0, tn) in enumerate(TCH):
        S.add(PE, TR(banks[4][0:tn, tc * H:(tc + 1) * H], logfT[:, t0:t0 + tn], identf[0:H, 0:H]),
              reads=[r_lf, r_const], writes=[bankr[4]])
    S.add(ACT, COPY(lfcol[:, 0:16, :], banks[4][:, 0:16 * H].rearrange("p (a b) -> p a b", b=H)),
          reads=[bankr[4]], writes=[r_lfcol])
    S.add(ACT, COPY(lfcol[0:NS, 16, :], banks[4][0:NS, 16 * H:17 * H]), reads=[bankr[4]], writes=[r_lfcol])
    d_fl = S.dmasem("fl")
    S.add(SP, DMA(O["fl_p"].rearrange("(c p) h -> p c h", p=128), lfcol[:, 0:16, :]), reads=[r_lfcol], dmasem=d_fl)
    S.add(SP, DMA(O["fl_s"], lfcol[0:NS, 16, :]), reads=[r_lfcol], dmasem=d_fl)
    for c in range(16):
        S.add(PE, TR(banks[5][:, c * H:(c + 1) * H], cumP[:, c * 128:(c + 1) * 128], identf[0:H, 0:H]),
              reads=[r_cum, r_const], writes=[bankr[5]])
    for c in range(9):
        n = 128 if c < 8 else NS
        S.add(PE, TR(banks[5][0:n, (16 + c) * H:(17 + c) * H], cumS[:, c * 128:c * 128 + n], identf[0:H, 0:H]),
              reads=[r_cum, r_const], writes=[bankr[5]])
    S.add(DVE, TS(ncc[:, 0:24, :], banks[5][:, 0:24 * H].rearrange("p (a b) -> p a b", b=H), -1.0, ALU.mult),
          reads=[bankr[5]], writes=[r_ncc])
    S.add(DVE, TS(ncc[0:NS, 24, :], banks[5][0:NS, 24 * H:25 * H], -1.0, ALU.mult),
          reads=[bankr[5]], writes=[r_ncc])
    A.release(mk1)

    mk2 = A.mark()
    r_aT = [[Res(f"aT{h}_{tb}") for tb in range(5)] for h in range(H)]
    aT = A.alloc("aT", [128, H, NT], BF16, [r for l in r_aT for r in l])
    mkB = A.mark()
    r_w = [Res("wq"), Res("wkv"), Res("wg")]
    wbuf = A.alloc("wbuf", [128, 16, 512], BF16, r_w)
    d_w = [S.dmasem("wq"), S.dmasem("wkv"), S.dmasem("wg")]
    r_qT = [Res(f"qT{i}") for i in range(5)]
    qT = A.alloc("qT", [128, NT], BF16, r_qT)
    r_kT = [Res(f"kT{i}") for i in range(5)]
    kT = A.alloc("kT", [128, NT], BF16, r_kT)
    r_vb = [Res(f"vb{i}") for i in range(5)]
    vb = A.alloc("vb", [128, 17, 128], BF16, r_vb)
    r_sg = [Res(f"sg{i}") for i in range(5)]
    sg = A.alloc("sg", [128, NT], F32, r_sg)
    r_cqb = Res("cqb")
    cqb = A.alloc("cqb", [128, NT], F32, [r_cqb])
    d_cqb = S.dmasem("cqb")
    r_kv32 = [Res("kv32_0"), Res("kv32_1")]
    kv32 = [A.alloc("kv32_0", [128, 4, 256], F32, [r_kv32[0]]), A.alloc("kv32_1", [128, 4, 256], F32, [r_kv32[1]])]
    d_kvo = [S.dmasem("kvo0"), S.dmasem("kvo1")]
    r_kb = [Res("kb0"), Res("kb1")]
    kb = [A.alloc("kb0", [128, 4, 128], BF16, [r_kb[0]]), A.alloc("kb1", [128, 4, 128], BF16, [r_kb[1]])]
    r_cache = Res("cache")
    kc_tm = A.alloc("kc_tm", [128, 8, 128], BF16, [r_cache])
    vc_tm = A.alloc("vc_tm", [128, 8, 128], BF16, [r_cache])
    d_cache = S.dmasem("cache")
    r_kcT = Res("kcT")
    kcT = A.alloc("kcT", [128, PAST], BF16, [r_kcT])
    NST, NPT = 4, 6
    r_st = [Res(f"st{i}") for i in range(NST)]
    stmp = [A.alloc(f"st{i}", [128, 512], F32, [r_st[i]]) for i in range(NST)]
    r_pt = [Res(f"pt{i}") for i in range(NPT)]
    pt = [A.alloc(f"pt{i}", [128, 512], BF16, [r_pt[i]]) for i in range(NPT)]
    SRING = [2, 3, 0, 1]
    stk = [0]
    r_rl = Res("rl")
    rl = A.alloc("rl", [128, 512], F32, [r_rl])
    r_on = Res("on")
    on = A.alloc("on", [128, 512], F32, [r_on])

    def load_w(h):
        for comp, (off, slot) in enumerate(((OFF_Q, 0), (OFF_K, 1), (OFF_V, 1), (OFF_G, 2))):
            S.add(POOL, DMA(wbuf[:, :, comp * 128:(comp + 1) * 128],
                            I["w_in"][:, off + h * 128:off + (h + 1) * 128].rearrange("(c p) n -> p c n", p=128)),
                  writes=[r_w[slot]], dmasem=d_w[slot])

    def load_cache(h):
        S.add(POOL, DMA(kc_tm[:], I["ck"][:, h * 128:(h + 1) * 128].rearrange("(c p) n -> p c n", p=128)),
              writes=[r_cache], dmasem=d_cache)
        S.add(POOL, DMA(vc_tm[:], I["cv"][:, h * 128:(h + 1) * 128].rearrange("(c p) n -> p c n", p=128)),
              writes=[r_cache], dmasem=d_cache)

    def load_cqb(h):
        S.add(SP, DMA(cqb[:, 0:NP], cum_scr[h:h + 1, 0:NP].partition_broadcast(128)),
              reads=[r_cscr], writes=[r_cqb], dmasem=d_cqb)
        S.add(SP, DMA(cqb[:, NP:NT], cum_scr[h:h + 1, NP + PAST:NP + PAST + NS].partition_broadcast(128)),
              reads=[r_cscr], writes=[r_cqb], dmasem=d_cqb)

    pj = [0]
    sbk = [0]
    pti = [0]
    obk = [0]

    nheads = H if upto >= 2 else 1
    load_w(0)
    load_cache(0)
    load_cqb(0)
    for h in range(nheads):
        pending_kt = []
        for grp in range(5):
            chunks = [(tc, TCH[tc]) for tc in range(grp * 4, min(grp * 4 + 4, 17))]
            s = grp % 2
            for j, (tc, (t0, tn)) in enumerate(chunks):
                bk = pj[0]; pj[0] ^= 1
                for c in range(16):
                    S.add(PE, MM(banks[bk][0:tn, 0:256], xT[:, c, t0:t0 + tn],
                                 wbuf[:, c, 128:384], c == 0, c == 15),
                          reads=[r_w[1], r_xT[tc]], writes=[bankr[bk]])
                S.add(ACT, COPY(kv32[s][0:tn, j, :], banks[bk][0:tn, 0:256]),
                      reads=[bankr[bk]], writes=[r_kv32[s]])
            nj = len(chunks)
            tn = chunks[0][1][1]
            t0 = chunks[0][1][0]
            S.add(POOL, COPY(kb[s][0:tn, 0:nj, :], kv32[s][0:tn, 0:nj, 0:128]), reads=[r_kv32[s]], writes=[r_kb[s]])
            S.add(POOL, COPY(vb[0:tn, grp * 4:grp * 4 + nj, :], kv32[s][0:tn, 0:nj, 128:256]),
                  reads=[r_kv32[s]], writes=[r_vb[grp]])
            if grp < 4:
                S.add(SP, DMA(O["fk_p"][t0:t0 + 512, h * 128:(h + 1) * 128].rearrange("(c p) n -> p c n", p=128),
                              kv32[s][:, :, 0:128]), reads=[r_kv32[s]], dmasem=d_kvo[s])
                S.add(SP, DMA(O["fv_p"][t0:t0 + 512, h * 128:(h + 1) * 128].rearrange("(c p) n -> p c n", p=128),
                              kv32[s][:, :, 128:256]), reads=[r_kv32[s]], dmasem=d_kvo[s])
            else:
                S.add(SP, DMA(O["fk_s"][:, h * 128:(h + 1) * 128], kv32[s][0:NS, 0, 0:128]),
                      reads=[r_kv32[s]], dmasem=d_kvo[s])
                S.add(SP, DMA(O["fv_s"][:, h * 128:(h + 1) * 128], kv32[s][0:NS, 0, 128:256]),
                      reads=[r_kv32[s]], dmasem=d_kvo[s])
            def ktrans(grp=grp, s=s, nj=nj, tn=tn, t0=t0):
                bk = pj[0]; pj[0] ^= 1
                bv = banks[bk].bitcast(BF16)
                for j in range(nj):
                    S.add(PE, TR(bv[:, j * 128:j * 128 + tn], kb[s][0:tn, j, :], identb[0:tn, 0:tn]),
                          reads=[r_kb[s], r_const], writes=[bankr[bk]])
                S.add(ACT, COPY(kT[:, t0:t0 + nj * tn], bv[:, 0:nj * 128] if tn == 128 else bv[:, 0:tn]),
                      reads=[bankr[bk]], writes=[r_kT[grp]])
            if pending_kt:
                pending_kt.pop()()
            pending_kt.append(ktrans)
        for comp, dst, rr, slot in ((0, qT, r_qT, 0), (3, sg, r_sg, 2)):
            for tb, (t0, tn) in enumerate(TBLK):
                if tb == 1 and pending_kt:
                    pending_kt.pop()()
                bk = pj[0]; pj[0] ^= 1
                for c in range(16):
                    S.add(PE, MM(banks[bk][:, 0:tn], wbuf[:, c, comp * 128:(comp + 1) * 128], xT[:, c, t0:t0 + tn], c == 0, c == 15),
                          reads=[r_w[slot]] + xT_res(t0, tn), writes=[bankr[bk]])
                if comp == 0:
                    S.add(ACT, COPY(dst[:, t0:t0 + tn], banks[bk][:, 0:tn]), reads=[bankr[bk]], writes=[rr[tb]])
                else:
                    S.add(ACT, ACTF(dst[:, t0:t0 + tn], banks[bk][:, 0:tn], AF.Silu), reads=[bankr[bk]], writes=[rr[tb]])
        if h + 1 < nheads:
            load_w(h + 1)
        for half in range(2):
            bk = pj[0]; pj[0] ^= 1
            bv = banks[bk].bitcast(BF16)
            for j in range(4):
                S.add(PE, TR(bv[:, j * 128:(j + 1) * 128], kc_tm[:, half * 4 + j, :], identb[:, :]),
                      reads=[r_cache, r_const], writes=[bankr[bk]])
            S.add(ACT, COPY(kcT[:, half * 512:(half + 1) * 512], bv[:, 0:512]), reads=[bankr[bk]], writes=[r_kcT])

        tiles = []

        def attend(qcol0, qn, keys, tbidx):
            ob = obk[0]; obk[0] ^= 1
            nk = len(keys)
            for i, kk in enumerate(keys):
                tiles.append((qcol0, qn, tbidx, 4 + ob, 6 + ob, i == 0, i == nk - 1) + kk)

        for qb in range(4):
            keys = []
            for kc in range(4 * qb + 4):
                diag = kc >= 4 * qb
                n0 = (kc - 4 * qb) * 128 if diag else 0
                keys.append((kT[:, kc * 128:(kc + 1) * 128], 128, vb[:, kc, :], ncc[:, kc, h:h + 1], n0, diag,
                             [r_kT[kc // 4], r_vb[kc // 4]]))
            attend(qb * 512, 512, keys, qb)
        keys = []
        for kc in range(8):
            keys.append((kcT[:, kc * 128:(kc + 1) * 128], 128, vc_tm[:, kc, :], ncc[:, 16 + kc, h:h + 1], 0, False,
                         [r_kcT, r_cache]))
        keys.append((kT[:, NP:NT], NS, vb[0:NS, 16, :], ncc[0:NS, 24, h:h + 1], 0, True, [r_kT[4], r_vb[4]]))
        attend(NP, NS, keys, 4)

        LA = 3
        slots = {}

        def issue_S(i):
            (qcol0, qn, tbidx, Ob, Lb, first, last, kl, kn, vv, bias, n0, diag, rds) = tiles[i]
            sb_ = SRING[sbk[0] % 4]; sbk[0] += 1
            st = stk[0] % NST; stk[0] += 1
            p = pti[0] % NPT; pti[0] += 1
            slots[i] = p
            S.add(PE, MM(banks[sb_][0:kn, n0:qn], kl, qT[:, qcol0 + n0:qcol0 + qn], True, not diag),
                  reads=rds + [r_qT[tbidx]], writes=[bankr[sb_]])
            if diag:
                S.add(PE, MM(banks[sb_][0:kn, n0:n0 + kn], identb[0:kn, 0:kn], masknegb[0:kn, 0:kn], False, True),
                      reads=[r_const], writes=[bankr[sb_]])
            S.add(DVE, STT(stmp[st][0:kn, n0:qn], banks[sb_][0:kn, n0:qn], SCALE,
                           cqb[0:kn, qcol0 + n0:qcol0 + qn], ALU.mult, ALU.add),
                  reads=[bankr[sb_], r_cqb], writes=[r_st[st]])
            S.add(ACT, ACTF(pt[p][0:kn, n0:qn], stmp[st][0:kn, n0:qn], AF.Exp, bias=bias),
                  reads=[r_st[st], r_ncc], writes=[r_pt[p]])

        def issue_PV(i):
            (qcol0, qn, tbidx, Ob, Lb, first, last, kl, kn, vv, bias, n0, diag, rds) = tiles[i]
            p = slots.pop(i)
            S.add(PE, MM(banks[Ob][:, n0:qn], vv, pt[p][0:kn, n0:qn], first, last),
                  reads=[r_pt[p]] + rds, writes=[bankr[Ob]])
            S.add(PE, MM(banks[Lb][:, n0:qn], onesb[0:kn, :], pt[p][0:kn, n0:qn], first, last),
                  reads=[r_pt[p], r_const], writes=[bankr[Lb]])
            if last:
                S.add(ACT, ACTF(rl[:, 0:qn], banks[Lb][:, 0:qn], AF.Ln), reads=[bankr[Lb]], writes=[r_rl])
                S.add(ACT, ACTF(rl[:, 0:qn], rl[:, 0:qn], AF.Exp, scale=-1.0), reads=[r_rl], writes=[r_rl])
                S.add(DVE, TT(on[:, 0:qn], banks[Ob][:, 0:qn], rl[:, 0:qn], ALU.mult), reads=[bankr[Ob], r_rl], writes=[r_on])
                S.add(POOL, TT(aT[:, h, qcol0:qcol0 + qn], on[:, 0:qn], sg[:, qcol0:qcol0 + qn], ALU.mult),
                      reads=[r_on, r_sg[tbidx]], writes=[r_aT[h][tbidx]])

        nt_ = len(tiles)
        for i in range(min(LA, nt_)):
            issue_S(i)
        for i in range(nt_):
            if i + LA < nt_:
                issue_S(i + LA)
            issue_PV(i)
        if h + 1 < nheads:
            load_cache(h + 1)
            load_cqb(h + 1)

    if debug is not None and upto <= 2:
        r_d = Res("dbgt")
        dt_ = A.alloc("dbgt", [128, NT], F32, [r_d])
        d_dbg = S.dmasem("dbg")
        for h in range(nheads):
            S.add(DVE, COPY(dt_[:], aT[:, h, :]), reads=r_aT[h], writes=[r_d])
            S.add(SP, DMA(dbg[:, h, :], dt_[:]), reads=[r_d], dmasem=d_dbg)

    A.release(mkB)
    r_mscr = [[Res(f"mscr{j}_{tb}") for tb in range(5)] for j in range(16)]

    def combine(aT_, r_a, nchunks, w_out, gate, first, hooks=None, last=False):
        mkc = A.mark()
        r_wo = [Res("wo0"), Res("wo1")]
        wo = [A.alloc(f"wo{i}", [128, nchunks, 256], BF16, [r_wo[i]]) for i in range(2)]
        wm = [A.alloc(f"wm{i}", [128, 16, 256], BF16, [r_wo[i]]) for i in range(2)]
        d_wo = [S.dmasem(f"wo{gate}_0"), S.dmasem(f"wo{gate}_1")]
        r_gt = [Res("gt0"), Res("gt1")]
        gt = [A.alloc(f"gt{i}", [128, 512], F32, [r_gt[i]]) for i in range(2)]
        r_mt = [Res("mt0"), Res("mt1")]
        mt = [A.alloc(f"mt{i}", [128, 512], F32, [r_mt[i]]) for i in range(2)]
        d_mo = [S.dmasem(f"mo{gate}_0"), S.dmasem(f"mo{gate}_1")]
        r_pv = [Res("pv0"), Res("pv1")]
        pv = [A.alloc(f"pv{i}", [128, 512], F32, [r_pv[i]]) for i in range(2)]
        if last:
            r_mtb = [Res("mtb0"), Res("mtb1")]
            mtb = [A.alloc(f"mtb{i}", [128, 512], BF16, [r_mtb[i]]) for i in range(2)]
        d_pv = [S.dmasem(f"pv{gate}_0"), S.dmasem(f"pv{gate}_1")]

        def loadw(jg):
            s = jg % 2
            S.add(POOL, DMA(wo[s][:], w_out[:, jg * 256:(jg + 1) * 256].rearrange("(c p) n -> p c n", p=128)),
                  writes=[r_wo[s]], dmasem=d_wo[s])
            S.add(POOL, DMA(wm[s][:], I["w_merge"][:, gate * D + jg * 256:gate * D + (jg + 1) * 256]
                            .rearrange("(c p) n -> p c n", p=128)),
                  writes=[r_wo[s]], dmasem=d_wo[s])
        loadw(0)
        it = 0
        for jg in range(8):
            if jg + 1 < 8:
                loadw(jg + 1)
            if hooks and jg in hooks:
                hooks[jg]()
            s = jg % 2
            for jj in range(2):
                j = jg * 2 + jj
                for tb, (t0, tn) in enumerate(TBLK):
                    p2 = it % 2
                    it += 1
                    yb, gb = 2 * p2, 2 * p2 + 1
                    if not first:
                        S.add(SP, DMA(pv[p2][:, 0:tn], m_scr[j, :, t0:t0 + tn]), reads=[r_mscr[j][tb]],
                              writes=[r_pv[p2]], dmasem=d_pv[p2])
                    for c in range(16):
                        S.add(PE, MM(banks[gb][:, 0:tn], wm[s][:, c, jj * 128:(jj + 1) * 128], xT[:, c, t0:t0 + tn],
                                     c == 0, c == 15), reads=[r_wo[s]] + xT_res(t0, tn), writes=[bankr[gb]])
                    for c in range(nchunks):
                        S.add(PE, MM(banks[yb][:, 0:tn], wo[s][:, c, jj * 128:(jj + 1) * 128], aT_[:, c, t0:t0 + tn],
                                     c == 0, c == nchunks - 1), reads=[r_wo[s], r_a(c, tb)], writes=[bankr[yb]])
                    S.add(ACT, ACTF(gt[p2][:, 0:tn], banks[gb][:, 0:tn], AF.Sigmoid, bias=bm[:, gate * 16 + j:gate * 16 + j + 1]),
                          reads=[bankr[gb], r_const], writes=[r_gt[p2]])
                    S.add(DVE, TT(mt[p2][:, 0:tn], banks[yb][:, 0:tn], gt[p2][:, 0:tn], ALU.mult),
                          reads=[bankr[yb], r_gt[p2]], writes=[r_mt[p2]])
                    if last:
                        S.add(POOL, TT(mtb[p2][:, 0:tn], mt[p2][:, 0:tn], pv[p2][:, 0:tn], ALU.add),
                              reads=[r_mt[p2], r_pv[p2]], writes=[r_mtb[p2]])
                        S.add(SP, DMA(m_scrb[j, :, t0:t0 + tn], mtb[p2][:, 0:tn]), reads=[r_mtb[p2]],
                              writes=[r_mscr[j][tb]], dmasem=d_mo[p2])
                        continue
                    if not first:
                        S.add(POOL, TT(mt[p2][:, 0:tn], mt[p2][:, 0:tn], pv[p2][:, 0:tn], ALU.add),
                              reads=[r_mt[p2], r_pv[p2]], writes=[r_mt[p2]])
                    S.add(SP, DMA(m_scr[j, :, t0:t0 + tn], mt[p2][:, 0:tn]), reads=[r_mt[p2]],
                          writes=[r_mscr[j][tb]], dmasem=d_mo[p2])
        A.release(mkc)

    chi = A.mark_hi()
    r_cw = [Res("cw0"), Res("cw1")]
    cwb = [A.alloc(f"cwb{i}", [128, 16, 512], BF16, [r_cw[i]], high=True) for i in range(2)]
    d_cw = [S.dmasem("cw0"), S.dmasem("cw1")]

    def load_cw(c):
        s = c % 2
        for comp, off in enumerate((OFF_CB, OFF_CC, OFF_CH, OFF_CG)):
            S.add(POOL, DMA(cwb[s][:, :, comp * 128:(comp + 1) * 128],
                            I["w_in"][:, off + c * 128:off + (c + 1) * 128].rearrange("(c p) n -> p c n", p=128)),
                  writes=[r_cw[s]], dmasem=d_cw[s])
    if upto >= 3:
        combine(aT, lambda c, tb: r_aT[c][tb], H, I["w_fox_out"], 0, True,
                hooks={7: (lambda: load_cw(0))} if upto >= 4 else None)
    A.release(mk2)

    if upto >= 4:
        mk3 = A.mark()
        r_aC = [[Res(f"aC{c}_{tb}") for tb in range(5)] for c in range(12)]
        aC = A.alloc("aC", [128, 12, NT], BF16, [r for l in r_aC for r in l])
        mk3b = A.mark()
        r_cs = Res("convsmall")
        cwT = A.alloc("cwT", [128, 3, 12], F32, [r_cs])
        cbT = A.alloc("cbT", [128, 12], F32, [r_cs])
        sconvT = A.alloc("sconvT", [128, 2, 12], F32, [r_cs])
        d_cs = S.dmasem("convsmall")
        for j in range(3):
            S.add(SP, DMA(cwT[:, j, :], I["conv_w"][j:j + 1, :].rearrange("o (c p) -> p (o c)", p=128),
                          allow_slow_non_contiguous=True), writes=[r_cs], dmasem=d_cs)
        S.add(SP, DMA(cbT[:], I["conv_b"].rearrange("(c p) o -> p (c o)", p=128), allow_slow_non_contiguous=True),
              writes=[r_cs], dmasem=d_cs)
        for j in range(2):
            S.add(SP, DMA(sconvT[:, j, :], I["sconv"][j:j + 1, :].rearrange("o (c p) -> p (o c)", p=128),
                          allow_slow_non_contiguous=True), writes=[r_cs], dmasem=d_cs)
        r_cg = [Res(f"cg{i}") for i in range(5)]
        cg = A.alloc("cg", [128, NT], F32, r_cg)
        r_u = [Res(f"u{i}") for i in range(5)]
        r_u0 = Res("u0")
        up = A.alloc("up", [128, 2 + NP], F32, r_u[0:4] + [r_u0])
        us = A.alloc("us", [128, 2 + NS], F32, [r_u[4]])
        r_bg = [Res(f"bg{i}") for i in range(5)]
        bg = A.alloc("bg", [128, NT], F32, r_bg)
        r_sc = [Res(f"sc{i}") for i in range(5)]
        scg = A.alloc("scg", [128, NT], F32, r_sc)
        r_tb = Res("tbuf")
        tbuf = A.alloc("tbuf", [128, NT], F32, [r_tb])
        S.add(POOL, lambda e: e.memset(up[:, 0:2], 0.0), writes=[r_u0])

        for c in range(12):
            s = c % 2
            for tb, (t0, tn) in enumerate(TBLK):
                if tb == 1 and c + 1 < 12:
                    load_cw(c + 1)
                def proj(comp):
                    bk = pj[0]; pj[0] ^= 1
                    for kc in range(16):
                        S.add(PE, MM(banks[bk][:, 0:tn], cwb[s][:, kc, comp * 128:(comp + 1) * 128], xT[:, kc, t0:t0 + tn],
                                     kc == 0, kc == 15), reads=[r_cw[s]] + xT_res(t0, tn), writes=[bankr[bk]])
                    return bk
                bk = proj(1)
                S.add(ACT, COPY(cg[:, t0:t0 + tn], banks[bk][:, 0:tn]), reads=[bankr[bk]], writes=[r_cg[tb]])
                bk = proj(2)
                udst = up[:, 2 + t0:2 + t0 + tn] if tb < 4 else us[:, 2:2 + NS]
                S.add(DVE, TT(udst, banks[bk][:, 0:tn], cg[:, t0:t0 + tn], ALU.mult),
                      reads=[bankr[bk], r_cg[tb]], writes=[r_u[tb]])
                bk = proj(0)
                S.add(ACT, COPY(bg[:, t0:t0 + tn], banks[bk][:, 0:tn]), reads=[bankr[bk]], writes=[r_bg[tb]])
                bk = proj(3)
                S.add(ACT, ACTF(scg[:, t0:t0 + tn], banks[bk][:, 0:tn], AF.Silu), reads=[bankr[bk]], writes=[r_sc[tb]])
            S.add(POOL, COPY(us[:, 0:2], sconvT[:, :, c]), reads=[r_cs], writes=[r_u[4]])
            for (ub, n, o0, rr) in ((up, NP, 0, r_u[0:4] + [r_u0]), (us, NS, NP, [r_u[4]])):
                S.add(DVE, TS(tbuf[:, o0:o0 + n], ub[:, 2:2 + n], cwT[:, 2, c:c + 1], ALU.mult, cbT[:, c:c + 1], ALU.add),
                      reads=rr + [r_cs], writes=[r_tb])
                S.add(DVE, STT(tbuf[:, o0:o0 + n], ub[:, 1:1 + n], cwT[:, 1, c:c + 1], tbuf[:, o0:o0 + n], ALU.mult, ALU.add),
                      reads=rr + [r_cs, r_tb], writes=[r_tb])
                S.add(DVE, STT(tbuf[:, o0:o0 + n], ub[:, 0:n], cwT[:, 0, c:c + 1], tbuf[:, o0:o0 + n], ALU.mult, ALU.add),
                      reads=rr + [r_cs, r_tb], writes=[r_tb])
            S.add(POOL, COPY(utail[:, :, c], up[:, NP:NP + 2]), reads=[r_u[3]], writes=[r_ut])
            S.add(POOL, COPY(ustail[:, :, c], us[:, NS:NS + 2]), reads=[r_u[4]], writes=[r_ut])
            S.add(POOL, TT(tbuf[:, :], tbuf[:, :], bg[:, :], ALU.mult), reads=[r_tb] + r_bg, writes=[r_tb])
            S.add(POOL, TT(aC[:, c, :], tbuf[:, :], scg[:, :], ALU.mult), reads=[r_tb] + r_sc, writes=r_aC[c])
        d_ut = S.dmasem("utail")
        for j in range(2):
            S.add(SP, DMA(O["cv_p"][j:j + 1, :].rearrange("o (c p) -> p (o c)", p=128), utail[:, j, :],
                          allow_slow_non_contiguous=True), reads=[r_ut], dmasem=d_ut)
            S.add(SP, DMA(O["cv_s"][j:j + 1, :].rearrange("o (c p) -> p (o c)", p=128), ustail[:, j, :],
                          allow_slow_non_contiguous=True), reads=[r_ut], dmasem=d_ut)
        A.release(mk3b)
        A.release_hi(chi)
        mhi = A.mark_hi()
        r_mw = [Res("mw0"), Res("mw1")]
        mwb = [A.alloc(f"mwb{i}", [128, 16, 512], BF16, [r_mw[i]], high=True) for i in range(2)]
        d_mw = [S.dmasem("mw0"), S.dmasem("mw1")]
        r_memb = Res("memb")
        memb = A.alloc("memb", [128, 2, D], BF16, [r_memb], high=True)
        d_memb = S.dmasem("memb")

        def load_mkv(nb):
            s = nb % 2
            S.add(POOL, DMA(mwb[s][:], I["w_mem_kv"][:, nb * 512:(nb + 1) * 512].rearrange("(c p) n -> p c n", p=128)),
                  writes=[r_mw[s]], dmasem=d_mw[s])

        def early_mem_a():
            S.add(POOL, DMA(memb[:].rearrange("p c (a b) -> p c a b", b=1024),
                            I["mem"].rearrange("(c p) (a b) -> p c a b", p=128, b=1024)),
                  writes=[r_memb], dmasem=d_memb)
            load_mkv(0)
        if upto >= 5:
            combine(aC, lambda c, tb: r_aC[c][tb], 12, I["w_conv_out"], 1, False,
                    hooks={5: early_mem_a, 6: lambda: load_mkv(1)})
        A.release(mk3)

    if upto >= 6:
        mk4 = A.mark()
        r_aM = [[Res(f"aM{c}_{tb}") for tb in range(5)] for c in range(8)]
        aM = A.alloc("aM", [128, 8, NT], BF16, [r for l in r_aM for r in l])
        mk4b = A.mark()
        r_mkb = [Res("mkb_p"), Res("mkb_s")]
        mkb = [A.alloc("mkb_p", [128, 2, 2 * MW], BF16, [r_mkb[0]]), A.alloc("mkb_s", [128, 2, 2 * MW], BF16, [r_mkb[1]])]
        d_cm = S.dmasem("cmem")
        r_mkT = [Res("mkT_p"), Res("mkT_s")]
        mkT = [A.alloc("mkT_p", [128, 8, MT], BF16, [r_mkT[0]]), A.alloc("mkT_s", [128, 8, MT], BF16, [r_mkT[1]])]
        mk4c = A.mark()
        r_memT = Res("memT")
        memT = A.alloc("memT", [128, 16, MT], BF16, [r_memT])
        r_m32 = [Res("m32_0"), Res("m32_1")]
        m32 = [A.alloc(f"m32_{i}", [128, 512], F32, [r_m32[i]]) for i in range(2)]
        d_m32 = [S.dmasem("m32_0"), S.dmasem("m32_1")]

        S.add(POOL, DMA(mkb[1][:, :, 0:MW], I["cmk"].rearrange("(c p) n -> p c n", p=128)), writes=[r_mkb[1]], dmasem=d_cm)
        S.add(POOL, DMA(mkb[1][:, :, MW:2 * MW], I["cmv"].rearrange("(c p) n -> p c n", p=128)), writes=[r_mkb[1]], dmasem=d_cm)
        for mc in range(2):
            for g in range(4):
                bk = pj[0]; pj[0] ^= 1
                bv = banks[bk].bitcast(BF16)
                for j in range(4):
                    c = g * 4 + j
                    S.add(PE, TR(bv[:, j * 128:(j + 1) * 128], memb[:, mc, c * 128:(c + 1) * 128], identb[:, :]),
                          reads=[r_memb, r_const], writes=[bankr[bk]])
                S.add(ACT, COPY(memT[:, g * 4:g * 4 + 4, mc * 128:(mc + 1) * 128],
                                bv[:, 0:512].rearrange("p (a b) -> p a b", b=128)), reads=[bankr[bk]], writes=[r_memT])

        def load_mq(h, s):
            S.add(POOL, DMA(mwb[s][:, :, 0:256], I["w_in"][:, OFF_MQ + h * 256:OFF_MQ + (h + 1) * 256]
                            .rearrange("(c p) n -> p c n", p=128)), writes=[r_mw[s]], dmasem=d_mw[s])
            S.add(POOL, DMA(mwb[s][:, :, 256:512], I["w_in"][:, OFF_MG + h * 256:OFF_MG + (h + 1) * 256]
                            .rearrange("(c p) n -> p c n", p=128)), writes=[r_mw[s]], dmasem=d_mw[s])
        it = 0
        for nb in range(4):
            if 1 <= nb + 1 < 4 and nb >= 1:
                load_mkv(nb + 1)
            elif nb == 3:
                load_mq(0, 0)
            s = nb % 2
            for mc in range(2):
                bk = pj[0]; pj[0] ^= 1
                p2 = it % 2; it += 1
                for c in range(16):
                    S.add(PE, MM(banks[bk][:, :], memT[:, c, mc * 128:(mc + 1) * 128], mwb[s][:, c, :], c == 0, c == 15),
                          reads=[r_memT, r_mw[s]], writes=[bankr[bk]])
                S.add(ACT, COPY(m32[p2][:], banks[bk][:, :]), reads=[bankr[bk]], writes=[r_m32[p2]])
                S.add(POOL, COPY(mkb[0][:, mc, nb * 512:(nb + 1) * 512], m32[p2][:]), reads=[r_m32[p2]], writes=[r_mkb[0]])
                dst = O["mk_p"] if nb < 2 else O["mv_p"]
                S.add(SP, DMA(dst[mc * 128:(mc + 1) * 128, (nb % 2) * 512:(nb % 2 + 1) * 512], m32[p2][:]),
                      reads=[r_m32[p2]], dmasem=d_m32[p2])
        load_mq(1, 1)
        for src in range(2):
            for hd in range(8):
                bk = pj[0]; pj[0] ^= 1
                bv = banks[bk].bitcast(BF16)
                for mc in range(2):
                    S.add(PE, TR(bv[:, mc * 128:(mc + 1) * 128], mkb[src][:, mc, hd * 128:(hd + 1) * 128], identb[:, :]),
                          reads=[r_mkb[src], r_const], writes=[bankr[bk]])
                S.add(ACT, COPY(mkT[src][:, hd, :], bv[:, 0:256]), reads=[bankr[bk]], writes=[r_mkT[src]])

        A.release(mk4c)
        r_mq = [Res(f"mq{i}") for i in range(5)]
        mqT = A.alloc("mqT", [128, 2, NT], BF16, r_mq)
        r_sgm = [Res(f"sgm{i}") for i in range(5)]
        sgm = A.alloc("sgm", [128, 2, NT], F32, r_sgm)
        r_mpt = [Res("mpt0"), Res("mpt1"), Res("mpt2"), Res("mpt3")]
        mpt = [A.alloc(f"mpt{i}", [128, 512], BF16, [r_mpt[i]]) for i in range(4)]
        r_mrl = Res("mrl")
        mrl = A.alloc("mrl", [128, 512], F32, [r_mrl])
        r_mon = [Res("mon0"), Res("mon1")]
        mon = [A.alloc(f"mon{i}", [128, 512], F32, [r_mon[i]]) for i in range(2)]

        def mproj(h, tb):
            s = h % 2
            t0, tn = TBLK[tb]
            for comp in range(4):
                bk = pj[0]; pj[0] ^= 1
                for c in range(16):
                    S.add(PE, MM(banks[bk][:, 0:tn], mwb[s][:, c, comp * 128:(comp + 1) * 128], xT[:, c, t0:t0 + tn],
                                 c == 0, c == 15), reads=[r_mw[s]] + xT_res(t0, tn), writes=[bankr[bk]])
                if comp < 2:
                    S.add(ACT, COPY(mqT[:, comp, t0:t0 + tn], banks[bk][:, 0:tn]), reads=[bankr[bk]], writes=[r_mq[tb]])
                else:
                    S.add(ACT, ACTF(sgm[:, comp - 2, t0:t0 + tn], banks[bk][:, 0:tn], AF.Silu),
                          reads=[bankr[bk]], writes=[r_sgm[tb]])

        units = [(h, tb) for h in range(4) for tb in range(5)]
        mproj(0, 0)
        for ui, (h, tb) in enumerate(units):
            t0, tn = TBLK[tb]
            src = 0 if tb < 4 else 1
            pts = []
            for mc in range(2):
                sb_ = 2 + mc
                p = pti[0] % 4; pti[0] += 1
                pts.append(p)
                for dc in range(2):
                    S.add(PE, MM(banks[sb_][:, 0:tn], mkT[src][:, h * 2 + dc, mc * 128:(mc + 1) * 128],
                                 mqT[:, dc, t0:t0 + tn], dc == 0, dc == 1),
                          reads=[r_mkT[src], r_mq[tb]], writes=[bankr[sb_]])
                S.add(ACT, ACTF(mpt[p][:, 0:tn], banks[sb_][:, 0:tn], AF.Exp, scale=MSCALE),
                      reads=[bankr[sb_]], writes=[r_mpt[p]])
            if ui + 1 < len(units):
                nh, ntb = units[ui + 1]
                mproj(nh, ntb)
                if ntb == 4 and nh + 2 < 4:
                    load_mq(nh + 2, nh % 2)
            for mc in range(2):
                S.add(PE, MM(banks[6][:, 0:tn], onesb[:, :], mpt[pts[mc]][:, 0:tn], mc == 0, mc == 1),
                      reads=[r_mpt[pts[mc]], r_const], writes=[bankr[6]])
            for dc in range(2):
                for mc in range(2):
                    S.add(PE, MM(banks[4 + dc][:, 0:tn],
                                 mkb[src][:, mc, MW + h * 256 + dc * 128:MW + h * 256 + (dc + 1) * 128],
                                 mpt[pts[mc]][:, 0:tn], mc == 0, mc == 1),
                          reads=[r_mpt[pts[mc]], r_mkb[src]], writes=[bankr[4 + dc]])
            S.add(DVE, lambda e, tn=tn: e.reciprocal(out=mrl[:, 0:tn], in_=banks[6][:, 0:tn]),
                  reads=[bankr[6]], writes=[r_mrl])
            for dc in range(2):
                S.add(DVE, TT(mon[dc][:, 0:tn], banks[4 + dc][:, 0:tn], mrl[:, 0:tn], ALU.mult),
                      reads=[bankr[4 + dc], r_mrl], writes=[r_mon[dc]])
                S.add(POOL, TT(aM[:, h * 2 + dc, t0:t0 + tn], mon[dc][:, 0:tn], sgm[:, dc, t0:t0 + tn], ALU.mult),
                      reads=[r_mon[dc], r_sgm[tb]], writes=[r_aM[h * 2 + dc][tb]])
        A.release(mk4b)
        A.release_hi(mhi)
        r_wob = Res("wob")
        wob = A.alloc("wob", [128, 16, D], BF16, [r_wob], high=True)
        d_wob = S.dmasem("wob")
        def wob_load(q4):
            return lambda: S.add(POOL, DMA(wob[:, :, q4 * 512:(q4 + 1) * 512],
                                           I["w_o"][:, q4 * 512:(q4 + 1) * 512].rearrange("(c p) n -> p c n", p=128)),
                                 writes=[r_wob], dmasem=d_wob)
        if upto >= 7:
            combine(aM, lambda c, tb: r_aM[c][tb], 8, I["w_mem_out"], 2, False,
                    hooks={1: wob_load(0), 3: wob_load(1), 5: wob_load(2), 7: wob_load(3)}, last=True)
        A.release(mk4)

    if upto >= 8:
        A.release(mkX)
        r_mT = [Res(f"mT{i}") for i in range(17)]
        mT = A.alloc("mT", [128, 16, NT], BF16, r_mT)
        d_mT = [S.dmasem(f"mT{i}") for i in range(5)]
        r_ln = Res("ln")
        lng = A.alloc("lng", [128, D], F32, [r_ln])
        lnb = A.alloc("lnb", [128, D], F32, [r_ln])
        d_ln = S.dmasem("ln")
        NR = 3
        r_x32 = [Res(f"x32_{i}") for i in range(NR)]
        x32 = [A.alloc(f"x32_{i}", [128, D], F32, [r_x32[i]]) for i in range(NR)]
        d_x32 = [S.dmasem(f"x32_{i}") for i in range(NR)]
        r_rr = [Res(f"rr{i}") for i in range(NR)]
        rr_ = [A.alloc(f"rr{i}", [128, D], F32, [r_rr[i]]) for i in range(NR)]
        d_yo = [S.dmasem(f"yo{i}") for i in range(NR)]
        r_stat = [Res(f"stat{i}") for i in range(NR)]
        stats = [A.alloc(f"stats{i}", [128, 4, 6], F32, [r_stat[i]]) for i in range(NR)]
        mv = [A.alloc(f"mv{i}", [128, 2], F32, [r_stat[i]]) for i in range(NR)]
        rstd = [A.alloc(f"rstd{i}", [128, 1], F32, [r_stat[i]]) for i in range(NR)]
        nmr = [A.alloc(f"nmr{i}", [128, 1], F32, [r_stat[i]]) for i in range(NR)]
        S.add(SP, DMA(lng[:], I["ln_g"].partition_broadcast(128)), writes=[r_ln], dmasem=d_ln)
        S.add(SP, DMA(lnb[:], I["ln_b"].partition_broadcast(128)), writes=[r_ln], dmasem=d_ln)
        for tb, (t0, tn) in enumerate(TBLK):
            for j4 in range(4):
                S.add(SP, DMA(mT[:, j4 * 4:j4 * 4 + 4, t0:t0 + tn],
                              m_scrb[j4 * 4:j4 * 4 + 4, :, t0:t0 + tn].rearrange("j p t -> p j t")),
                      reads=[r_mscr[j][tb] for j in range(j4 * 4, j4 * 4 + 4)],
                      writes=[r_mT[i] for i, (c0, cn) in enumerate(TCH) if t0 <= c0 < t0 + tn], dmasem=d_mT[tb])
        hb = [0]

        def load_x32(tc):
            t0, tn = TCH[tc]
            src = I["xp"][t0:t0 + tn, :] if tc < 16 else I["xs"][:, :]
            S.add(SP, DMA(x32[tc % NR][0:tn, :], src), writes=[r_x32[tc % NR]], dmasem=d_x32[tc % NR])
        for tc in range(NR - 1):
            load_x32(tc)
        for tc, (t0, tn) in enumerate(TCH):
            s = tc % NR
            if tc + NR - 1 < 17:
                load_x32(tc + NR - 1)
            for db in range(4):
                bk = hb[0] % 8; hb[0] += 1
                for c in range(16):
                    S.add(PE, MM(banks[bk][0:tn, :], mT[:, c, t0:t0 + tn], wob[:, c, db * 512:(db + 1) * 512],
                                 c == 0, c == 15), reads=[r_mT[tc], r_wob], writes=[bankr[bk]])
                S.add(DVE, STT(rr_[s][0:tn, db * 512:(db + 1) * 512], x32[s][0:tn, db * 512:(db + 1) * 512], DN_ALPHA,
                               banks[bk][0:tn, :], ALU.mult, ALU.add),
                      reads=[bankr[bk], r_x32[s]], writes=[r_rr[s]])
                S.add(DVE, lambda e, s=s, tn=tn, db=db: e.bn_stats(out=stats[s][0:tn, db, :], in_=rr_[s][0:tn, db * 512:(db + 1) * 512]),
                      reads=[r_rr[s]], writes=[r_stat[s]])
            S.add(DVE, lambda e, s=s, tn=tn: e.bn_aggr(out=mv[s][0:tn, :], in_=stats[s][0:tn, :, :].rearrange("p a b -> p (a b)")),
                  reads=[r_stat[s]], writes=[r_stat[s]])
            S.add(ACT, ACTF(rstd[s][0:tn, :], mv[s][0:tn, 1:2], AF.Sqrt, bias=epsb[0:tn, 0:1]),
                  reads=[r_stat[s], r_const], writes=[r_stat[s]])
            S.add(DVE, lambda e, s=s, tn=tn: e.reciprocal(out=rstd[s][0:tn, :], in_=rstd[s][0:tn, :]),
                  reads=[r_stat[s]], writes=[r_stat[s]])
            S.add(DVE, STT(nmr[s][0:tn, :], mv[s][0:tn, 0:1], -1.0, rstd[s][0:tn, :], ALU.mult, ALU.mult),
                  reads=[r_stat[s]], writes=[r_stat[s]])
            S.add(ACT, ACTF(rr_[s][0:tn, :], rr_[s][0:tn, :], AF.Identity, bias=nmr[s][0:tn, 0:1], scale=rstd[s][0:tn, 0:1]),
                  reads=[r_rr[s], r_stat[s]], writes=[r_rr[s]])
            S.add(DVE, TT(rr_[s][0:tn, :], rr_[s][0:tn, :], lng[0:tn, :], ALU.mult), reads=[r_rr[s], r_ln], writes=[r_rr[s]])
            S.add(POOL, TT(rr_[s][0:tn, :], rr_[s][0:tn, :], lnb[0:tn, :], ALU.add), reads=[r_rr[s], r_ln], writes=[r_rr[s]])
            dst = O["y_p"][t0:t0 + tn, :] if tc < 16 else O["y_s"][:, :]
            S.add(SP, DMA(dst, rr_[s][0:tn, :]), reads=[r_rr[s]], dmasem=d_yo[s])

    import contextlib
    es = contextlib.ExitStack()
    with nc.Block() as block:
        S.emit(block, es)
    return nc


def make_in_maps(inputs):
    f = lambda a: np.ascontiguousarray(np.asarray(a, dtype=np.float32))
    identb = np.eye(128, dtype=np.float32).astype(ml_dtypes.bfloat16)
    identf = np.eye(128, dtype=np.float32)
    onesb = np.ones((128, 128), dtype=np.float32).astype(ml_dtypes.bfloat16)
    kk = np.arange(128)[:, None]
    qq = np.arange(128)[None, :]
    maskneg = np.where(qq >= kk, 0.0, NEG).astype(np.float32)
    shared = {
        "w_in": f(inputs["w_in"][0]), "fox_bf": f(inputs["fox_bf"][0]).reshape(H, 1),
        "conv_w": f(inputs["conv_w"][0]), "conv_b": f(inputs["conv_b"][0]).reshape(FW, 1),
        "w_mem_kv": f(inputs["w_mem_kv"][0]), "w_fox_out": f(inputs["w_fox_out"][0]),
        "w_conv_out": f(inputs["w_conv_out"][0]), "w_mem_out": f(inputs["w_mem_out"][0]),
        "w_merge": f(inputs["w_merge"][0]), "b_merge": f(inputs["b_merge"][0]).reshape(3 * D, 1),
        "w_o": f(inputs["w_o"][0]), "ln_g": f(inputs["ln_g"][0]).reshape(1, D),
        "ln_b": f(inputs["ln_b"][0]).reshape(1, D),
        "identb": identb, "identf": identf, "onesb": onesb, "maskneg": maskneg,
        "masknegb": maskneg.astype(ml_dtypes.bfloat16),
    }
    maps = []
    for b in range(8):
        m = dict(shared)
        m["xp"] = f(inputs["x_prompt"][b])
        m["xs"] = f(inputs["x_sample"][b])
        m["mem"] = f(inputs["mem_prompt"][b])
        m["ck"] = f(inputs["cache_fox_k"][0, b]).reshape(PAST, FW)
        m["cv"] = f(inputs["cache_fox_v"][0, b]).reshape(PAST, FW)
        m["clf"] = f(inputs["cache_fox_logf"][0, b])
        m["sconv"] = f(inputs["state_conv"][0, b])
        m["cmk"] = f(inputs["cache_mem_k"][0, b]).reshape(MT, MW)
        m["cmv"] = f(inputs["cache_mem_v"][0, b]).reshape(MT, MW)
        maps.append(m)
    return maps


_NC_CACHE = {}


def kernel(**inputs):
    if "nc" not in _NC_CACHE:
        _NC_CACHE["nc"] = build()
    nc = _NC_CACHE["nc"]
    maps = make_in_maps(inputs)
    res = run_bass_kernel_spmd(nc, maps, core_ids=list(range(8)))
    R = res.results
    st = lambda n: np.stack([np.asarray(R[b][n], dtype=np.float32) for b in range(8)])
    return (
        st("y_p"), st("y_s"),
        st("fk_p").reshape(1, 8, NP, H, DH), st("fv_p").reshape(1, 8, NP, H, DH), st("fl_p").reshape(1, 8, NP, H),
        st("cv_p").reshape(1, 8, 2, FW), st("mk_p").reshape(1, 8, MT, 4, 256), st("mv_p").reshape(1, 8, MT, 4, 256),
        st("fk_s").reshape(1, 8, NS, H, DH), st("fv_s").reshape(1, 8, NS, H, DH), st("fl_s").reshape(1, 8, NS, H),
        st("cv_s").reshape(1, 8, 2, FW),
    )
```

```python
import numpy as np
import ml_dtypes
import concourse.bass as bass
import concourse.mybir as mybir
from concourse.bass_utils import run_bass_kernel_spmd

F32 = mybir.dt.float32
BF16 = mybir.dt.bfloat16
ALU = mybir.AluOpType
AF = mybir.ActivationFunctionType

PE, ACT, DVE, POOL, SP = "pe", "act", "dve", "pool", "sp"
ENGS = [PE, ACT, DVE, POOL, SP]


class Res:
    __slots__ = ("name", "w", "r")

    def __init__(self, name=""):
        self.name = name
        self.w = None
        self.r = {}


class DmaSem:
    __slots__ = ("sem", "count", "name")

    def __init__(self, name):
        self.name = name
        self.sem = None
        self.count = 0


class Op:
    __slots__ = ("eng", "fn", "deps", "dmasem", "marked", "val")

    def __init__(self, eng, fn):
        self.eng = eng
        self.fn = fn
        self.deps = []
        self.dmasem = None
        self.marked = False
        self.val = 0


class Sched:
    def __init__(self, nc, sync_same_engine=True):
        self.nc = nc
        self.ops = {e: [] for e in ENGS}
        self.dmasems = []
        self.sync_same = sync_same_engine

    def dmasem(self, name):
        d = DmaSem(name)
        self.dmasems.append(d)
        return d

    def _dep(self, op, prod):
        if prod is None or prod is op:
            return
        if prod.dmasem is not None:
            if op.dmasem is prod.dmasem:
                return
            op.deps.append((prod, prod.dmasem.count))
            return
        if prod.eng == op.eng and op.dmasem is None:
            if prod.eng == PE or not self.sync_same:
                return
        op.deps.append((prod, None))
        prod.marked = True

    def add(self, eng, fn, reads=(), writes=(), dmasem=None):
        op = Op(eng, fn)
        op.dmasem = dmasem
        for r in reads:
            self._dep(op, r.w)
        for w in writes:
            self._dep(op, w.w)
            for rr in w.r.values():
                self._dep(op, rr)
        if dmasem is not None:
            dmasem.count += 16
        key = dmasem if dmasem is not None else eng
        for r in reads:
            r.r[key] = op
        for w in writes:
            w.w = op
            w.r = {}
        self.ops[eng].append(op)
        return op

    def emit(self, block, es):
        nc = self.nc
        for e in ENGS:
            c = 0
            for op in self.ops[e]:
                if op.dmasem is None and op.marked:
                    c += 1
                    op.val = c
        esem = {e: es.enter_context(nc.semaphore("prog_" + e)) for e in [PE, ACT, DVE, POOL]}
        for d in self.dmasems:
            d.sem = es.enter_context(nc.semaphore("dma_" + d.name))

        def run(e, engobj):
            waited = {}
            for op in self.ops[e]:
                need = {}
                for (p, dv) in op.deps:
                    if p.dmasem is not None:
                        key, v = p.dmasem.sem, dv
                    else:
                        key, v = esem[p.eng], p.val
                    if v > need.get(key, 0):
                        need[key] = v
                for key, v in need.items():
                    if waited.get(key, 0) < v:
                        engobj.wait_ge(key, v)
                        waited[key] = v
                ins = op.fn(engobj)
                if op.dmasem is not None:
                    ins.then_inc(op.dmasem.sem, 16)
                elif op.marked:
                    ins.then_inc(esem[e], 1)
            if e == SP:
                for d in self.dmasems:
                    if d.count > 0:
                        engobj.wait_ge(d.sem, d.count)

        @block.tensor
        def _(eng):
            run(PE, eng)

        @block.scalar
        def _(eng):
            run(ACT, eng)

        @block.vector
        def _(eng):
            run(DVE, eng)

        @block.gpsimd
        def _(eng):
            run(POOL, eng)

        @block.sync
        def _(eng):
            run(SP, eng)


def MM(out, lhsT, rhs, start, stop):
    return lambda e: e.matmul(out, lhsT=lhsT, rhs=rhs, start=start, stop=stop)


def TR(out, in_, ident):
    return lambda e: e.transpose(out=out, in_=in_, identity=ident)


def TRM(out, in_, ident):
    return lambda e: e.matmul(out, lhsT=in_, rhs=ident, start=True, stop=True)


def DMA(out, in_, **kw):
    return lambda e: e.dma_start(out=out, in_=in_, **kw)


def ACTF(out, in_, func, bias=None, scale=None):
    kw = {}
    if bias is not None:
        kw["bias"] = bias
    if scale is not None:
        kw["scale"] = scale
    return lambda e: e.activation(out=out, in_=in_, func=func, **kw)


def COPY(out, in_):
    def f(e):
        if hasattr(e, "activation"):
            return e.activation(out=out, in_=in_, func=AF.Copy)
        return e.tensor_copy(out=out, in_=in_)
    return f


def TT(out, in0, in1, op):
    return lambda e: e.tensor_tensor(out=out, in0=in0, in1=in1, op=op)


def TS(out, in0, s1, op0, s2=None, op1=None):
    if op1 is None:
        return lambda e: e.tensor_scalar(out=out, in0=in0, scalar1=s1, scalar2=None, op0=op0)
    return lambda e: e.tensor_scalar(out=out, in0=in0, scalar1=s1, scalar2=s2, op0=op0, op1=op1)


def STT(out, in0, scalar, in1, op0, op1):
    return lambda e: e.scalar_tensor_tensor(out=out, in0=in0, scalar=scalar, in1=in1, op0=op0, op1=op1)


D = 2048
NP = 2048
NS = 64
NT = NP + NS
PAST = 1024
H = 12
DH = 128
FW = 1536
MW = 1024
MT = 256
INW = 14348
OFF_Q, OFF_K, OFF_V, OFF_F, OFF_G = 0, 1536, 3072, 4608, 4620
OFF_CB, OFF_CC, OFF_CH, OFF_CG = 6156, 7692, 9228, 10764
OFF_MQ, OFF_MG = 12300, 13324
TBLK = [(0, 512), (512, 512), (1024, 512), (1536, 512), (2048, 64)]
TCH = [(i * 128, 128) for i in range(16)] + [(2048, 64)]
SCALE = DH ** -0.5
MSCALE = 256 ** -0.5
DN_ALPHA = 2.0 ** 0.25
LN_EPS = 1e-5
NEG = -1.0e30

OUT_SPECS = [
    ("y_p", [NP, D]), ("y_s", [NS, D]), ("fk_p", [NP, FW]), ("fv_p", [NP, FW]), ("fl_p", [NP, H]),
    ("cv_p", [2, FW]), ("mk_p", [MT, MW]), ("mv_p", [MT, MW]), ("fk_s", [NS, FW]), ("fv_s", [NS, FW]),
    ("fl_s", [NS, H]), ("cv_s", [2, FW]),
]
IN_SPECS = [
    ("xp", [NP, D], F32), ("xs", [NS, D], F32), ("mem", [MT, D], F32), ("ck", [PAST, FW], F32),
    ("cv", [PAST, FW], F32), ("clf", [PAST, H], F32), ("sconv", [2, FW], F32), ("cmk", [MT, MW], F32),
    ("cmv", [MT, MW], F32), ("w_in", [D, INW], F32), ("fox_bf", [H, 1], F32), ("conv_w", [3, FW], F32),
    ("conv_b", [FW, 1], F32), ("w_mem_kv", [D, 2 * MW], F32), ("w_fox_out", [FW, D], F32),
    ("w_conv_out", [FW, D], F32), ("w_mem_out", [MW, D], F32), ("w_merge", [D, 3 * D], F32),
    ("b_merge", [3 * D, 1], F32), ("w_o", [D, D], F32), ("ln_g", [1, D], F32), ("ln_b", [1, D], F32),
    ("identb", [128, 128], BF16), ("identf", [128, 128], F32), ("onesb", [128, 128], BF16),
    ("maskneg", [128, 128], F32), ("masknegb", [128, 128], BF16),
]


class SbAlloc:
    def __init__(self, nc, lo=16512, hi=229376 - 2048):
        self.nc = nc
        self.lo = lo
        self.hi = hi
        self.top = lo
        self.n = 0
        self.live = []

    def mark(self):
        return self.top

    def mark_hi(self):
        return self.hi

    def release_hi(self, m):
        self.hi = m

    def release(self, mark):
        self.top = mark

    def alloc(self, name, shape, dt, res=(), high=False):
        nb = int(np.prod(shape[1:])) * mybir.dt.size(dt)
        nb = (nb + 63) // 64 * 64
        if high:
            assert self.hi - nb >= self.top, f"SBUF overflow allocating {name} (high)"
            self.hi -= nb
            off = self.hi
        else:
            off = self.top
            assert off + nb <= self.hi, f"SBUF overflow allocating {name}: {off + nb} > {self.hi}"
            self.top += nb
        self.n += 1
        t = self.nc.alloc_sbuf_tensor_at(f"{name}_{self.n}", list(shape), dt, offset=off)
        inherited = {}
        k = 0
        for (o, e, rl) in self.live:
            if o < off + nb and off < e:
                for r0 in rl:
                    if r0.w is not None:
                        inherited[("w", k)] = r0.w
                        k += 1
                    for rr in r0.r.values():
                        inherited[("r", k)] = rr
                        k += 1
        for r1 in res:
            r1.r.update(inherited)
        self.live = [(o, e, rl) for (o, e, rl) in self.live if not (o >= off and e <= off + nb)]
        self.live.append((off, off + nb, list(res)))
        return t


def build(debug=None, upto=99):
    nc = bass.Bass("TRN2", target_bir_lowering=False)
    I = {n: nc.dram_tensor(n, s, dt, kind="ExternalInput").ap() for (n, s, dt) in IN_SPECS}
    O = {n: nc.dram_tensor(n, s, F32, kind="ExternalOutput").ap() for (n, s) in OUT_SPECS}
    cum_scr = nc.dram_tensor("cum_scr", [H, NP + PAST + NS], F32).ap()
    m_scr = nc.dram_tensor("m_scr", [16, 128, NT], F32).ap()
    m_scrb = nc.dram_tensor("m_scrb", [16, 128, NT], BF16).ap()
    dbg = None
    if debug is not None:
        dbg = nc.dram_tensor("dbg", list(debug), F32, kind="ExternalOutput").ap()

    nc.alloc_sbuf_tensor("arena", [128, 229376 - 16512 - 64], mybir.dt.uint8)
    A = SbAlloc(nc)
    S = Sched(nc)
    banks = [nc.alloc_psum_tensor(f"bank{i}", [128, 512], F32) for i in range(8)]
    bankr = [Res(f"bank{i}") for i in range(8)]

    r_const = Res("const")
    identb = A.alloc("identb", [128, 128], BF16, [r_const])
    identf = A.alloc("identf", [128, 128], F32, [r_const])
    onesb = A.alloc("onesb", [128, 128], BF16, [r_const])
    maskneg = A.alloc("maskneg", [128, 128], F32, [r_const])
    masknegb = A.alloc("masknegb", [128, 128], BF16, [r_const])
    ones1 = A.alloc("ones1", [128, 1], F32, [r_const])
    nbf = A.alloc("nbf", [H, 1], F32, [r_const])
    dconst = S.dmasem("const")
    for t, n in ((identb, "identb"), (identf, "identf"), (onesb, "onesb"), (maskneg, "maskneg"), (masknegb, "masknegb")):
        S.add(SP, DMA(t[:], I[n]), writes=[r_const], dmasem=dconst)
    S.add(SP, DMA(nbf[:], I["fox_bf"]), writes=[r_const], dmasem=dconst)
    S.add(DVE, lambda e: e.memset(ones1[:], 1.0), writes=[r_const])
    S.add(DVE, TS(nbf[:], nbf[:], -1.0, ALU.mult), reads=[r_const], writes=[r_const])

    bm = A.alloc("bm", [128, 48], F32, [r_const])
    epsb = A.alloc("epsb", [128, 1], F32, [r_const])
    S.add(SP, DMA(bm[:], I["b_merge"].rearrange("(c p) o -> p (c o)", p=128), allow_slow_non_contiguous=True),
          writes=[r_const], dmasem=dconst)
    S.add(DVE, lambda e: e.memset(epsb[:], LN_EPS), writes=[r_const])
    r_ncc = Res("ncc")
    ncc = A.alloc("ncc", [128, 25, H], F32, [r_ncc])
    r_ut = Res("utail")
    utail = A.alloc("utail", [128, 2, 12], F32, [r_ut])
    ustail = A.alloc("ustail", [128, 2, 12], F32, [r_ut])
    mkX = A.mark()
    r_xT = [Res(f"xT{i}") for i in range(17)]
    xT = A.alloc("xT", [128, 16, NT], BF16, r_xT)
    def xT_res(t0, n):
        return [r_xT[i] for i, (c0, cn) in enumerate(TCH) if c0 < t0 + n and t0 < c0 + cn]

    mk0 = A.mark()
    r_xb = [Res("xb0"), Res("xb1")]
    xb = [A.alloc("xb0", [128, D], BF16, [r_xb[0]]), A.alloc("xb1", [128, D], BF16, [r_xb[1]])]
    d_xb = [S.dmasem("xb0"), S.dmasem("xb1")]
    for tc, (t0, tn) in enumerate(TCH):
        s = tc % 2
        src = I["xp"][t0:t0 + tn, :] if tc < 16 else I["xs"][:, :]
        S.add(POOL, DMA(xb[s][0:tn, :].rearrange("p (a b) -> p a b", b=1024),
                        src.rearrange("p (a b) -> p a b", b=1024)),
              writes=[r_xb[s]], dmasem=d_xb[s])
        for g in range(4):
            bk = (tc * 4 + g) % 2
            bv = banks[bk]
            for j in range(4):
                c = g * 4 + j
                S.add(PE, TRM(bv[:, j * 128:j * 128 + tn], xb[s][0:tn, c * 128:(c + 1) * 128], identb[0:tn, 0:tn]),
                      reads=[r_xb[s], r_const], writes=[bankr[bk]])
            eng = ACT if (tc * 4 + g) % 2 == 0 else DVE
            S.add(eng, COPY(xT[:, g * 4:g * 4 + 4, t0:t0 + tn],
                            bv[:, 0:512].rearrange("p (a b) -> p a b", b=128)[:, :, 0:tn]),
                  reads=[bankr[bk]], writes=[r_xT[tc]])
    A.release(mk0)

    mk1 = A.mark()
    r_pro = Res("pro")
    wfg = A.alloc("wfg", [128, 16, H], BF16, [r_pro])
    r_lf = Res("logfT")
    logfT = A.alloc("logfT", [H, NT], F32, [r_lf])
    r_lc = Res("lcT")
    lcT = A.alloc("lcT", [H, PAST + NS], F32, [r_lc])
    r_cum = Res("cum")
    cumP = A.alloc("cumP", [H, NP], F32, [r_cum])
    cumS = A.alloc("cumS", [H, PAST + NS], F32, [r_cum])
    r_et = [Res("et0"), Res("et1")]
    etmp = [A.alloc("et0", [H, 512], F32, [r_et[0]]), A.alloc("et1", [H, 512], F32, [r_et[1]])]
    r_clf = Res("clf")
    clf_tm = A.alloc("clf_tm", [128, 8, H], F32, [r_clf])
    r_lfcol = Res("lfcol")
    lfcol = A.alloc("lfcol", [128, 17, H], F32, [r_lfcol])
    d_pro = S.dmasem("pro")
    S.add(POOL, DMA(wfg[:], I["w_in"][:, OFF_F:OFF_F + H].rearrange("(c p) n -> p c n", p=128)),
          writes=[r_pro], dmasem=d_pro)
    d_clf = S.dmasem("clf")
    S.add(SP, DMA(clf_tm[:], I["clf"].rearrange("(c p) h -> p c h", p=128)), writes=[r_clf], dmasem=d_clf)
    for tb, (t0, tn) in enumerate(TBLK):
        bk = tb % 2
        for c in range(16):
            S.add(PE, MM(banks[bk][0:H, 0:tn], wfg[:, c, :], xT[:, c, t0:t0 + tn], c == 0, c == 15),
                  reads=[r_pro] + xT_res(t0, tn), writes=[bankr[bk]])
        S.add(ACT, ACTF(etmp[bk][:, 0:tn], banks[bk][0:H, 0:tn], AF.Exp, bias=nbf[:, 0:1], scale=-1.0),
              reads=[bankr[bk], r_const], writes=[r_et[bk]])
        S.add(ACT, ACTF(etmp[bk][:, 0:tn], etmp[bk][:, 0:tn], AF.Ln, bias=1.0),
              reads=[r_et[bk]], writes=[r_et[bk]])
        S.add(DVE, TS(logfT[:, t0:t0 + tn], etmp[bk][:, 0:tn], -1.0, ALU.mult),
              reads=[r_et[bk]], writes=[r_lf])
    for c in range(8):
        bk = 2 + c // 4
        S.add(PE, TR(banks[bk][0:H, (c % 4) * 128:(c % 4 + 1) * 128], clf_tm[:, c, :], identf[:, :]),
              reads=[r_clf, r_const], writes=[bankr[bk]])
        if c % 4 == 3:
            S.add(ACT, COPY(lcT[:, (c // 4) * 512:(c // 4 + 1) * 512], banks[bk][0:H, :]),
                  reads=[bankr[bk]], writes=[r_lc])
    S.add(DVE, COPY(lcT[:, PAST:PAST + NS], logfT[:, NP:NT]), reads=[r_lf], writes=[r_lc])

    def SCAN(out, data1, n):
        return lambda e: e.tensor_tensor_scan(out=out, data0=ones1[0:H, 0:1].broadcast_to([H, n]), data1=data1,
                                               initial=0.0, op0=ALU.mult, op1=ALU.add)
    S.add(DVE, SCAN(cumP[:], logfT[:, 0:NP], NP), reads=[r_lf, r_const], writes=[r_cum])
    S.add(DVE, SCAN(cumS[:], lcT[:], PAST + NS), reads=[r_lc, r_const], writes=[r_cum])
    r_cscr = Res("cum_scr")
    d_cscr = S.dmasem("cscr")
    S.add(SP, DMA(cum_scr[:, 0:NP], cumP[:]), reads=[r_cum], writes=[r_cscr], dmasem=d_cscr)
    S.add(SP, DMA(cum_scr[:, NP:NP + PAST + NS], cumS[:]), reads=[r_cum], writes=[r_cscr], dmasem=d_cscr)
    for tc, (t0, tn) in enumerate(TCH):
        S.add(PE, TR(banks[4][0:tn, tc * H:(tc + 1) * H], logfT[:, t0:t0 + tn], identf[0:H, 0:H]),
              reads=[r_lf, r_const], writes=[bankr[4]])
    S.add(ACT, COPY(lfcol[:, 0:16, :], banks[4][:, 0:16 * H].rearrange("p (a b) -> p a b", b=H)),
          reads=[bankr[4]], writes=[r_lfcol])
    S.add(ACT, COPY(lfcol[0:NS, 16, :], banks[4][0:NS, 16 * H:17 * H]), reads=[bankr[4]], writes=[r_lfcol])
    d_fl = S.dmasem("fl")
    S.add(SP, DMA(O["fl_p"].rearrange("(c p) h -> p c h", p=128), lfcol[:, 0:16, :]), reads=[r_lfcol], dmasem=d_fl)
    S.add(SP, DMA(O["fl_s"], lfcol[0:NS, 16, :]), reads=[r_lfcol], dmasem=d_fl)
    for c in range(16):
        S.add(PE, TR(banks[5][:, c * H:(c + 1) * H], cumP[:, c * 128:(c + 1) * 128], identf[0:H, 0:H]),
              reads=[r_cum, r_const], writes=[bankr[5]])
    for c in range(9):
        n = 128 if c < 8 else NS
        S.add(PE, TR(banks[5][0:n, (16 + c) * H:(17 + c) * H], cumS[:, c * 128:c * 128 + n], identf[0:H, 0:H]),
              reads=[r_cum, r_const], writes=[bankr[5]])
    S.add(DVE, TS(ncc[:, 0:24, :], banks[5][:, 0:24 * H].rearrange("p (a b) -> p a b", b=H), -1.0, ALU.mult),
          reads=[bankr[5]], writes=[r_ncc])
    S.add(DVE, TS(ncc[0:NS, 24, :], banks[5][0:NS, 24 * H:25 * H], -1.0, ALU.mult),
          reads=[bankr[5]], writes=[r_ncc])
    A.release(mk1)

    mk2 = A.mark()
    r_aT = [[Res(f"aT{h}_{tb}") for tb in range(5)] for h in range(H)]
    aT = A.alloc("aT", [128, H, NT], BF16, [r for l in r_aT for r in l])
    mkB = A.mark()
    r_w = [Res("wq"), Res("wkv"), Res("wg")]
    wbuf = A.alloc("wbuf", [128, 16, 512], BF16, r_w)
    d_w = [S.dmasem("wq"), S.dmasem("wkv"), S.dmasem("wg")]
    r_qT = [Res(f"qT{i}") for i in range(5)]
    qT = A.alloc("qT", [128, NT], BF16, r_qT)
    r_kT = [Res(f"kT{i}") for i in range(5)]
    kT = A.alloc("kT", [128, NT], BF16, r_kT)
    r_vb = [Res(f"vb{i}") for i in range(5)]
    vb = A.alloc("vb", [128, 17, 128], BF16, r_vb)
    r_sg = [Res(f"sg{i}") for i in range(5)]
    sg = A.alloc("sg", [128, NT], F32, r_sg)
    r_cqb = Res("cqb")
    cqb = A.alloc("cqb", [128, NT], F32, [r_cqb])
    d_cqb = S.dmasem("cqb")
    r_kv32 = [Res("kv32_0"), Res("kv32_1")]
    kv32 = [A.alloc("kv32_0", [128, 4, 256], F32, [r_kv32[0]]), A.alloc("kv32_1", [128, 4, 256], F32, [r_kv32[1]])]
    d_kvo = [S.dmasem("kvo0"), S.dmasem("kvo1")]
    r_kb = [Res("kb0"), Res("kb1")]
    kb = [A.alloc("kb0", [128, 4, 128], BF16, [r_kb[0]]), A.alloc("kb1", [128, 4, 128], BF16, [r_kb[1]])]
    r_cache = Res("cache")
    kc_tm = A.alloc("kc_tm", [128, 8, 128], BF16, [r_cache])
    vc_tm = A.alloc("vc_tm", [128, 8, 128], BF16, [r_cache])
    d_cache = S.dmasem("cache")
    r_kcT = Res("kcT")
    kcT = A.alloc("kcT", [128, PAST], BF16, [r_kcT])
    NST, NPT = 4, 6
    r_st = [Res(f"st{i}") for i in range(NST)]
    stmp = [A.alloc(f"st{i}", [128, 512], F32, [r_st[i]]) for i in range(NST)]
    r_pt = [Res(f"pt{i}") for i in range(NPT)]
    pt = [A.alloc(f"pt{i}", [128, 512], BF16, [r_pt[i]]) for i in range(NPT)]
    SRING = [2, 3, 0, 1]
    stk = [0]
    r_rl = Res("rl")
    rl = A.alloc("rl", [128, 512], F32, [r_rl])
    r_on = Res("on")
    on = A.alloc("on", [128, 512], F32, [r_on])

    def load_w(h):
        for comp, (off, slot) in enumerate(((OFF_Q, 0), (OFF_K, 1), (OFF_V, 1), (OFF_G, 2))):
            S.add(POOL, DMA(wbuf[:, :, comp * 128:(comp + 1) * 128],
                            I["w_in"][:, off + h * 128:off + (h + 1) * 128].rearrange("(c p) n -> p c n", p=128)),
                  writes=[r_w[slot]], dmasem=d_w[slot])

    def load_cache(h):
        S.add(POOL, DMA(kc_tm[:], I["ck"][:, h * 128:(h + 1) * 128].rearrange("(c p) n -> p c n", p=128)),
              writes=[r_cache], dmasem=d_cache)
        S.add(POOL, DMA(vc_tm[:], I["cv"][:, h * 128:(h + 1) * 128].rearrange("(c p) n -> p c n", p=128)),
              writes=[r_cache], dmasem=d_cache)

    def load_cqb(h):
        S.add(SP, DMA(cqb[:, 0:NP], cum_scr[h:h + 1, 0:NP].partition_broadcast(128)),
              reads=[r_cscr], writes=[r_cqb], dmasem=d_cqb)
        S.add(SP, DMA(cqb[:, NP:NT], cum_scr[h:h + 1, NP + PAST:NP + PAST + NS].partition_broadcast(128)),
              reads=[r_cscr], writes=[r_cqb], dmasem=d_cqb)

    pj = [0]
    sbk = [0]
    pti = [0]
    obk = [0]

    nheads = H if upto >= 2 else 1
    load_w(0)
    load_cache(0)
    load_cqb(0)
    for h in range(nheads):
        pending_kt = []
        for grp in range(5):
            chunks = [(tc, TCH[tc]) for tc in range(grp * 4, min(grp * 4 + 4, 17))]
            s = grp % 2
            for j, (tc, (t0, tn)) in enumerate(chunks):
                bk = pj[0]; pj[0] ^= 1
                for c in range(16):
                    S.add(PE, MM(banks[bk][0:tn, 0:256], xT[:, c, t0:t0 + tn],
                                 wbuf[:, c, 128:384], c == 0, c == 15),
                          reads=[r_w[1], r_xT[tc]], writes=[bankr[bk]])
                S.add(ACT, COPY(kv32[s][0:tn, j, :], banks[bk][0:tn, 0:256]),
                      reads=[bankr[bk]], writes=[r_kv32[s]])
            nj = len(chunks)
            tn = chunks[0][1][1]
            t0 = chunks[0][1][0]
            S.add(POOL, COPY(kb[s][0:tn, 0:nj, :], kv32[s][0:tn, 0:nj, 0:128]), reads=[r_kv32[s]], writes=[r_kb[s]])
            S.add(POOL, COPY(vb[0:tn, grp * 4:grp * 4 + nj, :], kv32[s][0:tn, 0:nj, 128:256]),
                  reads=[r_kv32[s]], writes=[r_vb[grp]])
            if grp < 4:
                S.add(SP, DMA(O["fk_p"][t0:t0 + 512, h * 128:(h + 1) * 128].rearrange("(c p) n -> p c n", p=128),
                              kv32[s][:, :, 0:128]), reads=[r_kv32[s]], dmasem=d_kvo[s])
                S.add(SP, DMA(O["fv_p"][t0:t0 + 512, h * 128:(h + 1) * 128].rearrange("(c p) n -> p c n", p=128),
                              kv32[s][:, :, 128:256]), reads=[r_kv32[s]], dmasem=d_kvo[s])
            else:
                S.add(SP, DMA(O["fk_s"][:, h * 128:(h + 1) * 128], kv32[s][0:NS, 0, 0:128]),
                      reads=[r_kv32[s]], dmasem=d_kvo[s])
                S.add(SP, DMA(O["fv_s"][:, h * 128:(h + 1) * 128], kv32[s][0:NS, 0, 128:256]),
                      reads=[r_kv32[s]], dmasem=d_kvo[s])
            def ktrans(grp=grp, s=s, nj=nj, tn=tn, t0=t0):
                bk = pj[0]; pj[0] ^= 1
                bv = banks[bk]
                for j in range(nj):
                    S.add(PE, TRM(bv[:, j * 128:j * 128 + tn], kb[s][0:tn, j, :], identb[0:tn, 0:tn]),
                          reads=[r_kb[s], r_const], writes=[bankr[bk]])
                S.add(ACT, COPY(kT[:, t0:t0 + nj * tn], bv[:, 0:nj * 128] if tn == 128 else bv[:, 0:tn]),
                      reads=[bankr[bk]], writes=[r_kT[grp]])
            if pending_kt:
                pending_kt.pop()()
            pending_kt.append(ktrans)
        for comp, dst, rr, slot in ((0, qT, r_qT, 0), (3, sg, r_sg, 2)):
            for tb, (t0, tn) in enumerate(TBLK):
                if tb == 1 and pending_kt:
                    pending_kt.pop()()
                bk = pj[0]; pj[0] ^= 1
                for c in range(16):
                    S.add(PE, MM(banks[bk][:, 0:tn], wbuf[:, c, comp * 128:(comp + 1) * 128], xT[:, c, t0:t0 + tn], c == 0, c == 15),
                          reads=[r_w[slot]] + xT_res(t0, tn), writes=[bankr[bk]])
                if comp == 0:
                    S.add(ACT, COPY(dst[:, t0:t0 + tn], banks[bk][:, 0:tn]), reads=[bankr[bk]], writes=[rr[tb]])
                else:
                    S.add(ACT, ACTF(dst[:, t0:t0 + tn], banks[bk][:, 0:tn], AF.Silu), reads=[bankr[bk]], writes=[rr[tb]])
        if h + 1 < nheads:
            load_w(h + 1)
        for half in range(2):
            bk = pj[0]; pj[0] ^= 1
            bv = banks[bk]
            for j in range(4):
                S.add(PE, TRM(bv[:, j * 128:(j + 1) * 128], kc_tm[:, half * 4 + j, :], identb[:, :]),
                      reads=[r_cache, r_const], writes=[bankr[bk]])
            S.add(ACT, COPY(kcT[:, half * 512:(half + 1) * 512], bv[:, 0:512]), reads=[bankr[bk]], writes=[r_kcT])

        tiles = []

        def attend(qcol0, qn, keys, tbidx):
            ob = obk[0]; obk[0] ^= 1
            nk = len(keys)
            for i, kk in enumerate(keys):
                tiles.append((qcol0, qn, tbidx, 4 + ob, 6 + ob, i == 0, i == nk - 1) + kk)

        for qb in range(4):
            keys = []
            for kc in range(4 * qb + 4):
                diag = kc >= 4 * qb
                n0 = (kc - 4 * qb) * 128 if diag else 0
                keys.append((kT[:, kc * 128:(kc + 1) * 128], 128, vb[:, kc, :], ncc[:, kc, h:h + 1], n0, diag,
                             [r_kT[kc // 4], r_vb[kc // 4]]))
            attend(qb * 512, 512, keys, qb)
        keys = []
        for kc in range(8):
            keys.append((kcT[:, kc * 128:(kc + 1) * 128], 128, vc_tm[:, kc, :], ncc[:, 16 + kc, h:h + 1], 0, False,
                         [r_kcT, r_cache]))
        keys.append((kT[:, NP:NT], NS, vb[0:NS, 16, :], ncc[0:NS, 24, h:h + 1], 0, True, [r_kT[4], r_vb[4]]))
        attend(NP, NS, keys, 4)

        LA = 3
        slots = {}

        def issue_S(i):
            (qcol0, qn, tbidx, Ob, Lb, first, last, kl, kn, vv, bias, n0, diag, rds) = tiles[i]
            sb_ = SRING[sbk[0] % 4]; sbk[0] += 1
            st = stk[0] % NST; stk[0] += 1
            p = pti[0] % NPT; pti[0] += 1
            slots[i] = p
            S.add(PE, MM(banks[sb_][0:kn, n0:qn], kl, qT[:, qcol0 + n0:qcol0 + qn], True, not diag),
                  reads=rds + [r_qT[tbidx]], writes=[bankr[sb_]])
            if diag:
                S.add(PE, MM(banks[sb_][0:kn, n0:n0 + kn], identb[0:kn, 0:kn], masknegb[0:kn, 0:kn], False, True),
                      reads=[r_const], writes=[bankr[sb_]])
            S.add(DVE, STT(stmp[st][0:kn, n0:qn], banks[sb_][0:kn, n0:qn], SCALE,
                           cqb[0:kn, qcol0 + n0:qcol0 + qn], ALU.mult, ALU.add),
                  reads=[bankr[sb_], r_cqb], writes=[r_st[st]])
            S.add(ACT, ACTF(pt[p][0:kn, n0:qn], stmp[st][0:kn, n0:qn], AF.Exp, bias=bias),
                  reads=[r_st[st], r_ncc], writes=[r_pt[p]])

        def issue_PV(i):
            (qcol0, qn, tbidx, Ob, Lb, first, last, kl, kn, vv, bias, n0, diag, rds) = tiles[i]
            p = slots.pop(i)
            S.add(PE, MM(banks[Ob][:, n0:qn], vv, pt[p][0:kn, n0:qn], first, last),
                  reads=[r_pt[p]] + rds, writes=[bankr[Ob]])
            S.add(PE, MM(banks[Lb][:, n0:qn], onesb[0:kn, :], pt[p][0:kn, n0:qn], first, last),
                  reads=[r_pt[p], r_const], writes=[bankr[Lb]])
            if last:
                S.add(ACT, ACTF(rl[:, 0:qn], banks[Lb][:, 0:qn], AF.Ln), reads=[bankr[Lb]], writes=[r_rl])
                S.add(ACT, ACTF(rl[:, 0:qn], rl[:, 0:qn], AF.Exp, scale=-1.0), reads=[r_rl], writes=[r_rl])
                S.add(DVE, TT(on[:, 0:qn], banks[Ob][:, 0:qn], rl[:, 0:qn], ALU.mult), reads=[bankr[Ob], r_rl], writes=[r_on])
                S.add(POOL, TT(aT[:, h, qcol0:qcol0 + qn], on[:, 0:qn], sg[:, qcol0:qcol0 + qn], ALU.mult),
                      reads=[r_on, r_sg[tbidx]], writes=[r_aT[h][tbidx]])

        nt_ = len(tiles)
        for i in range(min(LA, nt_)):
            issue_S(i)
        for i in range(nt_):
            if i + LA < nt_:
                issue_S(i + LA)
            issue_PV(i)
        if h + 1 < nheads:
            load_cache(h + 1)
            load_cqb(h + 1)

    if debug is not None and upto <= 2:
        r_d = Res("dbgt")
        dt_ = A.alloc("dbgt", [128, NT], F32, [r_d])
        d_dbg = S.dmasem("dbg")
        for h in range(nheads):
            S.add(DVE, COPY(dt_[:], aT[:, h, :]), reads=r_aT[h], writes=[r_d])
            S.add(SP, DMA(dbg[:, h, :], dt_[:]), reads=[r_d], dmasem=d_dbg)

    A.release(mkB)
    r_mscr = [[Res(f"mscr{j}_{tb}") for tb in range(5)] for j in range(16)]

    def combine(aT_, r_a, nchunks, w_out, gate, first, hooks=None, last=False):
        mkc = A.mark()
        r_wo = [Res("wo0"), Res("wo1")]
        wo = [A.alloc(f"wo{i}", [128, nchunks, 256], BF16, [r_wo[i]]) for i in range(2)]
        wm = [A.alloc(f"wm{i}", [128, 16, 256], BF16, [r_wo[i]]) for i in range(2)]
        d_wo = [S.dmasem(f"wo{gate}_0"), S.dmasem(f"wo{gate}_1")]
        r_gt = [Res("gt0"), Res("gt1")]
        gt = [A.alloc(f"gt{i}", [128, 512], F32, [r_gt[i]]) for i in range(2)]
        r_mt = [Res("mt0"), Res("mt1")]
        mt = [A.alloc(f"mt{i}", [128, 512], F32, [r_mt[i]]) for i in range(2)]
        d_mo = [S.dmasem(f"mo{gate}_0"), S.dmasem(f"mo{gate}_1")]
        r_pv = [Res("pv0"), Res("pv1")]
        pv = [A.alloc(f"pv{i}", [128, 512], F32, [r_pv[i]]) for i in range(2)]
        if last:
            r_mtb = [Res("mtb0"), Res("mtb1")]
            mtb = [A.alloc(f"mtb{i}", [128, 512], BF16, [r_mtb[i]]) for i in range(2)]
        d_pv = [S.dmasem(f"pv{gate}_0"), S.dmasem(f"pv{gate}_1")]

        def loadw(jg):
            s = jg % 2
            S.add(POOL, DMA(wo[s][:], w_out[:, jg * 256:(jg + 1) * 256].rearrange("(c p) n -> p c n", p=128)),
                  writes=[r_wo[s]], dmasem=d_wo[s])
            S.add(POOL, DMA(wm[s][:], I["w_merge"][:, gate * D + jg * 256:gate * D + (jg + 1) * 256]
                            .rearrange("(c p) n -> p c n", p=128)),
                  writes=[r_wo[s]], dmasem=d_wo[s])
        loadw(0)
        it = 0
        for jg in range(8):
            if jg + 1 < 8:
                loadw(jg + 1)
            if hooks and jg in hooks:
                hooks[jg]()
            s = jg % 2
            for jj in range(2):
                j = jg * 2 + jj
                for tb, (t0, tn) in enumerate(TBLK):
                    p2 = it % 2
                    it += 1
                    yb, gb = 2 * p2, 2 * p2 + 1
                    if not first:
                        S.add(SP, DMA(pv[p2][:, 0:tn], m_scr[j, :, t0:t0 + tn]), reads=[r_mscr[j][tb]],
                              writes=[r_pv[p2]], dmasem=d_pv[p2])
                    for c in range(16):
                        S.add(PE, MM(banks[gb][:, 0:tn], wm[s][:, c, jj * 128:(jj + 1) * 128], xT[:, c, t0:t0 + tn],
                                     c == 0, c == 15), reads=[r_wo[s]] + xT_res(t0, tn), writes=[bankr[gb]])
                    for c in range(nchunks):
                        S.add(PE, MM(banks[yb][:, 0:tn], wo[s][:, c, jj * 128:(jj + 1) * 128], aT_[:, c, t0:t0 + tn],
                                     c == 0, c == nchunks - 1), reads=[r_wo[s], r_a(c, tb)], writes=[bankr[yb]])
                    S.add(ACT, ACTF(gt[p2][:, 0:tn], banks[gb][:, 0:tn], AF.Sigmoid, bias=bm[:, gate * 16 + j:gate * 16 + j + 1]),
                          reads=[bankr[gb], r_const], writes=[r_gt[p2]])
                    S.add(DVE, TT(mt[p2][:, 0:tn], banks[yb][:, 0:tn], gt[p2][:, 0:tn], ALU.mult),
                          reads=[bankr[yb], r_gt[p2]], writes=[r_mt[p2]])
                    if last:
                        S.add(POOL, TT(mtb[p2][:, 0:tn], mt[p2][:, 0:tn], pv[p2][:, 0:tn], ALU.add),
                              reads=[r_mt[p2], r_pv[p2]], writes=[r_mtb[p2]])
                        S.add(SP, DMA(m_scrb[j, :, t0:t0 + tn], mtb[p2][:, 0:tn]), reads=[r_mtb[p2]],
                              writes=[r_mscr[j][tb]], dmasem=d_mo[p2])
                        continue
                    if not first:
                        S.add(POOL, TT(mt[p2][:, 0:tn], mt[p2][:, 0:tn], pv[p2][:, 0:tn], ALU.add),
                              reads=[r_mt[p2], r_pv[p2]], writes=[r_mt[p2]])
                    S.add(SP, DMA(m_scr[j, :, t0:t0 + tn], mt[p2][:, 0:tn]), reads=[r_mt[p2]],
                          writes=[r_mscr[j][tb]], dmasem=d_mo[p2])
        A.release(mkc)

    chi = A.mark_hi()
    r_cw = [Res("cw0"), Res("cw1")]
    cwb = [A.alloc(f"cwb{i}", [128, 16, 512], BF16, [r_cw[i]], high=True) for i in range(2)]
    d_cw = [S.dmasem("cw0"), S.dmasem("cw1")]

    def load_cw(c):
        s = c % 2
        for comp, off in enumerate((OFF_CB, OFF_CC, OFF_CH, OFF_CG)):
            S.add(POOL, DMA(cwb[s][:, :, comp * 128:(comp + 1) * 128],
                            I["w_in"][:, off + c * 128:off + (c + 1) * 128].rearrange("(c p) n -> p c n", p=128)),
                  writes=[r_cw[s]], dmasem=d_cw[s])
    if upto >= 3:
        combine(aT, lambda c, tb: r_aT[c][tb], H, I["w_fox_out"], 0, True,
                hooks={7: (lambda: load_cw(0))} if upto >= 4 else None)
    A.release(mk2)

    if upto >= 4:
        mk3 = A.mark()
        r_aC = [[Res(f"aC{c}_{tb}") for tb in range(5)] for c in range(12)]
        aC = A.alloc("aC", [128, 12, NT], BF16, [r for l in r_aC for r in l])
        mk3b = A.mark()
        r_cs = Res("convsmall")
        cwT = A.alloc("cwT", [128, 3, 12], F32, [r_cs])
        cbT = A.alloc("cbT", [128, 12], F32, [r_cs])
        sconvT = A.alloc("sconvT", [128, 2, 12], F32, [r_cs])
        d_cs = S.dmasem("convsmall")
        for j in range(3):
            S.add(SP, DMA(cwT[:, j, :], I["conv_w"][j:j + 1, :].rearrange("o (c p) -> p (o c)", p=128),
                          allow_slow_non_contiguous=True), writes=[r_cs], dmasem=d_cs)
        S.add(SP, DMA(cbT[:], I["conv_b"].rearrange("(c p) o -> p (c o)", p=128), allow_slow_non_contiguous=True),
              writes=[r_cs], dmasem=d_cs)
        for j in range(2):
            S.add(SP, DMA(sconvT[:, j, :], I["sconv"][j:j + 1, :].rearrange("o (c p) -> p (o c)", p=128),
                          allow_slow_non_contiguous=True), writes=[r_cs], dmasem=d_cs)
        r_cg = [Res(f"cg{i}") for i in range(5)]
        cg = A.alloc("cg", [128, NT], F32, r_cg)
        r_u = [Res(f"u{i}") for i in range(5)]
        r_u0 = Res("u0")
        up = A.alloc("up", [128, 2 + NP], F32, r_u[0:4] + [r_u0])
        us = A.alloc("us", [128, 2 + NS], F32, [r_u[4]])
        r_bg = [Res(f"bg{i}") for i in range(5)]
        bg = A.alloc("bg", [128, NT], F32, r_bg)
        r_sc = [Res(f"sc{i}") for i in range(5)]
        scg = A.alloc("scg", [128, NT], F32, r_sc)
        r_tb = Res("tbuf")
        tbuf = A.alloc("tbuf", [128, NT], F32, [r_tb])
        S.add(POOL, lambda e: e.memset(up[:, 0:2], 0.0), writes=[r_u0])

        for c in range(12):
            s = c % 2
            for tb, (t0, tn) in enumerate(TBLK):
                if tb == 1 and c + 1 < 12:
                    load_cw(c + 1)
                def proj(comp):
                    bk = pj[0]; pj[0] ^= 1
                    for kc in range(16):
                        S.add(PE, MM(banks[bk][:, 0:tn], cwb[s][:, kc, comp * 128:(comp + 1) * 128], xT[:, kc, t0:t0 + tn],
                                     kc == 0, kc == 15), reads=[r_cw[s]] + xT_res(t0, tn), writes=[bankr[bk]])
                    return bk
                bk = proj(1)
                S.add(ACT, COPY(cg[:, t0:t0 + tn], banks[bk][:, 0:tn]), reads=[bankr[bk]], writes=[r_cg[tb]])
                bk = proj(2)
                udst = up[:, 2 + t0:2 + t0 + tn] if tb < 4 else us[:, 2:2 + NS]
                S.add(DVE, TT(udst, banks[bk][:, 0:tn], cg[:, t0:t0 + tn], ALU.mult),
                      reads=[bankr[bk], r_cg[tb]], writes=[r_u[tb]])
                bk = proj(0)
                S.add(ACT, COPY(bg[:, t0:t0 + tn], banks[bk][:, 0:tn]), reads=[bankr[bk]], writes=[r_bg[tb]])
                bk = proj(3)
                S.add(ACT, ACTF(scg[:, t0:t0 + tn], banks[bk][:, 0:tn], AF.Silu), reads=[bankr[bk]], writes=[r_sc[tb]])
            S.add(POOL, COPY(us[:, 0:2], sconvT[:, :, c]), reads=[r_cs], writes=[r_u[4]])
            for (ub, n, o0, rr) in ((up, NP, 0, r_u[0:4] + [r_u0]), (us, NS, NP, [r_u[4]])):
                S.add(DVE, TS(tbuf[:, o0:o0 + n], ub[:, 2:2 + n], cwT[:, 2, c:c + 1], ALU.mult, cbT[:, c:c + 1], ALU.add),
                      reads=rr + [r_cs], writes=[r_tb])
                S.add(DVE, STT(tbuf[:, o0:o0 + n], ub[:, 1:1 + n], cwT[:, 1, c:c + 1], tbuf[:, o0:o0 + n], ALU.mult, ALU.add),
                      reads=rr + [r_cs, r_tb], writes=[r_tb])
                S.add(DVE, STT(tbuf[:, o0:o0 + n], ub[:, 0:n], cwT[:, 0, c:c + 1], tbuf[:, o0:o0 + n], ALU.mult, ALU.add),
                      reads=rr + [r_cs, r_tb], writes=[r_tb])
            S.add(POOL, COPY(utail[:, :, c], up[:, NP:NP + 2]), reads=[r_u[3]], writes=[r_ut])
            S.add(POOL, COPY(ustail[:, :, c], us[:, NS:NS + 2]), reads=[r_u[4]], writes=[r_ut])
            S.add(POOL, TT(tbuf[:, :], tbuf[:, :], bg[:, :], ALU.mult), reads=[r_tb] + r_bg, writes=[r_tb])
            S.add(POOL, TT(aC[:, c, :], tbuf[:, :], scg[:, :], ALU.mult), reads=[r_tb] + r_sc, writes=r_aC[c])
        d_ut = S.dmasem("utail")
        for j in range(2):
            S.add(SP, DMA(O["cv_p"][j:j + 1, :].rearrange("o (c p) -> p (o c)", p=128), utail[:, j, :],
                          allow_slow_non_contiguous=True), reads=[r_ut], dmasem=d_ut)
            S.add(SP, DMA(O["cv_s"][j:j + 1, :].rearrange("o (c p) -> p (o c)", p=128), ustail[:, j, :],
                          allow_slow_non_contiguous=True), reads=[r_ut], dmasem=d_ut)
        A.release(mk3b)
        A.release_hi(chi)
        mhi = A.mark_hi()
        r_mw = [Res("mw0"), Res("mw1")]
        mwb = [A.alloc(f"mwb{i}", [128, 16, 512], BF16, [r_mw[i]], high=True) for i in range(2)]
        d_mw = [S.dmasem("mw0"), S.dmasem("mw1")]
        r_memb = Res("memb")
        memb = A.alloc("memb", [128, 2, D], BF16, [r_memb], high=True)
        d_memb = S.dmasem("memb")

        def load_mkv(nb):
            s = nb % 2
            S.add(POOL, DMA(mwb[s][:], I["w_mem_kv"][:, nb * 512:(nb + 1) * 512].rearrange("(c p) n -> p c n", p=128)),
                  writes=[r_mw[s]], dmasem=d_mw[s])

        def early_mem_a():
            S.add(POOL, DMA(memb[:].rearrange("p c (a b) -> p c a b", b=1024),
                            I["mem"].rearrange("(c p) (a b) -> p c a b", p=128, b=1024)),
                  writes=[r_memb], dmasem=d_memb)
            load_mkv(0)
        if upto >= 5:
            combine(aC, lambda c, tb: r_aC[c][tb], 12, I["w_conv_out"], 1, False,
                    hooks={5: early_mem_a, 6: lambda: load_mkv(1)})
        A.release(mk3)

    if upto >= 6:
        mk4 = A.mark()
        r_aM = [[Res(f"aM{c}_{tb}") for tb in range(5)] for c in range(8)]
        aM = A.alloc("aM", [128, 8, NT], BF16, [r for l in r_aM for r in l])
        mk4b = A.mark()
        r_mkb = [Res("mkb_p"), Res("mkb_s")]
        mkb = [A.alloc("mkb_p", [128, 2, 2 * MW], BF16, [r_mkb[0]]), A.alloc("mkb_s", [128, 2, 2 * MW], BF16, [r_mkb[1]])]
        d_cm = S.dmasem("cmem")
        r_mkT = [Res("mkT_p"), Res("mkT_s")]
        mkT = [A.alloc("mkT_p", [128, 8, MT], BF16, [r_mkT[0]]), A.alloc("mkT_s", [128, 8, MT], BF16, [r_mkT[1]])]
        mk4c = A.mark()
        r_memT = Res("memT")
        memT = A.alloc("memT", [128, 16, MT], BF16, [r_memT])
        r_m32 = [Res("m32_0"), Res("m32_1")]
        m32 = [A.alloc(f"m32_{i}", [128, 512], F32, [r_m32[i]]) for i in range(2)]
        d_m32 = [S.dmasem("m32_0"), S.dmasem("m32_1")]

        S.add(POOL, DMA(mkb[1][:, :, 0:MW], I["cmk"].rearrange("(c p) n -> p c n", p=128)), writes=[r_mkb[1]], dmasem=d_cm)
        S.add(POOL, DMA(mkb[1][:, :, MW:2 * MW], I["cmv"].rearrange("(c p) n -> p c n", p=128)), writes=[r_mkb[1]], dmasem=d_cm)
        for mc in range(2):
            for g in range(4):
                bk = pj[0]; pj[0] ^= 1
                bv = banks[bk]
                for j in range(4):
                    c = g * 4 + j
                    S.add(PE, TRM(bv[:, j * 128:(j + 1) * 128], memb[:, mc, c * 128:(c + 1) * 128], identb[:, :]),
                          reads=[r_memb, r_const], writes=[bankr[bk]])
                S.add(ACT, COPY(memT[:, g * 4:g * 4 + 4, mc * 128:(mc + 1) * 128],
                                bv[:, 0:512].rearrange("p (a b) -> p a b", b=128)), reads=[bankr[bk]], writes=[r_memT])

        def load_mq(h, s):
            S.add(POOL, DMA(mwb[s][:, :, 0:256], I["w_in"][:, OFF_MQ + h * 256:OFF_MQ + (h + 1) * 256]
                            .rearrange("(c p) n -> p c n", p=128)), writes=[r_mw[s]], dmasem=d_mw[s])
            S.add(POOL, DMA(mwb[s][:, :, 256:512], I["w_in"][:, OFF_MG + h * 256:OFF_MG + (h + 1) * 256]
                            .rearrange("(c p) n -> p c n", p=128)), writes=[r_mw[s]], dmasem=d_mw[s])
        it = 0
        for nb in range(4):
            if 1 <= nb + 1 < 4 and nb >= 1:
                load_mkv(nb + 1)
            elif nb == 3:
                load_mq(0, 0)
            s = nb % 2
            for mc in range(2):
                bk = pj[0]; pj[0] ^= 1
                p2 = it % 2; it += 1
                for c in range(16):
                    S.add(PE, MM(banks[bk][:, :], memT[:, c, mc * 128:(mc + 1) * 128], mwb[s][:, c, :], c == 0, c == 15),
                          reads=[r_memT, r_mw[s]], writes=[bankr[bk]])
                S.add(ACT, COPY(m32[p2][:], banks[bk][:, :]), reads=[bankr[bk]], writes=[r_m32[p2]])
                S.add(POOL, COPY(mkb[0][:, mc, nb * 512:(nb + 1) * 512], m32[p2][:]), reads=[r_m32[p2]], writes=[r_mkb[0]])
                dst = O["mk_p"] if nb < 2 else O["mv_p"]
                S.add(SP, DMA(dst[mc * 128:(mc + 1) * 128, (nb % 2) * 512:(nb % 2 + 1) * 512], m32[p2][:]),
                      reads=[r_m32[p2]], dmasem=d_m32[p2])
        load_mq(1, 1)
        for src in range(2):
            for hd in range(8):
                bk = pj[0]; pj[0] ^= 1
                bv = banks[bk]
                for mc in range(2):
                    S.add(PE, TRM(bv[:, mc * 128:(mc + 1) * 128], mkb[src][:, mc, hd * 128:(hd + 1) * 128], identb[:, :]),
                          reads=[r_mkb[src], r_const], writes=[bankr[bk]])
                S.add(ACT, COPY(mkT[src][:, hd, :], bv[:, 0:256]), reads=[bankr[bk]], writes=[r_mkT[src]])

        A.release(mk4c)
        r_mq = [Res(f"mq{i}") for i in range(5)]
        mqT = A.alloc("mqT", [128, 2, NT], BF16, r_mq)
        r_sgm = [Res(f"sgm{i}") for i in range(5)]
        sgm = A.alloc("sgm", [128, 2, NT], F32, r_sgm)
        r_mpt = [Res("mpt0"), Res("mpt1"), Res("mpt2"), Res("mpt3")]
        mpt = [A.alloc(f"mpt{i}", [128, 512], BF16, [r_mpt[i]]) for i in range(4)]
        r_mrl = Res("mrl")
        mrl = A.alloc("mrl", [128, 512], F32, [r_mrl])
        r_mon = [Res("mon0"), Res("mon1")]
        mon = [A.alloc(f"mon{i}", [128, 512], F32, [r_mon[i]]) for i in range(2)]

        def mproj(h, tb):
            s = h % 2
            t0, tn = TBLK[tb]
            for comp in range(4):
                bk = pj[0]; pj[0] ^= 1
                for c in range(16):
                    S.add(PE, MM(banks[bk][:, 0:tn], mwb[s][:, c, comp * 128:(comp + 1) * 128], xT[:, c, t0:t0 + tn],
                                 c == 0, c == 15), reads=[r_mw[s]] + xT_res(t0, tn), writes=[bankr[bk]])
                if comp < 2:
                    S.add(ACT, COPY(mqT[:, comp, t0:t0 + tn], banks[bk][:, 0:tn]), reads=[bankr[bk]], writes=[r_mq[tb]])
                else:
                    S.add(ACT, ACTF(sgm[:, comp - 2, t0:t0 + tn], banks[bk][:, 0:tn], AF.Silu),
                          reads=[bankr[bk]], writes=[r_sgm[tb]])

        units = [(h, tb) for h in range(4) for tb in range(5)]
        mproj(0, 0)
        for ui, (h, tb) in enumerate(units):
            t0, tn = TBLK[tb]
            src = 0 if tb < 4 else 1
            pts = []
            for mc in range(2):
                sb_ = 2 + mc
                p = pti[0] % 4; pti[0] += 1
                pts.append(p)
                for dc in range(2):
                    S.add(PE, MM(banks[sb_][:, 0:tn], mkT[src][:, h * 2 + dc, mc * 128:(mc + 1) * 128],
                                 mqT[:, dc, t0:t0 + tn], dc == 0, dc == 1),
                          reads=[r_mkT[src], r_mq[tb]], writes=[bankr[sb_]])
                S.add(ACT, ACTF(mpt[p][:, 0:tn], banks[sb_][:, 0:tn], AF.Exp, scale=MSCALE),
                      reads=[bankr[sb_]], writes=[r_mpt[p]])
            if ui + 1 < len(units):
                nh, ntb = units[ui + 1]
                mproj(nh, ntb)
                if ntb == 4 and nh + 2 < 4:
                    load_mq(nh + 2, nh % 2)
            for mc in range(2):
                S.add(PE, MM(banks[6][:, 0:tn], onesb[:, :], mpt[pts[mc]][:, 0:tn], mc == 0, mc == 1),
                      reads=[r_mpt[pts[mc]], r_const], writes=[bankr[6]])
            for dc in range(2):
                for mc in range(2):
                    S.add(PE, MM(banks[4 + dc][:, 0:tn],
                                 mkb[src][:, mc, MW + h * 256 + dc * 128:MW + h * 256 + (dc + 1) * 128],
                                 mpt[pts[mc]][:, 0:tn], mc == 0, mc == 1),
                          reads=[r_mpt[pts[mc]], r_mkb[src]], writes=[bankr[4 + dc]])
            S.add(DVE, lambda e, tn=tn: e.reciprocal(out=mrl[:, 0:tn], in_=banks[6][:, 0:tn]),
                  reads=[bankr[6]], writes=[r_mrl])
            for dc in range(2):
                S.add(DVE, TT(mon[dc][:, 0:tn], banks[4 + dc][:, 0:tn], mrl[:, 0:tn], ALU.mult),
                      reads=[bankr[4 + dc], r_mrl], writes=[r_mon[dc]])
                S.add(POOL, TT(aM[:, h * 2 + dc, t0:t0 + tn], mon[dc][:, 0:tn], sgm[:, dc, t0:t0 + tn], ALU.mult),
                      reads=[r_mon[dc], r_sgm[tb]], writes=[r_aM[h * 2 + dc][tb]])
        A.release(mk4b)
        A.release_hi(mhi)
        r_wob = Res("wob")
        wob = A.alloc("wob", [128, 16, D], BF16, [r_wob], high=True)
        d_wob = S.dmasem("wob")
        def wob_load(q4):
            return lambda: S.add(POOL, DMA(wob[:, :, q4 * 512:(q4 + 1) * 512],
                                           I["w_o"][:, q4 * 512:(q4 + 1) * 512].rearrange("(c p) n -> p c n", p=128)),
                                 writes=[r_wob], dmasem=d_wob)
        if upto >= 7:
            combine(aM, lambda c, tb: r_aM[c][tb], 8, I["w_mem_out"], 2, False,
                    hooks={1: wob_load(0), 3: wob_load(1), 5: wob_load(2), 7: wob_load(3)}, last=True)
        A.release(mk4)

    if upto >= 8:
        A.release(mkX)
        r_mT = [Res(f"mT{i}") for i in range(17)]
        mT = A.alloc("mT", [128, 16, NT], BF16, r_mT)
        d_mT = [S.dmasem(f"mT{i}") for i in range(5)]
        r_ln = Res("ln")
        lng = A.alloc("lng", [128, D], F32, [r_ln])
        lnb = A.alloc("lnb", [128, D], F32, [r_ln])
        d_ln = S.dmasem("ln")
        NR = 3
        r_x32 = [Res(f"x32_{i}") for i in range(NR)]
        x32 = [A.alloc(f"x32_{i}", [128, D], F32, [r_x32[i]]) for i in range(NR)]
        d_x32 = [S.dmasem(f"x32_{i}") for i in range(NR)]
        r_rr = [Res(f"rr{i}") for i in range(NR)]
        rr_ = [A.alloc(f"rr{i}", [128, D], F32, [r_rr[i]]) for i in range(NR)]
        d_yo = [S.dmasem(f"yo{i}") for i in range(NR)]
        r_stat = [Res(f"stat{i}") for i in range(NR)]
        stats = [A.alloc(f"stats{i}", [128, 4, 6], F32, [r_stat[i]]) for i in range(NR)]
        mv = [A.alloc(f"mv{i}", [128, 2], F32, [r_stat[i]]) for i in range(NR)]
        rstd = [A.alloc(f"rstd{i}", [128, 1], F32, [r_stat[i]]) for i in range(NR)]
        nmr = [A.alloc(f"nmr{i}", [128, 1], F32, [r_stat[i]]) for i in range(NR)]
        S.add(SP, DMA(lng[:], I["ln_g"].partition_broadcast(128)), writes=[r_ln], dmasem=d_ln)
        S.add(SP, DMA(lnb[:], I["ln_b"].partition_broadcast(128)), writes=[r_ln], dmasem=d_ln)
        for tb, (t0, tn) in enumerate(TBLK):
            for j4 in range(4):
                S.add(SP, DMA(mT[:, j4 * 4:j4 * 4 + 4, t0:t0 + tn],
                              m_scrb[j4 * 4:j4 * 4 + 4, :, t0:t0 + tn].rearrange("j p t -> p j t")),
                      reads=[r_mscr[j][tb] for j in range(j4 * 4, j4 * 4 + 4)],
                      writes=[r_mT[i] for i, (c0, cn) in enumerate(TCH) if t0 <= c0 < t0 + tn], dmasem=d_mT[tb])
        hb = [0]

        def load_x32(tc):
            t0, tn = TCH[tc]
            src = I["xp"][t0:t0 + tn, :] if tc < 16 else I["xs"][:, :]
            S.add(SP, DMA(x32[tc % NR][0:tn, :], src), writes=[r_x32[tc % NR]], dmasem=d_x32[tc % NR])
        for tc in range(NR - 1):
            load_x32(tc)
        for tc, (t0, tn) in enumerate(TCH):
            s = tc % NR
            if tc + NR - 1 < 17:
                load_x32(tc + NR - 1)
            for db in range(4):
                bk = hb[0] % 8; hb[0] += 1
                for c in range(16):
                    S.add(PE, MM(banks[bk][0:tn, :], mT[:, c, t0:t0 + tn], wob[:, c, db * 512:(db + 1) * 512],
                                 c == 0, c == 15), reads=[r_mT[tc], r_wob], writes=[bankr[bk]])
                S.add(DVE, STT(rr_[s][0:tn, db * 512:(db + 1) * 512], x32[s][0:tn, db * 512:(db + 1) * 512], DN_ALPHA,
                               banks[bk][0:tn, :], ALU.mult, ALU.add),
                      reads=[bankr[bk], r_x32[s]], writes=[r_rr[s]])
                S.add(DVE, lambda e, s=s, tn=tn, db=db: e.bn_stats(out=stats[s][0:tn, db, :], in_=rr_[s][0:tn, db * 512:(db + 1) * 512]),
                      reads=[r_rr[s]], writes=[r_stat[s]])
            S.add(DVE, lambda e, s=s, tn=tn: e.bn_aggr(out=mv[s][0:tn, :], in_=stats[s][0:tn, :, :].rearrange("p a b -> p (a b)")),
                  reads=[r_stat[s]], writes=[r_stat[s]])
            S.add(ACT, ACTF(rstd[s][0:tn, :], mv[s][0:tn, 1:2], AF.Sqrt, bias=epsb[0:tn, 0:1]),
                  reads=[r_stat[s], r_const], writes=[r_stat[s]])
            S.add(DVE, lambda e, s=s, tn=tn: e.reciprocal(out=rstd[s][0:tn, :], in_=rstd[s][0:tn, :]),
                  reads=[r_stat[s]], writes=[r_stat[s]])
            S.add(DVE, STT(nmr[s][0:tn, :], mv[s][0:tn, 0:1], -1.0, rstd[s][0:tn, :], ALU.mult, ALU.mult),
                  reads=[r_stat[s]], writes=[r_stat[s]])
            S.add(ACT, ACTF(rr_[s][0:tn, :], rr_[s][0:tn, :], AF.Identity, bias=nmr[s][0:tn, 0:1], scale=rstd[s][0:tn, 0:1]),
                  reads=[r_rr[s], r_stat[s]], writes=[r_rr[s]])
            S.add(DVE, TT(rr_[s][0:tn, :], rr_[s][0:tn, :], lng[0:tn, :], ALU.mult), reads=[r_rr[s], r_ln], writes=[r_rr[s]])
            S.add(POOL, TT(rr_[s][0:tn, :], rr_[s][0:tn, :], lnb[0:tn, :], ALU.add), reads=[r_rr[s], r_ln], writes=[r_rr[s]])
            dst = O["y_p"][t0:t0 + tn, :] if tc < 16 else O["y_s"][:, :]
            S.add(SP, DMA(dst, rr_[s][0:tn, :]), reads=[r_rr[s]], dmasem=d_yo[s])

    import contextlib
    es = contextlib.ExitStack()
    with nc.Block() as block:
        S.emit(block, es)
    return nc


def make_in_maps(inputs):
    f = lambda a: np.ascontiguousarray(np.asarray(a, dtype=np.float32))
    identb = np.eye(128, dtype=np.float32).astype(ml_dtypes.bfloat16)
    identf = np.eye(128, dtype=np.float32)
    onesb = np.ones((128, 128), dtype=np.float32).astype(ml_dtypes.bfloat16)
    kk = np.arange(128)[:, None]
    qq = np.arange(128)[None, :]
    maskneg = np.where(qq >= kk, 0.0, NEG).astype(np.float32)
    shared = {
        "w_in": f(inputs["w_in"][0]), "fox_bf": f(inputs["fox_bf"][0]).reshape(H, 1),
        "conv_w": f(inputs["conv_w"][0]), "conv_b": f(inputs["conv_b"][0]).reshape(FW, 1),
        "w_mem_kv": f(inputs["w_mem_kv"][0]), "w_fox_out": f(inputs["w_fox_out"][0]),
        "w_conv_out": f(inputs["w_conv_out"][0]), "w_mem_out": f(inputs["w_mem_out"][0]),
        "w_merge": f(inputs["w_merge"][0]), "b_merge": f(inputs["b_merge"][0]).reshape(3 * D, 1),
        "w_o": f(inputs["w_o"][0]), "ln_g": f(inputs["ln_g"][0]).reshape(1, D),
        "ln_b": f(inputs["ln_b"][0]).reshape(1, D),
        "identb": identb, "identf": identf, "onesb": onesb, "maskneg": maskneg,
        "masknegb": maskneg.astype(ml_dtypes.bfloat16),
    }
    maps = []
    for b in range(8):
        m = dict(shared)
        m["xp"] = f(inputs["x_prompt"][b])
        m["xs"] = f(inputs["x_sample"][b])
        m["mem"] = f(inputs["mem_prompt"][b])
        m["ck"] = f(inputs["cache_fox_k"][0, b]).reshape(PAST, FW)
        m["cv"] = f(inputs["cache_fox_v"][0, b]).reshape(PAST, FW)
        m["clf"] = f(inputs["cache_fox_logf"][0, b])
        m["sconv"] = f(inputs["state_conv"][0, b])
        m["cmk"] = f(inputs["cache_mem_k"][0, b]).reshape(MT, MW)
        m["cmv"] = f(inputs["cache_mem_v"][0, b]).reshape(MT, MW)
        maps.append(m)
    return maps


_NC_CACHE = {}


def kernel(**inputs):
    if "nc" not in _NC_CACHE:
        _NC_CACHE["nc"] = build()
    nc = _NC_CACHE["nc"]
    maps = make_in_maps(inputs)
    res = run_bass_kernel_spmd(nc, maps, core_ids=list(range(8)))
    R = res.results
    st = lambda n: np.stack([np.asarray(R[b][n], dtype=np.float32) for b in range(8)])
    return (
        st("y_p"), st("y_s"),
        st("fk_p").reshape(1, 8, NP, H, DH), st("fv_p").reshape(1, 8, NP, H, DH), st("fl_p").reshape(1, 8, NP, H),
        st("cv_p").reshape(1, 8, 2, FW), st("mk_p").reshape(1, 8, MT, 4, 256), st("mv_p").reshape(1, 8, MT, 4, 256),
        st("fk_s").reshape(1, 8, NS, H, DH), st("fv_s").reshape(1, 8, NS, H, DH), st("fl_s").reshape(1, 8, NS, H),
        st("cv_s").reshape(1, 8, 2, FW),
    )
```

```python
import numpy as np
import ml_dtypes
import concourse.bass as bass
import concourse.mybir as mybir
from concourse.bass_utils import run_bass_kernel_spmd

F32 = mybir.dt.float32
BF16 = mybir.dt.bfloat16
ALU = mybir.AluOpType
AF = mybir.ActivationFunctionType

PE, ACT, DVE, POOL, SP = "pe", "act", "dve", "pool", "sp"
ENGS = [PE, ACT, DVE, POOL, SP]


class Res:
    __slots__ = ("name", "w", "r")

    def __init__(self, name=""):
        self.name = name
        self.w = None
        self.r = {}


class DmaSem:
    __slots__ = ("sem", "count", "name")

    def __init__(self, name):
        self.name = name
        self.sem = None
        self.count = 0


class Op:
    __slots__ = ("eng", "fn", "deps", "dmasem", "marked", "val")

    def __init__(self, eng, fn):
        self.eng = eng
        self.fn = fn
        self.deps = []
        self.dmasem = None
        self.marked = False
        self.val = 0


class Sched:
    def __init__(self, nc, sync_same_engine=True):
        self.nc = nc
        self.ops = {e: [] for e in ENGS}
        self.dmasems = []
        self.sync_same = sync_same_engine

    def dmasem(self, name):
        d = DmaSem(name)
        self.dmasems.append(d)
        return d

    def _dep(self, op, prod):
        if prod is None or prod is op:
            return
        if prod.dmasem is not None:
            if op.dmasem is prod.dmasem:
                return
            op.deps.append((prod, prod.dmasem.count))
            return
        if prod.eng == op.eng and op.dmasem is None:
            if prod.eng == PE or not self.sync_same:
                return
        op.deps.append((prod, None))
        prod.marked = True

    def add(self, eng, fn, reads=(), writes=(), dmasem=None):
        op = Op(eng, fn)
        op.dmasem = dmasem
        for r in reads:
            self._dep(op, r.w)
        for w in writes:
            self._dep(op, w.w)
            for rr in w.r.values():
                self._dep(op, rr)
        if dmasem is not None:
            dmasem.count += 16
        key = dmasem if dmasem is not None else eng
        for r in reads:
            r.r[key] = op
        for w in writes:
            w.w = op
            w.r = {}
        self.ops[eng].append(op)
        return op

    def emit(self, block, es):
        nc = self.nc
        for e in ENGS:
            c = 0
            for op in self.ops[e]:
                if op.dmasem is None and op.marked:
                    c += 1
                    op.val = c
        esem = {e: es.enter_context(nc.semaphore("prog_" + e)) for e in [PE, ACT, DVE, POOL]}
        for d in self.dmasems:
            d.sem = es.enter_context(nc.semaphore("dma_" + d.name))

        def run(e, engobj):
            waited = {}
            for op in self.ops[e]:
                need = {}
                for (p, dv) in op.deps:
                    if p.dmasem is not None:
                        key, v = p.dmasem.sem, dv
                    else:
                        key, v = esem[p.eng], p.val
                    if v > need.get(key, 0):
                        need[key] = v
                for key, v in need.items():
                    if waited.get(key, 0) < v:
                        engobj.wait_ge(key, v)
                        waited[key] = v
                ins = op.fn(engobj)
                if op.dmasem is not None:
                    ins.then_inc(op.dmasem.sem, 16)
                elif op.marked:
                    ins.then_inc(esem[e], 1)
            if e == SP:
                for d in self.dmasems:
                    if d.count > 0:
                        engobj.wait_ge(d.sem, d.count)

        @block.tensor
        def _(eng):
            run(PE, eng)

        @block.scalar
        def _(eng):
            run(ACT, eng)

        @block.vector
        def _(eng):
            run(DVE, eng)

        @block.gpsimd
        def _(eng):
            run(POOL, eng)

        @block.sync
        def _(eng):
            run(SP, eng)


def MM(out, lhsT, rhs, start, stop):
    return lambda e: e.matmul(out, lhsT=lhsT, rhs=rhs, start=start, stop=stop)


def TR(out, in_, ident):
    return lambda e: e.transpose(out=out, in_=in_, identity=ident)


def TRM(out, in_, ident):
    return lambda e: e.matmul(out, lhsT=in_, rhs=ident, start=True, stop=True)


def DMA(out, in_, **kw):
    return lambda e: e.dma_start(out=out, in_=in_, **kw)


def ACTF(out, in_, func, bias=None, scale=None):
    kw = {}
    if bias is not None:
        kw["bias"] = bias
    if scale is not None:
        kw["scale"] = scale
    return lambda e: e.activation(out=out, in_=in_, func=func, **kw)


def COPY(out, in_):
    def f(e):
        if hasattr(e, "activation"):
            return e.activation(out=out, in_=in_, func=AF.Copy)
        return e.tensor_copy(out=out, in_=in_)
    return f


def TT(out, in0, in1, op):
    return lambda e: e.tensor_tensor(out=out, in0=in0, in1=in1, op=op)


def TS(out, in0, s1, op0, s2=None, op1=None):
    if op1 is None:
        return lambda e: e.tensor_scalar(out=out, in0=in0, scalar1=s1, scalar2=None, op0=op0)
    return lambda e: e.tensor_scalar(out=out, in0=in0, scalar1=s1, scalar2=s2, op0=op0, op1=op1)


def STT(out, in0, scalar, in1, op0, op1):
    return lambda e: e.scalar_tensor_tensor(out=out, in0=in0, scalar=scalar, in1=in1, op0=op0, op1=op1)


D = 2048
NP = 2048
NS = 64
NT = NP + NS
PAST = 1024
H = 12
DH = 128
FW = 1536
MW = 1024
MT = 256
INW = 14348
OFF_Q, OFF_K, OFF_V, OFF_F, OFF_G = 0, 1536, 3072, 4608, 4620
OFF_CB, OFF_CC, OFF_CH, OFF_CG = 6156, 7692, 9228, 10764
OFF_MQ, OFF_MG = 12300, 13324
TBLK = [(0, 512), (512, 512), (1024, 512), (1536, 512), (2048, 64)]
TCH = [(i * 128, 128) for i in range(16)] + [(2048, 64)]
SCALE = DH ** -0.5
MSCALE = 256 ** -0.5
DN_ALPHA = 2.0 ** 0.25
LN_EPS = 1e-5
NEG = -1.0e30

OUT_SPECS = [
    ("y_p", [NP, D]), ("y_s", [NS, D]), ("fk_p", [NP, FW]), ("fv_p", [NP, FW]), ("fl_p", [NP, H]),
    ("cv_p", [2, FW]), ("mk_p", [MT, MW]), ("mv_p", [MT, MW]), ("fk_s", [NS, FW]), ("fv_s", [NS, FW]),
    ("fl_s", [NS, H]), ("cv_s", [2, FW]),
]
IN_SPECS = [
    ("xp", [NP, D], F32), ("xs", [NS, D], F32), ("mem", [MT, D], F32), ("ck", [PAST, FW], F32),
    ("cv", [PAST, FW], F32), ("clf", [PAST, H], F32), ("sconv", [2, FW], F32), ("cmk", [MT, MW], F32),
    ("cmv", [MT, MW], F32), ("w_in", [D, INW], F32), ("fox_bf", [H, 1], F32), ("conv_w", [3, FW], F32),
    ("conv_b", [FW, 1], F32), ("w_mem_kv", [D, 2 * MW], F32), ("w_fox_out", [FW, D], F32),
    ("w_conv_out", [FW, D], F32), ("w_mem_out", [MW, D], F32), ("w_merge", [D, 3 * D], F32),
    ("b_merge", [3 * D, 1], F32), ("w_o", [D, D], F32), ("ln_g", [1, D], F32), ("ln_b", [1, D], F32),
    ("identb", [128, 128], BF16), ("identf", [128, 128], F32), ("onesb", [128, 128], BF16),
    ("maskneg", [128, 128], F32), ("masknegb", [128, 128], BF16),
]


class SbAlloc:
    def __init__(self, nc, lo=16512, hi=229376 - 2048):
        self.nc = nc
        self.lo = lo
        self.hi = hi
        self.top = lo
        self.n = 0
        self.live = []

    def mark(self):
        return self.top

    def mark_hi(self):
        return self.hi

    def release_hi(self, m):
        self.hi = m

    def release(self, mark):
        self.top = mark

    def alloc(self, name, shape, dt, res=(), high=False):
        nb = int(np.prod(shape[1:])) * mybir.dt.size(dt)
        nb = (nb + 63) // 64 * 64
        if high:
            assert self.hi - nb >= self.top, f"SBUF overflow allocating {name} (high)"
            self.hi -= nb
            off = self.hi
        else:
            off = self.top
            assert off + nb <= self.hi, f"SBUF overflow allocating {name}: {off + nb} > {self.hi}"
            self.top += nb
        self.n += 1
        t = self.nc.alloc_sbuf_tensor_at(f"{name}_{self.n}", list(shape), dt, offset=off)
        inherited = {}
        k = 0
        for (o, e, rl) in self.live:
            if o < off + nb and off < e:
                for r0 in rl:
                    if r0.w is not None:
                        inherited[("w", k)] = r0.w
                        k += 1
                    for rr in r0.r.values():
                        inherited[("r", k)] = rr
                        k += 1
        for r1 in res:
            r1.r.update(inherited)
        self.live = [(o, e, rl) for (o, e, rl) in self.live if not (o >= off and e <= off + nb)]
        self.live.append((off, off + nb, list(res)))
        return t


def build(debug=None, upto=99):
    nc = bass.Bass("TRN2", target_bir_lowering=False)
    I = {n: nc.dram_tensor(n, s, dt, kind="ExternalInput").ap() for (n, s, dt) in IN_SPECS}
    O = {n: nc.dram_tensor(n, s, F32, kind="ExternalOutput").ap() for (n, s) in OUT_SPECS}
    cum_scr = nc.dram_tensor("cum_scr", [H, NP + PAST + NS], F32).ap()
    m_scr = nc.dram_tensor("m_scr", [16, 128, NT], F32).ap()
    m_scrb = nc.dram_tensor("m_scrb", [16, 128, NT], BF16).ap()
    dbg = None
    if debug is not None:
        dbg = nc.dram_tensor("dbg", list(debug), F32, kind="ExternalOutput").ap()

    nc.alloc_sbuf_tensor("arena", [128, 229376 - 16512 - 64], mybir.dt.uint8)
    A = SbAlloc(nc)
    S = Sched(nc)
    banks = [nc.alloc_psum_tensor(f"bank{i}", [128, 512], F32) for i in range(8)]
    bankr = [Res(f"bank{i}") for i in range(8)]

    r_const = Res("const")
    identb = A.alloc("identb", [128, 128], BF16, [r_const])
    identf = A.alloc("identf", [128, 128], F32, [r_const])
    onesb = A.alloc("onesb", [128, 128], BF16, [r_const])
    maskneg = A.alloc("maskneg", [128, 128], F32, [r_const])
    masknegb = A.alloc("masknegb", [128, 128], BF16, [r_const])
    ones1 = A.alloc("ones1", [128, 1], F32, [r_const])
    nbf = A.alloc("nbf", [H, 1], F32, [r_const])
    dconst = S.dmasem("const")
    for t, n in ((identb, "identb"), (identf, "identf"), (onesb, "onesb"), (maskneg, "maskneg"), (masknegb, "masknegb")):
        S.add(SP, DMA(t[:], I[n]), writes=[r_const], dmasem=dconst)
    S.add(SP, DMA(nbf[:], I["fox_bf"]), writes=[r_const], dmasem=dconst)
    S.add(DVE, lambda e: e.memset(ones1[:], 1.0), writes=[r_const])
    S.add(DVE, TS(nbf[:], nbf[:], -1.0, ALU.mult), reads=[r_const], writes=[r_const])

    bm = A.alloc("bm", [128, 48], F32, [r_const])
    epsb = A.alloc("epsb", [128, 1], F32, [r_const])
    S.add(SP, DMA(bm[:], I["b_merge"].rearrange("(c p) o -> p (c o)", p=128), allow_slow_non_contiguous=True),
          writes=[r_const], dmasem=dconst)
    S.add(DVE, lambda e: e.memset(epsb[:], LN_EPS), writes=[r_const])
    r_ncc = Res("ncc")
    ncc = A.alloc("ncc", [128, 25, H], F32, [r_ncc])
    r_ut = Res("utail")
    utail = A.alloc("utail", [128, 2, 12], F32, [r_ut])
    ustail = A.alloc("ustail", [128, 2, 12], F32, [r_ut])
    mkX = A.mark()
    r_xT = [[Res(f"xT{i}_{g}") for g in range(4)] for i in range(17)]
    xT = A.alloc("xT", [128, 16, NT], BF16, [r for l in r_xT for r in l])
    def xT_res(t0, n):
        return [r for i, (c0, cn) in enumerate(TCH) if c0 < t0 + n and t0 < c0 + cn for r in r_xT[i]]

    mk0 = A.mark()
    r_xb = [Res(f"xb{i}") for i in range(3)]
    xb = [A.alloc(f"xb{i}", [128, D], BF16, [r_xb[i]]) for i in range(3)]
    d_xb = [S.dmasem(f"xb{i}") for i in range(3)]
    for tc, (t0, tn) in enumerate(TCH):
        s = tc % 3
        src = I["xp"][t0:t0 + tn, :] if tc < 16 else I["xs"][:, :]
        S.add(POOL, DMA(xb[s][0:tn, :].rearrange("p (a b) -> p a b", b=1024),
                        src.rearrange("p (a b) -> p a b", b=1024)),
              writes=[r_xb[s]], dmasem=d_xb[s])
        for g in range(4):
            bk = (tc * 4 + g) % 8
            bv = banks[bk]
            for j in range(4):
                c = g * 4 + j
                S.add(PE, TRM(bv[:, j * 128:j * 128 + tn], xb[s][0:tn, c * 128:(c + 1) * 128], identb[0:tn, 0:tn]),
                      reads=[r_xb[s], r_const], writes=[bankr[bk]])
            eng = ACT if (tc * 4 + g) % 2 == 0 else DVE
            S.add(eng, COPY(xT[:, g * 4:g * 4 + 4, t0:t0 + tn],
                            bv[:, 0:512].rearrange("p (a b) -> p a b", b=128)[:, :, 0:tn]),
                  reads=[bankr[bk]], writes=[r_xT[tc][g]])
    A.release(mk0)

    mk1 = A.mark()
    r_pro = Res("pro")
    wfg = A.alloc("wfg", [128, 16, H], BF16, [r_pro])
    r_lf = Res("logfT")
    logfT = A.alloc("logfT", [H, NT], F32, [r_lf])
    r_lc = Res("lcT")
    lcT = A.alloc("lcT", [H, PAST + NS], F32, [r_lc])
    r_cum = Res("cum")
    cumP = A.alloc("cumP", [H, NP], F32, [r_cum])
    cumS = A.alloc("cumS", [H, PAST + NS], F32, [r_cum])
    r_et = [Res("et0"), Res("et1")]
    etmp = [A.alloc("et0", [H, 512], F32, [r_et[0]]), A.alloc("et1", [H, 512], F32, [r_et[1]])]
    r_clf = Res("clf")
    clf_tm = A.alloc("clf_tm", [128, 8, H], F32, [r_clf])
    r_lfcol = Res("lfcol")
    lfcol = A.alloc("lfcol", [128, 17, H], F32, [r_lfcol])
    d_pro = S.dmasem("pro")
    S.add(POOL, DMA(wfg[:], I["w_in"][:, OFF_F:OFF_F + H].rearrange("(c p) n -> p c n", p=128)),
          writes=[r_pro], dmasem=d_pro)
    d_clf = S.dmasem("clf")
    S.add(SP, DMA(clf_tm[:], I["clf"].rearrange("(c p) h -> p c h", p=128)), writes=[r_clf], dmasem=d_clf)
    for tb, (t0, tn) in enumerate(TBLK):
        bk = tb % 2
        for c in range(16):
            S.add(PE, MM(banks[bk][0:H, 0:tn], wfg[:, c, :], xT[:, c, t0:t0 + tn], c == 0, c == 15),
                  reads=[r_pro] + xT_res(t0, tn), writes=[bankr[bk]])
        S.add(ACT, ACTF(etmp[bk][:, 0:tn], banks[bk][0:H, 0:tn], AF.Exp, bias=nbf[:, 0:1], scale=-1.0),
              reads=[bankr[bk], r_const], writes=[r_et[bk]])
        S.add(ACT, ACTF(etmp[bk][:, 0:tn], etmp[bk][:, 0:tn], AF.Ln, bias=1.0),
              reads=[r_et[bk]], writes=[r_et[bk]])
        S.add(DVE, TS(logfT[:, t0:t0 + tn], etmp[bk][:, 0:tn], -1.0, ALU.mult),
              reads=[r_et[bk]], writes=[r_lf])
    for c in range(8):
        bk = 2 + c // 4
        S.add(PE, TR(banks[bk][0:H, (c % 4) * 128:(c % 4 + 1) * 128], clf_tm[:, c, :], identf[:, :]),
              reads=[r_clf, r_const], writes=[bankr[bk]])
        if c % 4 == 3:
            S.add(ACT, COPY(lcT[:, (c // 4) * 512:(c // 4 + 1) * 512], banks[bk][0:H, :]),
                  reads=[bankr[bk]], writes=[r_lc])
    S.add(DVE, COPY(lcT[:, PAST:PAST + NS], logfT[:, NP:NT]), reads=[r_lf], writes=[r_lc])

    def SCAN(out, data1, n):
        return lambda e: e.tensor_tensor_scan(out=out, data0=ones1[0:H, 0:1].broadcast_to([H, n]), data1=data1,
                                               initial=0.0, op0=ALU.mult, op1=ALU.add)
    S.add(DVE, SCAN(cumP[:], logfT[:, 0:NP], NP), reads=[r_lf, r_const], writes=[r_cum])
    S.add(DVE, SCAN(cumS[:], lcT[:], PAST + NS), reads=[r_lc, r_const], writes=[r_cum])
    r_cscr = Res("cum_scr")
    d_cscr = S.dmasem("cscr")
    S.add(SP, DMA(cum_scr[:, 0:NP], cumP[:]), reads=[r_cum], writes=[r_cscr], dmasem=d_cscr)
    S.add(SP, DMA(cum_scr[:, NP:NP + PAST + NS], cumS[:]), reads=[r_cum], writes=[r_cscr], dmasem=d_cscr)
    for tc, (t0, tn) in enumerate(TCH):
        S.add(PE, TR(banks[4][0:tn, tc * H:(tc + 1) * H], logfT[:, t0:t0 + tn], identf[0:H, 0:H]),
              reads=[r_lf, r_const], writes=[bankr[4]])
    S.add(ACT, COPY(lfcol[:, 0:16, :], banks[4][:, 0:16 * H].rearrange("p (a b) -> p a b", b=H)),
          reads=[bankr[4]], writes=[r_lfcol])
    S.add(ACT, COPY(lfcol[0:NS, 16, :], banks[4][0:NS, 16 * H:17 * H]), reads=[bankr[4]], writes=[r_lfcol])
    d_fl = S.dmasem("fl")
    S.add(SP, DMA(O["fl_p"].rearrange("(c p) h -> p c h", p=128), lfcol[:, 0:16, :]), reads=[r_lfcol], dmasem=d_fl)
    S.add(SP, DMA(O["fl_s"], lfcol[0:NS, 16, :]), reads=[r_lfcol], dmasem=d_fl)
    for c in range(16):
        S.add(PE, TR(banks[5][:, c * H:(c + 1) * H], cumP[:, c * 128:(c + 1) * 128], identf[0:H, 0:H]),
              reads=[r_cum, r_const], writes=[bankr[5]])
    for c in range(9):
        n = 128 if c < 8 else NS
        S.add(PE, TR(banks[5][0:n, (16 + c) * H:(17 + c) * H], cumS[:, c * 128:c * 128 + n], identf[0:H, 0:H]),
              reads=[r_cum, r_const], writes=[bankr[5]])
    S.add(DVE, TS(ncc[:, 0:24, :], banks[5][:, 0:24 * H].rearrange("p (a b) -> p a b", b=H), -1.0, ALU.mult),
          reads=[bankr[5]], writes=[r_ncc])
    S.add(DVE, TS(ncc[0:NS, 24, :], banks[5][0:NS, 24 * H:25 * H], -1.0, ALU.mult),
          reads=[bankr[5]], writes=[r_ncc])
    A.release(mk1)

    mk2 = A.mark()
    r_aT = [[Res(f"aT{h}_{tb}") for tb in range(5)] for h in range(H)]
    aT = A.alloc("aT", [128, H, NT], BF16, [r for l in r_aT for r in l])
    mkB = A.mark()
    r_w = [Res("wq"), Res("wkv"), Res("wg")]
    wbuf = A.alloc("wbuf", [128, 16, 512], BF16, r_w)
    d_w = [S.dmasem("wq"), S.dmasem("wkv"), S.dmasem("wg")]
    r_qT = [Res(f"qT{i}") for i in range(5)]
    qT = A.alloc("qT", [128, NT], BF16, r_qT)
    r_kT = [Res(f"kT{i}") for i in range(5)]
    kT = A.alloc("kT", [128, NT], BF16, r_kT)
    r_vb = [Res(f"vb{i}") for i in range(5)]
    vb = A.alloc("vb", [128, 17, 128], BF16, r_vb)
    r_sg = [Res(f"sg{i}") for i in range(5)]
    sg = A.alloc("sg", [128, NT], F32, r_sg)
    r_cqb = Res("cqb")
    cqb = A.alloc("cqb", [128, NT], F32, [r_cqb])
    d_cqb = S.dmasem("cqb")
    r_kv32 = [Res("kv32_0"), Res("kv32_1")]
    kv32 = [A.alloc("kv32_0", [128, 4, 256], F32, [r_kv32[0]]), A.alloc("kv32_1", [128, 4, 256], F32, [r_kv32[1]])]
    d_kvo = [S.dmasem("kvo0"), S.dmasem("kvo1")]
    r_kb = [Res("kb0"), Res("kb1")]
    kb = [A.alloc("kb0", [128, 4, 128], BF16, [r_kb[0]]), A.alloc("kb1", [128, 4, 128], BF16, [r_kb[1]])]
    r_cache = Res("cache")
    kc_tm = A.alloc("kc_tm", [128, 8, 128], BF16, [r_cache])
    vc_tm = A.alloc("vc_tm", [128, 8, 128], BF16, [r_cache])
    d_cache = S.dmasem("cache")
    r_kcT = Res("kcT")
    kcT = A.alloc("kcT", [128, PAST], BF16, [r_kcT])
    NST, NPT = 4, 6
    r_st = [Res(f"st{i}") for i in range(NST)]
    stmp = [A.alloc(f"st{i}", [128, 512], F32, [r_st[i]]) for i in range(NST)]
    r_pt = [Res(f"pt{i}") for i in range(NPT)]
    pt = [A.alloc(f"pt{i}", [128, 512], BF16, [r_pt[i]]) for i in range(NPT)]
    SRING = [2, 3, 0, 1]
    stk = [0]
    r_rl = Res("rl")
    rl = A.alloc("rl", [128, 512], F32, [r_rl])
    r_on = Res("on")
    on = A.alloc("on", [128, 512], F32, [r_on])

    def load_w(h):
        for comp, (off, slot) in enumerate(((OFF_Q, 0), (OFF_K, 1), (OFF_V, 1), (OFF_G, 2))):
            S.add(POOL, DMA(wbuf[:, :, comp * 128:(comp + 1) * 128],
                            I["w_in"][:, off + h * 128:off + (h + 1) * 128].rearrange("(c p) n -> p c n", p=128)),
                  writes=[r_w[slot]], dmasem=d_w[slot])

    def load_cache(h):
        S.add(POOL, DMA(kc_tm[:], I["ck"][:, h * 128:(h + 1) * 128].rearrange("(c p) n -> p c n", p=128)),
              writes=[r_cache], dmasem=d_cache)
        S.add(POOL, DMA(vc_tm[:], I["cv"][:, h * 128:(h + 1) * 128].rearrange("(c p) n -> p c n", p=128)),
              writes=[r_cache], dmasem=d_cache)

    def load_cqb(h):
        S.add(SP, DMA(cqb[:, 0:NP], cum_scr[h:h + 1, 0:NP].partition_broadcast(128)),
              reads=[r_cscr], writes=[r_cqb], dmasem=d_cqb)
        S.add(SP, DMA(cqb[:, NP:NT], cum_scr[h:h + 1, NP + PAST:NP + PAST + NS].partition_broadcast(128)),
              reads=[r_cscr], writes=[r_cqb], dmasem=d_cqb)

    pj = [0]
    sbk = [0]
    pti = [0]
    obk = [0]

    nheads = H if upto >= 2 else 1
    load_w(0)
    load_cache(0)
    load_cqb(0)
    for h in range(nheads):
        pending_kt = []
        for grp in range(5):
            chunks = [(tc, TCH[tc]) for tc in range(grp * 4, min(grp * 4 + 4, 17))]
            s = grp % 2
            for j, (tc, (t0, tn)) in enumerate(chunks):
                bk = pj[0]; pj[0] ^= 1
                for c in range(16):
                    S.add(PE, MM(banks[bk][0:tn, 0:256], xT[:, c, t0:t0 + tn],
                                 wbuf[:, c, 128:384], c == 0, c == 15),
                          reads=[r_w[1]] + r_xT[tc], writes=[bankr[bk]])
                S.add(ACT, COPY(kv32[s][0:tn, j, :], banks[bk][0:tn, 0:256]),
                      reads=[bankr[bk]], writes=[r_kv32[s]])
            nj = len(chunks)
            tn = chunks[0][1][1]
            t0 = chunks[0][1][0]
            S.add(POOL, COPY(kb[s][0:tn, 0:nj, :], kv32[s][0:tn, 0:nj, 0:128]), reads=[r_kv32[s]], writes=[r_kb[s]])
            S.add(POOL, COPY(vb[0:tn, grp * 4:grp * 4 + nj, :], kv32[s][0:tn, 0:nj, 128:256]),
                  reads=[r_kv32[s]], writes=[r_vb[grp]])
            if grp < 4:
                S.add(SP, DMA(O["fk_p"][t0:t0 + 512, h * 128:(h + 1) * 128].rearrange("(c p) n -> p c n", p=128),
                              kv32[s][:, :, 0:128]), reads=[r_kv32[s]], dmasem=d_kvo[s])
                S.add(SP, DMA(O["fv_p"][t0:t0 + 512, h * 128:(h + 1) * 128].rearrange("(c p) n -> p c n", p=128),
                              kv32[s][:, :, 128:256]), reads=[r_kv32[s]], dmasem=d_kvo[s])
            else:
                S.add(SP, DMA(O["fk_s"][:, h * 128:(h + 1) * 128], kv32[s][0:NS, 0, 0:128]),
                      reads=[r_kv32[s]], dmasem=d_kvo[s])
                S.add(SP, DMA(O["fv_s"][:, h * 128:(h + 1) * 128], kv32[s][0:NS, 0, 128:256]),
                      reads=[r_kv32[s]], dmasem=d_kvo[s])
            def ktrans(grp=grp, s=s, nj=nj, tn=tn, t0=t0):
                bk = pj[0]; pj[0] ^= 1
                bv = banks[bk]
                for j in range(nj):
                    S.add(PE, TRM(bv[:, j * 128:j * 128 + tn], kb[s][0:tn, j, :], identb[0:tn, 0:tn]),
                          reads=[r_kb[s], r_const], writes=[bankr[bk]])
                S.add(ACT, COPY(kT[:, t0:t0 + nj * tn], bv[:, 0:nj * 128] if tn == 128 else bv[:, 0:tn]),
                      reads=[bankr[bk]], writes=[r_kT[grp]])
            if pending_kt:
                pending_kt.pop()()
            pending_kt.append(ktrans)
        for comp, dst, rr, slot in ((0, qT, r_qT, 0), (3, sg, r_sg, 2)):
            for tb, (t0, tn) in enumerate(TBLK):
                if tb == 1 and pending_kt:
                    pending_kt.pop()()
                bk = pj[0]; pj[0] ^= 1
                for c in range(16):
                    S.add(PE, MM(banks[bk][:, 0:tn], wbuf[:, c, comp * 128:(comp + 1) * 128], xT[:, c, t0:t0 + tn], c == 0, c == 15),
                          reads=[r_w[slot]] + xT_res(t0, tn), writes=[bankr[bk]])
                if comp == 0:
                    S.add(ACT, COPY(dst[:, t0:t0 + tn], banks[bk][:, 0:tn]), reads=[bankr[bk]], writes=[rr[tb]])
                else:
                    S.add(ACT, ACTF(dst[:, t0:t0 + tn], banks[bk][:, 0:tn], AF.Silu), reads=[bankr[bk]], writes=[rr[tb]])
        if h + 1 < nheads:
            load_w(h + 1)
        for half in range(2):
            bk = pj[0]; pj[0] ^= 1
            bv = banks[bk]
            for j in range(4):
                S.add(PE, TRM(bv[:, j * 128:(j + 1) * 128], kc_tm[:, half * 4 + j, :], identb[:, :]),
                      reads=[r_cache, r_const], writes=[bankr[bk]])
            S.add(ACT, COPY(kcT[:, half * 512:(half + 1) * 512], bv[:, 0:512]), reads=[bankr[bk]], writes=[r_kcT])

        tiles = []

        def attend(qcol0, qn, keys, tbidx):
            ob = obk[0]; obk[0] ^= 1
            nk = len(keys)
            for i, kk in enumerate(keys):
                tiles.append((qcol0, qn, tbidx, 4 + ob, 6 + ob, i == 0, i == nk - 1) + kk)

        for qb in range(4):
            keys = []
            for kc in range(4 * qb + 4):
                diag = kc >= 4 * qb
                n0 = (kc - 4 * qb) * 128 if diag else 0
                keys.append((kT[:, kc * 128:(kc + 1) * 128], 128, vb[:, kc, :], ncc[:, kc, h:h + 1], n0, diag,
                             [r_kT[kc // 4], r_vb[kc // 4]]))
            attend(qb * 512, 512, keys, qb)
        keys = []
        for kc in range(8):
            keys.append((kcT[:, kc * 128:(kc + 1) * 128], 128, vc_tm[:, kc, :], ncc[:, 16 + kc, h:h + 1], 0, False,
                         [r_kcT, r_cache]))
        keys.append((kT[:, NP:NT], NS, vb[0:NS, 16, :], ncc[0:NS, 24, h:h + 1], 0, True, [r_kT[4], r_vb[4]]))
        attend(NP, NS, keys, 4)

        LA = 3
        slots = {}

        def issue_S(i):
            (qcol0, qn, tbidx, Ob, Lb, first, last, kl, kn, vv, bias, n0, diag, rds) = tiles[i]
            sb_ = SRING[sbk[0] % 4]; sbk[0] += 1
            st = stk[0] % NST; stk[0] += 1
            p = pti[0] % NPT; pti[0] += 1
            slots[i] = p
            S.add(PE, MM(banks[sb_][0:kn, n0:qn], kl, qT[:, qcol0 + n0:qcol0 + qn], True, not diag),
                  reads=rds + [r_qT[tbidx]], writes=[bankr[sb_]])
            if diag:
                S.add(PE, MM(banks[sb_][0:kn, n0:n0 + kn], identb[0:kn, 0:kn], masknegb[0:kn, 0:kn], False, True),
                      reads=[r_const], writes=[bankr[sb_]])
            S.add(DVE, STT(stmp[st][0:kn, n0:qn], banks[sb_][0:kn, n0:qn], SCALE,
                           cqb[0:kn, qcol0 + n0:qcol0 + qn], ALU.mult, ALU.add),
                  reads=[bankr[sb_], r_cqb], writes=[r_st[st]])
            S.add(ACT, ACTF(pt[p][0:kn, n0:qn], stmp[st][0:kn, n0:qn], AF.Exp, bias=bias),
                  reads=[r_st[st], r_ncc], writes=[r_pt[p]])

        def issue_PV(i):
            (qcol0, qn, tbidx, Ob, Lb, first, last, kl, kn, vv, bias, n0, diag, rds) = tiles[i]
            p = slots.pop(i)
            S.add(PE, MM(banks[Ob][:, n0:qn], vv, pt[p][0:kn, n0:qn], first, last),
                  reads=[r_pt[p]] + rds, writes=[bankr[Ob]])
            S.add(PE, MM(banks[Lb][:, n0:qn], onesb[0:kn, :], pt[p][0:kn, n0:qn], first, last),
                  reads=[r_pt[p], r_const], writes=[bankr[Lb]])
            if last:
                S.add(ACT, ACTF(rl[:, 0:qn], banks[Lb][:, 0:qn], AF.Ln), reads=[bankr[Lb]], writes=[r_rl])
                S.add(ACT, ACTF(rl[:, 0:qn], rl[:, 0:qn], AF.Exp, scale=-1.0), reads=[r_rl], writes=[r_rl])
                S.add(DVE, TT(on[:, 0:qn], banks[Ob][:, 0:qn], rl[:, 0:qn], ALU.mult), reads=[bankr[Ob], r_rl], writes=[r_on])
                S.add(POOL, TT(aT[:, h, qcol0:qcol0 + qn], on[:, 0:qn], sg[:, qcol0:qcol0 + qn], ALU.mult),
                      reads=[r_on, r_sg[tbidx]], writes=[r_aT[h][tbidx]])

        nt_ = len(tiles)
        for i in range(min(LA, nt_)):
            issue_S(i)
        for i in range(nt_):
            if i + LA < nt_:
                issue_S(i + LA)
            issue_PV(i)
        if h + 1 < nheads:
            load_cache(h + 1)
            load_cqb(h + 1)

    if debug is not None and upto <= 2:
        r_d = Res("dbgt")
        dt_ = A.alloc("dbgt", [128, NT], F32, [r_d])
        d_dbg = S.dmasem("dbg")
        for h in range(nheads):
            S.add(DVE, COPY(dt_[:], aT[:, h, :]), reads=r_aT[h], writes=[r_d])
            S.add(SP, DMA(dbg[:, h, :], dt_[:]), reads=[r_d], dmasem=d_dbg)

    A.release(mkB)
    r_mscr = [[Res(f"mscr{j}_{tb}") for tb in range(5)] for j in range(16)]

    def combine(aT_, r_a, nchunks, w_out, gate, first, hooks=None, last=False):
        mkc = A.mark()
        r_wo = [Res("wo0"), Res("wo1")]
        wo = [A.alloc(f"wo{i}", [128, nchunks, 256], BF16, [r_wo[i]]) for i in range(2)]
        wm = [A.alloc(f"wm{i}", [128, 16, 256], BF16, [r_wo[i]]) for i in range(2)]
        d_wo = [S.dmasem(f"wo{gate}_0"), S.dmasem(f"wo{gate}_1")]
        r_gt = [Res("gt0"), Res("gt1")]
        gt = [A.alloc(f"gt{i}", [128, 512], F32, [r_gt[i]]) for i in range(2)]
        r_mt = [Res("mt0"), Res("mt1")]
        mt = [A.alloc(f"mt{i}", [128, 512], F32, [r_mt[i]]) for i in range(2)]
        d_mo = [S.dmasem(f"mo{gate}_0"), S.dmasem(f"mo{gate}_1")]
        r_pv = [Res("pv0"), Res("pv1")]
        pv = [A.alloc(f"pv{i}", [128, 512], F32, [r_pv[i]]) for i in range(2)]
        if last:
            r_mtb = [Res("mtb0"), Res("mtb1")]
            mtb = [A.alloc(f"mtb{i}", [128, 512], BF16, [r_mtb[i]]) for i in range(2)]
        d_pv = [S.dmasem(f"pv{gate}_0"), S.dmasem(f"pv{gate}_1")]

        def loadw(jg):
            s = jg % 2
            S.add(POOL, DMA(wo[s][:], w_out[:, jg * 256:(jg + 1) * 256].rearrange("(c p) n -> p c n", p=128)),
                  writes=[r_wo[s]], dmasem=d_wo[s])
            S.add(POOL, DMA(wm[s][:], I["w_merge"][:, gate * D + jg * 256:gate * D + (jg + 1) * 256]
                            .rearrange("(c p) n -> p c n", p=128)),
                  writes=[r_wo[s]], dmasem=d_wo[s])
        loadw(0)
        it = 0
        for jg in range(8):
            if jg + 1 < 8:
                loadw(jg + 1)
            if hooks and jg in hooks:
                hooks[jg]()
            s = jg % 2
            for jj in range(2):
                j = jg * 2 + jj
                for tb, (t0, tn) in enumerate(TBLK):
                    p2 = it % 2
                    it += 1
                    yb, gb = 2 * p2, 2 * p2 + 1
                    if not first:
                        S.add(SP, DMA(pv[p2][:, 0:tn], m_scr[j, :, t0:t0 + tn]), reads=[r_mscr[j][tb]],
                              writes=[r_pv[p2]], dmasem=d_pv[p2])
                    for c in range(16):
                        S.add(PE, MM(banks[gb][:, 0:tn], wm[s][:, c, jj * 128:(jj + 1) * 128], xT[:, c, t0:t0 + tn],
                                     c == 0, c == 15), reads=[r_wo[s]] + xT_res(t0, tn), writes=[bankr[gb]])
                    for c in range(nchunks):
                        S.add(PE, MM(banks[yb][:, 0:tn], wo[s][:, c, jj * 128:(jj + 1) * 128], aT_[:, c, t0:t0 + tn],
                                     c == 0, c == nchunks - 1), reads=[r_wo[s], r_a(c, tb)], writes=[bankr[yb]])
                    S.add(ACT, ACTF(gt[p2][:, 0:tn], banks[gb][:, 0:tn], AF.Sigmoid, bias=bm[:, gate * 16 + j:gate * 16 + j + 1]),
                          reads=[bankr[gb], r_const], writes=[r_gt[p2]])
                    S.add(DVE, TT(mt[p2][:, 0:tn], banks[yb][:, 0:tn], gt[p2][:, 0:tn], ALU.mult),
                          reads=[bankr[yb], r_gt[p2]], writes=[r_mt[p2]])
                    if last:
                        S.add(POOL, TT(mtb[p2][:, 0:tn], mt[p2][:, 0:tn], pv[p2][:, 0:tn], ALU.add),
                              reads=[r_mt[p2], r_pv[p2]], writes=[r_mtb[p2]])
                        S.add(SP, DMA(m_scrb[j, :, t0:t0 + tn], mtb[p2][:, 0:tn]), reads=[r_mtb[p2]],
                              writes=[r_mscr[j][tb]], dmasem=d_mo[p2])
                        continue
                    if not first:
                        S.add(POOL, TT(mt[p2][:, 0:tn], mt[p2][:, 0:tn], pv[p2][:, 0:tn], ALU.add),
                              reads=[r_mt[p2], r_pv[p2]], writes=[r_mt[p2]])
                    S.add(SP, DMA(m_scr[j, :, t0:t0 + tn], mt[p2][:, 0:tn]), reads=[r_mt[p2]],
                          writes=[r_mscr[j][tb]], dmasem=d_mo[p2])
        A.release(mkc)

    chi = A.mark_hi()
    r_cw = [Res("cw0"), Res("cw1")]
    cwb = [A.alloc(f"cwb{i}", [128, 16, 512], BF16, [r_cw[i]], high=True) for i in range(2)]
    d_cw = [S.dmasem("cw0"), S.dmasem("cw1")]

    def load_cw(c):
        s = c % 2
        for comp, off in enumerate((OFF_CB, OFF_CC, OFF_CH, OFF_CG)):
            S.add(POOL, DMA(cwb[s][:, :, comp * 128:(comp + 1) * 128],
                            I["w_in"][:, off + c * 128:off + (c + 1) * 128].rearrange("(c p) n -> p c n", p=128)),
                  writes=[r_cw[s]], dmasem=d_cw[s])
    if upto >= 3:
        combine(aT, lambda c, tb: r_aT[c][tb], H, I["w_fox_out"], 0, True,
                hooks={7: (lambda: load_cw(0))} if upto >= 4 else None)
    A.release(mk2)

    if upto >= 4:
        mk3 = A.mark()
        r_aC = [[Res(f"aC{c}_{tb}") for tb in range(5)] for c in range(12)]
        aC = A.alloc("aC", [128, 12, NT], BF16, [r for l in r_aC for r in l])
        mk3b = A.mark()
        r_cs = Res("convsmall")
        cwT = A.alloc("cwT", [128, 3, 12], F32, [r_cs])
        cbT = A.alloc("cbT", [128, 12], F32, [r_cs])
        sconvT = A.alloc("sconvT", [128, 2, 12], F32, [r_cs])
        d_cs = S.dmasem("convsmall")
        for j in range(3):
            S.add(SP, DMA(cwT[:, j, :], I["conv_w"][j:j + 1, :].rearrange("o (c p) -> p (o c)", p=128),
                          allow_slow_non_contiguous=True), writes=[r_cs], dmasem=d_cs)
        S.add(SP, DMA(cbT[:], I["conv_b"].rearrange("(c p) o -> p (c o)", p=128), allow_slow_non_contiguous=True),
              writes=[r_cs], dmasem=d_cs)
        for j in range(2):
            S.add(SP, DMA(sconvT[:, j, :], I["sconv"][j:j + 1, :].rearrange("o (c p) -> p (o c)", p=128),
                          allow_slow_non_contiguous=True), writes=[r_cs], dmasem=d_cs)
        r_cg = [Res(f"cg{i}") for i in range(5)]
        cg = A.alloc("cg", [128, NT], F32, r_cg)
        r_u = [Res(f"u{i}") for i in range(5)]
        r_u0 = Res("u0")
        up = A.alloc("up", [128, 2 + NP], F32, r_u[0:4] + [r_u0])
        us = A.alloc("us", [128, 2 + NS], F32, [r_u[4]])
        r_bg = [Res(f"bg{i}") for i in range(5)]
        bg = A.alloc("bg", [128, NT], F32, r_bg)
        r_sc = [Res(f"sc{i}") for i in range(5)]
        scg = A.alloc("scg", [128, NT], F32, r_sc)
        r_tb = Res("tbuf")
        tbuf = A.alloc("tbuf", [128, NT], F32, [r_tb])
        S.add(POOL, lambda e: e.memset(up[:, 0:2], 0.0), writes=[r_u0])

        for c in range(12):
            s = c % 2
            for tb, (t0, tn) in enumerate(TBLK):
                if tb == 1 and c + 1 < 12:
                    load_cw(c + 1)
                def proj(comp):
                    bk = pj[0]; pj[0] ^= 1
                    for kc in range(16):
                        S.add(PE, MM(banks[bk][:, 0:tn], cwb[s][:, kc, comp * 128:(comp + 1) * 128], xT[:, kc, t0:t0 + tn],
                                     kc == 0, kc == 15), reads=[r_cw[s]] + xT_res(t0, tn), writes=[bankr[bk]])
                    return bk
                bk = proj(1)
                S.add(ACT, COPY(cg[:, t0:t0 + tn], banks[bk][:, 0:tn]), reads=[bankr[bk]], writes=[r_cg[tb]])
                bk = proj(2)
                udst = up[:, 2 + t0:2 + t0 + tn] if tb < 4 else us[:, 2:2 + NS]
                S.add(DVE, TT(udst, banks[bk][:, 0:tn], cg[:, t0:t0 + tn], ALU.mult),
                      reads=[bankr[bk], r_cg[tb]], writes=[r_u[tb]])
                bk = proj(0)
                S.add(ACT, COPY(bg[:, t0:t0 + tn], banks[bk][:, 0:tn]), reads=[bankr[bk]], writes=[r_bg[tb]])
                bk = proj(3)
                S.add(ACT, ACTF(scg[:, t0:t0 + tn], banks[bk][:, 0:tn], AF.Silu), reads=[bankr[bk]], writes=[r_sc[tb]])
            S.add(POOL, COPY(us[:, 0:2], sconvT[:, :, c]), reads=[r_cs], writes=[r_u[4]])
            for (ub, n, o0, rr) in ((up, NP, 0, r_u[0:4] + [r_u0]), (us, NS, NP, [r_u[4]])):
                S.add(DVE, TS(tbuf[:, o0:o0 + n], ub[:, 2:2 + n], cwT[:, 2, c:c + 1], ALU.mult, cbT[:, c:c + 1], ALU.add),
                      reads=rr + [r_cs], writes=[r_tb])
                S.add(DVE, STT(tbuf[:, o0:o0 + n], ub[:, 1:1 + n], cwT[:, 1, c:c + 1], tbuf[:, o0:o0 + n], ALU.mult, ALU.add),
                      reads=rr + [r_cs, r_tb], writes=[r_tb])
                S.add(DVE, STT(tbuf[:, o0:o0 + n], ub[:, 0:n], cwT[:, 0, c:c + 1], tbuf[:, o0:o0 + n], ALU.mult, ALU.add),
                      reads=rr + [r_cs, r_tb], writes=[r_tb])
            S.add(POOL, COPY(utail[:, :, c], up[:, NP:NP + 2]), reads=[r_u[3]], writes=[r_ut])
            S.add(POOL, COPY(ustail[:, :, c], us[:, NS:NS + 2]), reads=[r_u[4]], writes=[r_ut])
            S.add(POOL, TT(tbuf[:, :], tbuf[:, :], bg[:, :], ALU.mult), reads=[r_tb] + r_bg, writes=[r_tb])
            S.add(POOL, TT(aC[:, c, :], tbuf[:, :], scg[:, :], ALU.mult), reads=[r_tb] + r_sc, writes=r_aC[c])
        d_ut = S.dmasem("utail")
        for j in range(2):
            S.add(SP, DMA(O["cv_p"][j:j + 1, :].rearrange("o (c p) -> p (o c)", p=128), utail[:, j, :],
                          allow_slow_non_contiguous=True), reads=[r_ut], dmasem=d_ut)
            S.add(SP, DMA(O["cv_s"][j:j + 1, :].rearrange("o (c p) -> p (o c)", p=128), ustail[:, j, :],
                          allow_slow_non_contiguous=True), reads=[r_ut], dmasem=d_ut)
        A.release(mk3b)
        A.release_hi(chi)
        mhi = A.mark_hi()
        r_mw = [Res("mw0"), Res("mw1")]
        mwb = [A.alloc(f"mwb{i}", [128, 16, 512], BF16, [r_mw[i]], high=True) for i in range(2)]
        d_mw = [S.dmasem("mw0"), S.dmasem("mw1")]
        r_memb = Res("memb")
        memb = A.alloc("memb", [128, 2, D], BF16, [r_memb], high=True)
        d_memb = S.dmasem("memb")

        def load_mkv(nb):
            s = nb % 2
            S.add(POOL, DMA(mwb[s][:], I["w_mem_kv"][:, nb * 512:(nb + 1) * 512].rearrange("(c p) n -> p c n", p=128)),
                  writes=[r_mw[s]], dmasem=d_mw[s])

        def early_mem_a():
            S.add(POOL, DMA(memb[:].rearrange("p c (a b) -> p c a b", b=1024),
                            I["mem"].rearrange("(c p) (a b) -> p c a b", p=128, b=1024)),
                  writes=[r_memb], dmasem=d_memb)
            load_mkv(0)
        if upto >= 5:
            combine(aC, lambda c, tb: r_aC[c][tb], 12, I["w_conv_out"], 1, False,
                    hooks={5: early_mem_a, 6: lambda: load_mkv(1)})
        A.release(mk3)

    if upto >= 6:
        mk4 = A.mark()
        r_aM = [[Res(f"aM{c}_{tb}") for tb in range(5)] for c in range(8)]
        aM = A.alloc("aM", [128, 8, NT], BF16, [r for l in r_aM for r in l])
        mk4b = A.mark()
        r_mkb = [Res("mkb_p"), Res("mkb_s")]
        mkb = [A.alloc("mkb_p", [128, 2, 2 * MW], BF16, [r_mkb[0]]), A.alloc("mkb_s", [128, 2, 2 * MW], BF16, [r_mkb[1]])]
        d_cm = S.dmasem("cmem")
        r_mkT = [Res("mkT_p"), Res("mkT_s")]
        mkT = [A.alloc("mkT_p", [128, 8, MT], BF16, [r_mkT[0]]), A.alloc("mkT_s", [128, 8, MT], BF16, [r_mkT[1]])]
        mk4c = A.mark()
        r_memT = Res("memT")
        memT = A.alloc("memT", [128, 16, MT], BF16, [r_memT])
        r_m32 = [Res("m32_0"), Res("m32_1")]
        m32 = [A.alloc(f"m32_{i}", [128, 512], F32, [r_m32[i]]) for i in range(2)]
        d_m32 = [S.dmasem("m32_0"), S.dmasem("m32_1")]

        S.add(POOL, DMA(mkb[1][:, :, 0:MW], I["cmk"].rearrange("(c p) n -> p c n", p=128)), writes=[r_mkb[1]], dmasem=d_cm)
        S.add(POOL, DMA(mkb[1][:, :, MW:2 * MW], I["cmv"].rearrange("(c p) n -> p c n", p=128)), writes=[r_mkb[1]], dmasem=d_cm)
        for mc in range(2):
            for g in range(4):
                bk = pj[0]; pj[0] ^= 1
                bv = banks[bk]
                for j in range(4):
                    c = g * 4 + j
                    S.add(PE, TRM(bv[:, j * 128:(j + 1) * 128], memb[:, mc, c * 128:(c + 1) * 128], identb[:, :]),
                          reads=[r_memb, r_const], writes=[bankr[bk]])
                S.add(ACT, COPY(memT[:, g * 4:g * 4 + 4, mc * 128:(mc + 1) * 128],
                                bv[:, 0:512].rearrange("p (a b) -> p a b", b=128)), reads=[bankr[bk]], writes=[r_memT])

        def load_mq(h, s):
            S.add(POOL, DMA(mwb[s][:, :, 0:256], I["w_in"][:, OFF_MQ + h * 256:OFF_MQ + (h + 1) * 256]
                            .rearrange("(c p) n -> p c n", p=128)), writes=[r_mw[s]], dmasem=d_mw[s])
            S.add(POOL, DMA(mwb[s][:, :, 256:512], I["w_in"][:, OFF_MG + h * 256:OFF_MG + (h + 1) * 256]
                            .rearrange("(c p) n -> p c n", p=128)), writes=[r_mw[s]], dmasem=d_mw[s])
        it = 0
        for nb in range(4):
            if 1 <= nb + 1 < 4 and nb >= 1:
                load_mkv(nb + 1)
            elif nb == 3:
                load_mq(0, 0)
            s = nb % 2
            for mc in range(2):
                bk = pj[0]; pj[0] ^= 1
                p2 = it % 2; it += 1
                for c in range(16):
                    S.add(PE, MM(banks[bk][:, :], memT[:, c, mc * 128:(mc + 1) * 128], mwb[s][:, c, :], c == 0, c == 15),
                          reads=[r_memT, r_mw[s]], writes=[bankr[bk]])
                S.add(ACT, COPY(m32[p2][:], banks[bk][:, :]), reads=[bankr[bk]], writes=[r_m32[p2]])
                S.add(POOL, COPY(mkb[0][:, mc, nb * 512:(nb + 1) * 512], m32[p2][:]), reads=[r_m32[p2]], writes=[r_mkb[0]])
                dst = O["mk_p"] if nb < 2 else O["mv_p"]
                S.add(SP, DMA(dst[mc * 128:(mc + 1) * 128, (nb % 2) * 512:(nb % 2 + 1) * 512], m32[p2][:]),
                      reads=[r_m32[p2]], dmasem=d_m32[p2])
        load_mq(1, 1)
        for src in range(2):
            for hd in range(8):
                bk = pj[0]; pj[0] ^= 1
                bv = banks[bk]
                for mc in range(2):
                    S.add(PE, TRM(bv[:, mc * 128:(mc + 1) * 128], mkb[src][:, mc, hd * 128:(hd + 1) * 128], identb[:, :]),
                          reads=[r_mkb[src], r_const], writes=[bankr[bk]])
                S.add(ACT, COPY(mkT[src][:, hd, :], bv[:, 0:256]), reads=[bankr[bk]], writes=[r_mkT[src]])

        A.release(mk4c)
        r_mq = [Res(f"mq{i}") for i in range(5)]
        mqT = A.alloc("mqT", [128, 2, NT], BF16, r_mq)
        r_sgm = [Res(f"sgm{i}") for i in range(5)]
        sgm = A.alloc("sgm", [128, 2, NT], F32, r_sgm)
        r_mpt = [Res("mpt0"), Res("mpt1"), Res("mpt2"), Res("mpt3")]
        mpt = [A.alloc(f"mpt{i}", [128, 512], BF16, [r_mpt[i]]) for i in range(4)]
        r_mrl = Res("mrl")
        mrl = A.alloc("mrl", [128, 512], F32, [r_mrl])
        r_mon = [Res("mon0"), Res("mon1")]
        mon = [A.alloc(f"mon{i}", [128, 512], F32, [r_mon[i]]) for i in range(2)]

        def mproj(h, tb):
            s = h % 2
            t0, tn = TBLK[tb]
            for comp in range(4):
                bk = pj[0]; pj[0] ^= 1
                for c in range(16):
                    S.add(PE, MM(banks[bk][:, 0:tn], mwb[s][:, c, comp * 128:(comp + 1) * 128], xT[:, c, t0:t0 + tn],
                                 c == 0, c == 15), reads=[r_mw[s]] + xT_res(t0, tn), writes=[bankr[bk]])
                if comp < 2:
                    S.add(ACT, COPY(mqT[:, comp, t0:t0 + tn], banks[bk][:, 0:tn]), reads=[bankr[bk]], writes=[r_mq[tb]])
                else:
                    S.add(ACT, ACTF(sgm[:, comp - 2, t0:t0 + tn], banks[bk][:, 0:tn], AF.Silu),
                          reads=[bankr[bk]], writes=[r_sgm[tb]])

        units = [(h, tb) for h in range(4) for tb in range(5)]
        mproj(0, 0)
        for ui, (h, tb) in enumerate(units):
            t0, tn = TBLK[tb]
            src = 0 if tb < 4 else 1
            pts = []
            for mc in range(2):
                sb_ = 2 + mc
                p = pti[0] % 4; pti[0] += 1
                pts.append(p)
                for dc in range(2):
                    S.add(PE, MM(banks[sb_][:, 0:tn], mkT[src][:, h * 2 + dc, mc * 128:(mc + 1) * 128],
                                 mqT[:, dc, t0:t0 + tn], dc == 0, dc == 1),
                          reads=[r_mkT[src], r_mq[tb]], writes=[bankr[sb_]])
                S.add(ACT, ACTF(mpt[p][:, 0:tn], banks[sb_][:, 0:tn], AF.Exp, scale=MSCALE),
                      reads=[bankr[sb_]], writes=[r_mpt[p]])
            if ui + 1 < len(units):
                nh, ntb = units[ui + 1]
                mproj(nh, ntb)
                if ntb == 4 and nh + 2 < 4:
                    load_mq(nh + 2, nh % 2)
            for mc in range(2):
                S.add(PE, MM(banks[6][:, 0:tn], onesb[:, :], mpt[pts[mc]][:, 0:tn], mc == 0, mc == 1),
                      reads=[r_mpt[pts[mc]], r_const], writes=[bankr[6]])
            for dc in range(2):
                for mc in range(2):
                    S.add(PE, MM(banks[4 + dc][:, 0:tn],
                                 mkb[src][:, mc, MW + h * 256 + dc * 128:MW + h * 256 + (dc + 1) * 128],
                                 mpt[pts[mc]][:, 0:tn], mc == 0, mc == 1),
                          reads=[r_mpt[pts[mc]], r_mkb[src]], writes=[bankr[4 + dc]])
            S.add(DVE, lambda e, tn=tn: e.reciprocal(out=mrl[:, 0:tn], in_=banks[6][:, 0:tn]),
                  reads=[bankr[6]], writes=[r_mrl])
            for dc in range(2):
                S.add(DVE, TT(mon[dc][:, 0:tn], banks[4 + dc][:, 0:tn], mrl[:, 0:tn], ALU.mult),
                      reads=[bankr[4 + dc], r_mrl], writes=[r_mon[dc]])
                S.add(POOL, TT(aM[:, h * 2 + dc, t0:t0 + tn], mon[dc][:, 0:tn], sgm[:, dc, t0:t0 + tn], ALU.mult),
                      reads=[r_mon[dc], r_sgm[tb]], writes=[r_aM[h * 2 + dc][tb]])
        A.release(mk4b)
        A.release_hi(mhi)
        r_wob = Res("wob")
        wob = A.alloc("wob", [128, 16, D], BF16, [r_wob], high=True)
        d_wob = S.dmasem("wob")
        def wob_load(q4):
            return lambda: S.add(POOL, DMA(wob[:, :, q4 * 512:(q4 + 1) * 512],
                                           I["w_o"][:, q4 * 512:(q4 + 1) * 512].rearrange("(c p) n -> p c n", p=128)),
                                 writes=[r_wob], dmasem=d_wob)
        if upto >= 7:
            combine(aM, lambda c, tb: r_aM[c][tb], 8, I["w_mem_out"], 2, False,
                    hooks={1: wob_load(0), 3: wob_load(1), 5: wob_load(2), 7: wob_load(3)}, last=True)
        A.release(mk4)

    if upto >= 8:
        A.release(mkX)
        r_mT = [Res(f"mT{i}") for i in range(17)]
        mT = A.alloc("mT", [128, 16, NT], BF16, r_mT)
        d_mT = [S.dmasem(f"mT{i}") for i in range(5)]
        r_ln = Res("ln")
        lng = A.alloc("lng", [128, D], F32, [r_ln])
        lnb = A.alloc("lnb", [128, D], F32, [r_ln])
        d_ln = S.dmasem("ln")
        NR = 3
        r_x32 = [Res(f"x32_{i}") for i in range(NR)]
        x32 = [A.alloc(f"x32_{i}", [128, D], F32, [r_x32[i]]) for i in range(NR)]
        d_x32 = [S.dmasem(f"x32_{i}") for i in range(NR)]
        r_rr = [Res(f"rr{i}") for i in range(NR)]
        rr_ = [A.alloc(f"rr{i}", [128, D], F32, [r_rr[i]]) for i in range(NR)]
        d_yo = [S.dmasem(f"yo{i}") for i in range(NR)]
        r_stat = [Res(f"stat{i}") for i in range(NR)]
        stats = [A.alloc(f"stats{i}", [128, 4, 6], F32, [r_stat[i]]) for i in range(NR)]
        mv = [A.alloc(f"mv{i}", [128, 2], F32, [r_stat[i]]) for i in range(NR)]
        rstd = [A.alloc(f"rstd{i}", [128, 1], F32, [r_stat[i]]) for i in range(NR)]
        nmr = [A.alloc(f"nmr{i}", [128, 1], F32, [r_stat[i]]) for i in range(NR)]
        S.add(SP, DMA(lng[:], I["ln_g"].partition_broadcast(128)), writes=[r_ln], dmasem=d_ln)
        S.add(SP, DMA(lnb[:], I["ln_b"].partition_broadcast(128)), writes=[r_ln], dmasem=d_ln)
        for tb, (t0, tn) in enumerate(TBLK):
            for j4 in range(4):
                S.add(SP, DMA(mT[:, j4 * 4:j4 * 4 + 4, t0:t0 + tn],
                              m_scrb[j4 * 4:j4 * 4 + 4, :, t0:t0 + tn].rearrange("j p t -> p j t")),
                      reads=[r_mscr[j][tb] for j in range(j4 * 4, j4 * 4 + 4)],
                      writes=[r_mT[i] for i, (c0, cn) in enumerate(TCH) if t0 <= c0 < t0 + tn], dmasem=d_mT[tb])
        hb = [0]

        def load_x32(tc):
            t0, tn = TCH[tc]
            src = I["xp"][t0:t0 + tn, :] if tc < 16 else I["xs"][:, :]
            S.add(SP, DMA(x32[tc % NR][0:tn, :], src), writes=[r_x32[tc % NR]], dmasem=d_x32[tc % NR])
        for tc in range(NR - 1):
            load_x32(tc)
        for tc, (t0, tn) in enumerate(TCH):
            s = tc % NR
            if tc + NR - 1 < 17:
                load_x32(tc + NR - 1)
            for db in range(4):
                bk = hb[0] % 8; hb[0] += 1
                for c in range(16):
                    S.add(PE, MM(banks[bk][0:tn, :], mT[:, c, t0:t0 + tn], wob[:, c, db * 512:(db + 1) * 512],
                                 c == 0, c == 15), reads=[r_mT[tc], r_wob], writes=[bankr[bk]])
                S.add(DVE, STT(rr_[s][0:tn, db * 512:(db + 1) * 512], x32[s][0:tn, db * 512:(db + 1) * 512], DN_ALPHA,
                               banks[bk][0:tn, :], ALU.mult, ALU.add),
                      reads=[bankr[bk], r_x32[s]], writes=[r_rr[s]])
                S.add(DVE, lambda e, s=s, tn=tn, db=db: e.bn_stats(out=stats[s][0:tn, db, :], in_=rr_[s][0:tn, db * 512:(db + 1) * 512]),
                      reads=[r_rr[s]], writes=[r_stat[s]])
            S.add(DVE, lambda e, s=s, tn=tn: e.bn_aggr(out=mv[s][0:tn, :], in_=stats[s][0:tn, :, :].rearrange("p a b -> p (a b)")),
                  reads=[r_stat[s]], writes=[r_stat[s]])
            S.add(ACT, ACTF(rstd[s][0:tn, :], mv[s][0:tn, 1:2], AF.Sqrt, bias=epsb[0:tn, 0:1]),
                  reads=[r_stat[s], r_const], writes=[r_stat[s]])
            S.add(DVE, lambda e, s=s, tn=tn: e.reciprocal(out=rstd[s][0:tn, :], in_=rstd[s][0:tn, :]),
                  reads=[r_stat[s]], writes=[r_stat[s]])
            S.add(DVE, STT(nmr[s][0:tn, :], mv[s][0:tn, 0:1], -1.0, rstd[s][0:tn, :], ALU.mult, ALU.mult),
                  reads=[r_stat[s]], writes=[r_stat[s]])
            S.add(ACT, ACTF(rr_[s][0:tn, :], rr_[s][0:tn, :], AF.Identity, bias=nmr[s][0:tn, 0:1], scale=rstd[s][0:tn, 0:1]),
                  reads=[r_rr[s], r_stat[s]], writes=[r_rr[s]])
            S.add(DVE, TT(rr_[s][0:tn, :], rr_[s][0:tn, :], lng[0:tn, :], ALU.mult), reads=[r_rr[s], r_ln], writes=[r_rr[s]])
            S.add(POOL, TT(rr_[s][0:tn, :], rr_[s][0:tn, :], lnb[0:tn, :], ALU.add), reads=[r_rr[s], r_ln], writes=[r_rr[s]])
            dst = O["y_p"][t0:t0 + tn, :] if tc < 16 else O["y_s"][:, :]
            S.add(SP, DMA(dst, rr_[s][0:tn, :]), reads=[r_rr[s]], dmasem=d_yo[s])

    import contextlib
    es = contextlib.ExitStack()
    with nc.Block() as block:
        S.emit(block, es)
    return nc


def make_in_maps(inputs):
    f = lambda a: np.ascontiguousarray(np.asarray(a, dtype=np.float32))
    identb = np.eye(128, dtype=np.float32).astype(ml_dtypes.bfloat16)
    identf = np.eye(128, dtype=np.float32)
    onesb = np.ones((128, 128), dtype=np.float32).astype(ml_dtypes.bfloat16)
    kk = np.arange(128)[:, None]
    qq = np.arange(128)[None, :]
    maskneg = np.where(qq >= kk, 0.0, NEG).astype(np.float32)
    shared = {
        "w_in": f(inputs["w_in"][0]), "fox_bf": f(inputs["fox_bf"][0]).reshape(H, 1),
        "conv_w": f(inputs["conv_w"][0]), "conv_b": f(inputs["conv_b"][0]).reshape(FW, 1),
        "w_mem_kv": f(inputs["w_mem_kv"][0]), "w_fox_out": f(inputs["w_fox_out"][0]),
        "w_conv_out": f(inputs["w_conv_out"][0]), "w_mem_out": f(inputs["w_mem_out"][0]),
        "w_merge": f(inputs["w_merge"][0]), "b_merge": f(inputs["b_merge"][0]).reshape(3 * D, 1),
        "w_o": f(inputs["w_o"][0]), "ln_g": f(inputs["ln_g"][0]).reshape(1, D),
        "ln_b": f(inputs["ln_b"][0]).reshape(1, D),
        "identb": identb, "identf": identf, "onesb": onesb, "maskneg": maskneg,
        "masknegb": maskneg.astype(ml_dtypes.bfloat16),
    }
    maps = []
    for b in range(8):
        m = dict(shared)
        m["xp"] = f(inputs["x_prompt"][b])
        m["xs"] = f(inputs["x_sample"][b])
        m["mem"] = f(inputs["mem_prompt"][b])
        m["ck"] = f(inputs["cache_fox_k"][0, b]).reshape(PAST, FW)
        m["cv"] = f(inputs["cache_fox_v"][0, b]).reshape(PAST, FW)
        m["clf"] = f(inputs["cache_fox_logf"][0, b])
        m["sconv"] = f(inputs["state_conv"][0, b])
        m["cmk"] = f(inputs["cache_mem_k"][0, b]).reshape(MT, MW)
        m["cmv"] = f(inputs["cache_mem_v"][0, b]).reshape(MT, MW)
        maps.append(m)
    return maps


_NC_CACHE = {}


def kernel(**inputs):
    if "nc" not in _NC_CACHE:
        _NC_CACHE["nc"] = build()
    nc = _NC_CACHE["nc"]
    maps = make_in_maps(inputs)
    res = run_bass_kernel_spmd(nc, maps, core_ids=list(range(8)))
    R = res.results
    st = lambda n: np.stack([np.asarray(R[b][n], dtype=np.float32) for b in range(8)])
    return (
        st("y_p"), st("y_s"),
        st("fk_p").reshape(1, 8, NP, H, DH), st("fv_p").reshape(1, 8, NP, H, DH), st("fl_p").reshape(1, 8, NP, H),
        st("cv_p").reshape(1, 8, 2, FW), st("mk_p").reshape(1, 8, MT, 4, 256), st("mv_p").reshape(1, 8, MT, 4, 256),
        st("fk_s").reshape(1, 8, NS, H, DH), st("fv_s").reshape(1, 8, NS, H, DH), st("fl_s").reshape(1, 8, NS, H),
        st("cv_s").reshape(1, 8, 2, FW),
    )
```

```python
import numpy as np
import ml_dtypes
import concourse.bass as bass
import concourse.mybir as mybir
from concourse.bass_utils import run_bass_kernel_spmd

F32 = mybir.dt.float32
BF16 = mybir.dt.bfloat16
ALU = mybir.AluOpType
AF = mybir.ActivationFunctionType

PE, ACT, DVE, POOL, SP = "pe", "act", "dve", "pool", "sp"
ENGS = [PE, ACT, DVE, POOL, SP]


class Res:
    __slots__ = ("name", "w", "r")

    def __init__(self, name=""):
        self.name = name
        self.w = None
        self.r = {}


class DmaSem:
    __slots__ = ("sem", "count", "name")

    def __init__(self, name):
        self.name = name
        self.sem = None
        self.count = 0


class Op:
    __slots__ = ("eng", "fn", "deps", "dmasem", "marked", "val")

    def __init__(self, eng, fn):
        self.eng = eng
        self.fn = fn
        self.deps = []
        self.dmasem = None
        self.marked = False
        self.val = 0


class Sched:
    def __init__(self, nc, sync_same_engine=True):
        self.nc = nc
        self.ops = {e: [] for e in ENGS}
        self.dmasems = []
        self.sync_same = sync_same_engine

    def dmasem(self, name):
        d = DmaSem(name)
        self.dmasems.append(d)
        return d

    def _dep(self, op, prod):
        if prod is None or prod is op:
            return
        if prod.dmasem is not None:
            if op.dmasem is prod.dmasem:
                return
            op.deps.append((prod, prod.dmasem.count))
            return
        if prod.eng == op.eng and op.dmasem is None:
            if prod.eng == PE or not self.sync_same:
                return
        op.deps.append((prod, None))
        prod.marked = True

    def add(self, eng, fn, reads=(), writes=(), dmasem=None):
        op = Op(eng, fn)
        op.dmasem = dmasem
        for r in reads:
            self._dep(op, r.w)
        for w in writes:
            self._dep(op, w.w)
            for rr in w.r.values():
                self._dep(op, rr)
        if dmasem is not None:
            dmasem.count += 16
        key = dmasem if dmasem is not None else eng
        for r in reads:
            r.r[key] = op
        for w in writes:
            w.w = op
            w.r = {}
        self.ops[eng].append(op)
        return op

    def emit(self, block, es):
        nc = self.nc
        for e in ENGS:
            c = 0
            for op in self.ops[e]:
                if op.dmasem is None and op.marked:
                    c += 1
                    op.val = c
        esem = {e: es.enter_context(nc.semaphore("prog_" + e)) for e in [PE, ACT, DVE, POOL]}
        for d in self.dmasems:
            d.sem = es.enter_context(nc.semaphore("dma_" + d.name))

        def run(e, engobj):
            waited = {}
            for op in self.ops[e]:
                need = {}
                for (p, dv) in op.deps:
                    if p.dmasem is not None:
                        key, v = p.dmasem.sem, dv
                    else:
                        key, v = esem[p.eng], p.val
                    if v > need.get(key, 0):
                        need[key] = v
                for key, v in need.items():
                    if waited.get(key, 0) < v:
                        engobj.wait_ge(key, v)
                        waited[key] = v
                ins = op.fn(engobj)
                if op.dmasem is not None:
                    ins.then_inc(op.dmasem.sem, 16)
                elif op.marked:
                    ins.then_inc(esem[e], 1)
            if e == SP:
                for d in self.dmasems:
                    if d.count > 0:
                        engobj.wait_ge(d.sem, d.count)

        @block.tensor
        def _(eng):
            run(PE, eng)

        @block.scalar
        def _(eng):
            run(ACT, eng)

        @block.vector
        def _(eng):
            run(DVE, eng)

        @block.gpsimd
        def _(eng):
            run(POOL, eng)

        @block.sync
        def _(eng):
            run(SP, eng)


def MM(out, lhsT, rhs, start, stop):
    return lambda e: e.matmul(out, lhsT=lhsT, rhs=rhs, start=start, stop=stop)


def TR(out, in_, ident):
    return lambda e: e.transpose(out=out, in_=in_, identity=ident)


def TRM(out, in_, ident):
    return lambda e: e.matmul(out, lhsT=in_, rhs=ident, start=True, stop=True)


def DMA(out, in_, **kw):
    return lambda e: e.dma_start(out=out, in_=in_, **kw)


def ACTF(out, in_, func, bias=None, scale=None):
    kw = {}
    if bias is not None:
        kw["bias"] = bias
    if scale is not None:
        kw["scale"] = scale
    return lambda e: e.activation(out=out, in_=in_, func=func, **kw)


def COPY(out, in_):
    def f(e):
        if hasattr(e, "activation"):
            return e.activation(out=out, in_=in_, func=AF.Copy)
        return e.tensor_copy(out=out, in_=in_)
    return f


def TT(out, in0, in1, op):
    return lambda e: e.tensor_tensor(out=out, in0=in0, in1=in1, op=op)


def TS(out, in0, s1, op0, s2=None, op1=None):
    if op1 is None:
        return lambda e: e.tensor_scalar(out=out, in0=in0, scalar1=s1, scalar2=None, op0=op0)
    return lambda e: e.tensor_scalar(out=out, in0=in0, scalar1=s1, scalar2=s2, op0=op0, op1=op1)


def STT(out, in0, scalar, in1, op0, op1):
    return lambda e: e.scalar_tensor_tensor(out=out, in0=in0, scalar=scalar, in1=in1, op0=op0, op1=op1)


D = 2048
NP = 2048
NS = 64
NT = NP + NS
PAST = 1024
H = 12
DH = 128
FW = 1536
MW = 1024
MT = 256
INW = 14348
OFF_Q, OFF_K, OFF_V, OFF_F, OFF_G = 0, 1536, 3072, 4608, 4620
OFF_CB, OFF_CC, OFF_CH, OFF_CG = 6156, 7692, 9228, 10764
OFF_MQ, OFF_MG = 12300, 13324
TBLK = [(0, 512), (512, 512), (1024, 512), (1536, 512), (2048, 64)]
TCH = [(i * 128, 128) for i in range(16)] + [(2048, 64)]
SCALE = DH ** -0.5
MSCALE = 256 ** -0.5
DN_ALPHA = 2.0 ** 0.25
LN_EPS = 1e-5
NEG = -1.0e30

OUT_SPECS = [
    ("y_p", [NP, D]), ("y_s", [NS, D]), ("fk_p", [NP, FW]), ("fv_p", [NP, FW]), ("fl_p", [NP, H]),
    ("cv_p", [2, FW]), ("mk_p", [MT, MW]), ("mv_p", [MT, MW]), ("fk_s", [NS, FW]), ("fv_s", [NS, FW]),
    ("fl_s", [NS, H]), ("cv_s", [2, FW]),
]
IN_SPECS = [
    ("xp", [NP, D], F32), ("xs", [NS, D], F32), ("mem", [MT, D], F32), ("ck", [PAST, FW], F32),
    ("cv", [PAST, FW], F32), ("clf", [PAST, H], F32), ("sconv", [2, FW], F32), ("cmk", [MT, MW], F32),
    ("cmv", [MT, MW], F32), ("w_in", [D, INW], F32), ("fox_bf", [H, 1], F32), ("conv_w", [3, FW], F32),
    ("conv_b", [FW, 1], F32), ("w_mem_kv", [D, 2 * MW], F32), ("w_fox_out", [FW, D], F32),
    ("w_conv_out", [FW, D], F32), ("w_mem_out", [MW, D], F32), ("w_merge", [D, 3 * D], F32),
    ("b_merge", [3 * D, 1], F32), ("w_o", [D, D], F32), ("ln_g", [1, D], F32), ("ln_b", [1, D], F32),
    ("identb", [128, 128], BF16), ("identf", [128, 128], F32), ("onesb", [128, 128], BF16),
    ("maskneg", [128, 128], F32), ("masknegb", [128, 128], BF16),
]


class SbAlloc:
    def __init__(self, nc, lo=16512, hi=229376 - 2048):
        self.nc = nc
        self.lo = lo
        self.hi = hi
        self.top = lo
        self.n = 0
        self.live = []

    def mark(self):
        return self.top

    def mark_hi(self):
        return self.hi

    def release_hi(self, m):
        self.hi = m

    def release(self, mark):
        self.top = mark

    def alloc(self, name, shape, dt, res=(), high=False):
        nb = int(np.prod(shape[1:])) * mybir.dt.size(dt)
        nb = (nb + 63) // 64 * 64
        if high:
            assert self.hi - nb >= self.top, f"SBUF overflow allocating {name} (high)"
            self.hi -= nb
            off = self.hi
        else:
            off = self.top
            assert off + nb <= self.hi, f"SBUF overflow allocating {name}: {off + nb} > {self.hi}"
            self.top += nb
        self.n += 1
        t = self.nc.alloc_sbuf_tensor_at(f"{name}_{self.n}", list(shape), dt, offset=off)
        inherited = {}
        k = 0
        for (o, e, rl) in self.live:
            if o < off + nb and off < e:
                for r0 in rl:
                    if r0.w is not None:
                        inherited[("w", k)] = r0.w
                        k += 1
                    for rr in r0.r.values():
                        inherited[("r", k)] = rr
                        k += 1
        for r1 in res:
            r1.r.update(inherited)
        self.live = [(o, e, rl) for (o, e, rl) in self.live if not (o >= off and e <= off + nb)]
        self.live.append((off, off + nb, list(res)))
        return t


def build(debug=None, upto=99):
    nc = bass.Bass("TRN2", target_bir_lowering=False)
    I = {n: nc.dram_tensor(n, s, dt, kind="ExternalInput").ap() for (n, s, dt) in IN_SPECS}
    O = {n: nc.dram_tensor(n, s, F32, kind="ExternalOutput").ap() for (n, s) in OUT_SPECS}
    cum_scr = nc.dram_tensor("cum_scr", [H, NP + PAST + NS], F32).ap()
    m_scr = nc.dram_tensor("m_scr", [16, 128, NT], F32).ap()
    m_scrb = nc.dram_tensor("m_scrb", [16, 128, NT], BF16).ap()
    dbg = None
    if debug is not None:
        dbg = nc.dram_tensor("dbg", list(debug), F32, kind="ExternalOutput").ap()

    nc.alloc_sbuf_tensor("arena", [128, 229376 - 16512 - 64], mybir.dt.uint8)
    A = SbAlloc(nc)
    S = Sched(nc)
    banks = [nc.alloc_psum_tensor(f"bank{i}", [128, 512], F32) for i in range(8)]
    bankr = [Res(f"bank{i}") for i in range(8)]

    r_const = Res("const")
    identb = A.alloc("identb", [128, 128], BF16, [r_const])
    identf = A.alloc("identf", [128, 128], F32, [r_const])
    onesb = A.alloc("onesb", [128, 128], BF16, [r_const])
    maskneg = A.alloc("maskneg", [128, 128], F32, [r_const])
    masknegb = A.alloc("masknegb", [128, 128], BF16, [r_const])
    ones1 = A.alloc("ones1", [128, 1], F32, [r_const])
    nbf = A.alloc("nbf", [H, 1], F32, [r_const])
    dconst = S.dmasem("const")
    for t, n in ((identb, "identb"), (identf, "identf"), (onesb, "onesb"), (maskneg, "maskneg"), (masknegb, "masknegb")):
        S.add(SP, DMA(t[:], I[n]), writes=[r_const], dmasem=dconst)
    S.add(SP, DMA(nbf[:], I["fox_bf"]), writes=[r_const], dmasem=dconst)
    S.add(DVE, lambda e: e.memset(ones1[:], 1.0), writes=[r_const])
    S.add(DVE, TS(nbf[:], nbf[:], -1.0, ALU.mult), reads=[r_const], writes=[r_const])

    bm = A.alloc("bm", [128, 48], F32, [r_const])
    epsb = A.alloc("epsb", [128, 1], F32, [r_const])
    S.add(SP, DMA(bm[:], I["b_merge"].rearrange("(c p) o -> p (c o)", p=128), allow_slow_non_contiguous=True),
          writes=[r_const], dmasem=dconst)
    S.add(DVE, lambda e: e.memset(epsb[:], LN_EPS), writes=[r_const])
    r_ncc = Res("ncc")
    ncc = A.alloc("ncc", [128, 25, H], F32, [r_ncc])
    r_ut = Res("utail")
    utail = A.alloc("utail", [128, 2, 12], F32, [r_ut])
    ustail = A.alloc("ustail", [128, 2, 12], F32, [r_ut])
    mkX = A.mark()
    r_xT = [[Res(f"xT{i}_{g}") for g in range(4)] for i in range(17)]
    xT = A.alloc("xT", [128, 16, NT], BF16, [r for l in r_xT for r in l])
    def xT_res(t0, n):
        return [r for i, (c0, cn) in enumerate(TCH) if c0 < t0 + n and t0 < c0 + cn for r in r_xT[i]]

    mk0 = A.mark()
    r_xb = [Res(f"xb{i}") for i in range(3)]
    xb = [A.alloc(f"xb{i}", [128, D], BF16, [r_xb[i]]) for i in range(3)]
    d_xb = [S.dmasem(f"xb{i}") for i in range(3)]
    for tc, (t0, tn) in enumerate(TCH):
        s = tc % 3
        src = I["xp"][t0:t0 + tn, :] if tc < 16 else I["xs"][:, :]
        S.add(POOL, DMA(xb[s][0:tn, :].rearrange("p (a b) -> p a b", b=1024),
                        src.rearrange("p (a b) -> p a b", b=1024)),
              writes=[r_xb[s]], dmasem=d_xb[s])
        for g in range(4):
            bk = (tc * 4 + g) % 8
            bv = banks[bk]
            for j in range(4):
                c = g * 4 + j
                S.add(PE, TRM(bv[:, j * 128:j * 128 + tn], xb[s][0:tn, c * 128:(c + 1) * 128], identb[0:tn, 0:tn]),
                      reads=[r_xb[s], r_const], writes=[bankr[bk]])
            eng = ACT if (tc * 4 + g) % 2 == 0 else DVE
            S.add(eng, COPY(xT[:, g * 4:g * 4 + 4, t0:t0 + tn],
                            bv[:, 0:512].rearrange("p (a b) -> p a b", b=128)[:, :, 0:tn]),
                  reads=[bankr[bk]], writes=[r_xT[tc][g]])
    A.release(mk0)

    mk1 = A.mark()
    r_pro = Res("pro")
    wfg = A.alloc("wfg", [128, 16, H], BF16, [r_pro])
    r_lf = Res("logfT")
    logfT = A.alloc("logfT", [H, NT], F32, [r_lf])
    r_lc = Res("lcT")
    lcT = A.alloc("lcT", [H, PAST + NS], F32, [r_lc])
    r_cum = Res("cum")
    cumP = A.alloc("cumP", [H, NP], F32, [r_cum])
    cumS = A.alloc("cumS", [H, PAST + NS], F32, [r_cum])
    r_et = [Res("et0"), Res("et1")]
    etmp = [A.alloc("et0", [H, 512], F32, [r_et[0]]), A.alloc("et1", [H, 512], F32, [r_et[1]])]
    r_clf = Res("clf")
    clf_tm = A.alloc("clf_tm", [128, 8, H], F32, [r_clf])
    r_lfcol = Res("lfcol")
    lfcol = A.alloc("lfcol", [128, 17, H], F32, [r_lfcol])
    d_pro = S.dmasem("pro")
    S.add(POOL, DMA(wfg[:], I["w_in"][:, OFF_F:OFF_F + H].rearrange("(c p) n -> p c n", p=128)),
          writes=[r_pro], dmasem=d_pro)
    d_clf = S.dmasem("clf")
    S.add(SP, DMA(clf_tm[:], I["clf"].rearrange("(c p) h -> p c h", p=128)), writes=[r_clf], dmasem=d_clf)
    for tb, (t0, tn) in enumerate(TBLK):
        bk = tb % 2
        for c in range(16):
            S.add(PE, MM(banks[bk][0:H, 0:tn], wfg[:, c, :], xT[:, c, t0:t0 + tn], c == 0, c == 15),
                  reads=[r_pro] + xT_res(t0, tn), writes=[bankr[bk]])
        S.add(ACT, ACTF(etmp[bk][:, 0:tn], banks[bk][0:H, 0:tn], AF.Exp, bias=nbf[:, 0:1], scale=-1.0),
              reads=[bankr[bk], r_const], writes=[r_et[bk]])
        S.add(ACT, ACTF(etmp[bk][:, 0:tn], etmp[bk][:, 0:tn], AF.Ln, bias=1.0),
              reads=[r_et[bk]], writes=[r_et[bk]])
        S.add(DVE, TS(logfT[:, t0:t0 + tn], etmp[bk][:, 0:tn], -1.0, ALU.mult),
              reads=[r_et[bk]], writes=[r_lf])
    for c in range(8):
        bk = 2 + c // 4
        S.add(PE, TR(banks[bk][0:H, (c % 4) * 128:(c % 4 + 1) * 128], clf_tm[:, c, :], identf[:, :]),
              reads=[r_clf, r_const], writes=[bankr[bk]])
        if c % 4 == 3:
            S.add(ACT, COPY(lcT[:, (c // 4) * 512:(c // 4 + 1) * 512], banks[bk][0:H, :]),
                  reads=[bankr[bk]], writes=[r_lc])
    S.add(DVE, COPY(lcT[:, PAST:PAST + NS], logfT[:, NP:NT]), reads=[r_lf], writes=[r_lc])

    def SCAN(out, data1, n):
        return lambda e: e.tensor_tensor_scan(out=out, data0=ones1[0:H, 0:1].broadcast_to([H, n]), data1=data1,
                                               initial=0.0, op0=ALU.mult, op1=ALU.add)
    S.add(DVE, SCAN(cumP[:], logfT[:, 0:NP], NP), reads=[r_lf, r_const], writes=[r_cum])
    S.add(DVE, SCAN(cumS[:], lcT[:], PAST + NS), reads=[r_lc, r_const], writes=[r_cum])
    r_cscr = Res("cum_scr")
    d_cscr = S.dmasem("cscr")
    S.add(SP, DMA(cum_scr[:, 0:NP], cumP[:]), reads=[r_cum], writes=[r_cscr], dmasem=d_cscr)
    S.add(SP, DMA(cum_scr[:, NP:NP + PAST + NS], cumS[:]), reads=[r_cum], writes=[r_cscr], dmasem=d_cscr)
    for tc, (t0, tn) in enumerate(TCH):
        S.add(PE, TR(banks[4][0:tn, tc * H:(tc + 1) * H], logfT[:, t0:t0 + tn], identf[0:H, 0:H]),
              reads=[r_lf, r_const], writes=[bankr[4]])
    S.add(ACT, COPY(lfcol[:, 0:16, :], banks[4][:, 0:16 * H].rearrange("p (a b) -> p a b", b=H)),
          reads=[bankr[4]], writes=[r_lfcol])
    S.add(ACT, COPY(lfcol[0:NS, 16, :], banks[4][0:NS, 16 * H:17 * H]), reads=[bankr[4]], writes=[r_lfcol])
    d_fl = S.dmasem("fl")
    S.add(SP, DMA(O["fl_p"].rearrange("(c p) h -> p c h", p=128), lfcol[:, 0:16, :]), reads=[r_lfcol], dmasem=d_fl)
    S.add(SP, DMA(O["fl_s"], lfcol[0:NS, 16, :]), reads=[r_lfcol], dmasem=d_fl)
    for c in range(16):
        S.add(PE, TR(banks[5][:, c * H:(c + 1) * H], cumP[:, c * 128:(c + 1) * 128], identf[0:H, 0:H]),
              reads=[r_cum, r_const], writes=[bankr[5]])
    for c in range(9):
        n = 128 if c < 8 else NS
        S.add(PE, TR(banks[5][0:n, (16 + c) * H:(17 + c) * H], cumS[:, c * 128:c * 128 + n], identf[0:H, 0:H]),
              reads=[r_cum, r_const], writes=[bankr[5]])
    S.add(DVE, TS(ncc[:, 0:24, :], banks[5][:, 0:24 * H].rearrange("p (a b) -> p a b", b=H), -1.0, ALU.mult),
          reads=[bankr[5]], writes=[r_ncc])
    S.add(DVE, TS(ncc[0:NS, 24, :], banks[5][0:NS, 24 * H:25 * H], -1.0, ALU.mult),
          reads=[bankr[5]], writes=[r_ncc])
    A.release(mk1)

    mk2 = A.mark()
    r_aT = [[Res(f"aT{h}_{tb}") for tb in range(5)] for h in range(H)]
    aT = A.alloc("aT", [128, H, NT], BF16, [r for l in r_aT for r in l])
    mkB = A.mark()
    r_w = [Res("wq"), Res("wkv"), Res("wg")]
    wbuf = A.alloc("wbuf", [128, 16, 512], BF16, r_w)
    d_w = [S.dmasem("wq"), S.dmasem("wkv"), S.dmasem("wg")]
    r_qT = [Res(f"qT{i}") for i in range(5)]
    qT = A.alloc("qT", [128, NT], BF16, r_qT)
    r_kT = [Res(f"kT{i}") for i in range(5)]
    kT = A.alloc("kT", [128, NT], BF16, r_kT)
    r_vb = [Res(f"vb{i}") for i in range(5)]
    vb = A.alloc("vb", [128, 17, 128], BF16, r_vb)
    r_sg = [Res(f"sg{i}") for i in range(5)]
    sg = A.alloc("sg", [128, NT], F32, r_sg)
    r_cqb = Res("cqb")
    cqb = A.alloc("cqb", [128, NT], F32, [r_cqb])
    d_cqb = S.dmasem("cqb")
    r_kv32 = [Res("kv32_0"), Res("kv32_1")]
    kv32 = [A.alloc("kv32_0", [128, 4, 256], F32, [r_kv32[0]]), A.alloc("kv32_1", [128, 4, 256], F32, [r_kv32[1]])]
    d_kvo = [S.dmasem("kvo0"), S.dmasem("kvo1")]
    r_kb = [Res("kb0"), Res("kb1")]
    kb = [A.alloc("kb0", [128, 4, 128], BF16, [r_kb[0]]), A.alloc("kb1", [128, 4, 128], BF16, [r_kb[1]])]
    r_cache = Res("cache")
    kc_tm = A.alloc("kc_tm", [128, 8, 128], BF16, [r_cache])
    vc_tm = A.alloc("vc_tm", [128, 8, 128], BF16, [r_cache])
    d_cache = S.dmasem("cache")
    r_kcT = Res("kcT")
    kcT = A.alloc("kcT", [128, PAST], BF16, [r_kcT])
    NST, NPT = 4, 6
    r_st = [Res(f"st{i}") for i in range(NST)]
    stmp = [A.alloc(f"st{i}", [128, 512], F32, [r_st[i]]) for i in range(NST)]
    r_pt = [Res(f"pt{i}") for i in range(NPT)]
    pt = [A.alloc(f"pt{i}", [128, 512], BF16, [r_pt[i]]) for i in range(NPT)]
    SRING = [2, 3, 0, 1]
    stk = [0]
    r_rl = Res("rl")
    rl = A.alloc("rl", [128, 512], F32, [r_rl])
    r_on = Res("on")
    on = A.alloc("on", [128, 512], F32, [r_on])

    def load_w(h):
        for comp, (off, slot) in enumerate(((OFF_Q, 0), (OFF_K, 1), (OFF_V, 1), (OFF_G, 2))):
            S.add(POOL, DMA(wbuf[:, :, comp * 128:(comp + 1) * 128],
                            I["w_in"][:, off + h * 128:off + (h + 1) * 128].rearrange("(c p) n -> p c n", p=128)),
                  writes=[r_w[slot]], dmasem=d_w[slot])

    def load_cache(h):
        S.add(POOL, DMA(kc_tm[:], I["ck"][:, h * 128:(h + 1) * 128].rearrange("(c p) n -> p c n", p=128)),
              writes=[r_cache], dmasem=d_cache)
        S.add(POOL, DMA(vc_tm[:], I["cv"][:, h * 128:(h + 1) * 128].rearrange("(c p) n -> p c n", p=128)),
              writes=[r_cache], dmasem=d_cache)

    def load_cqb(h):
        S.add(SP, DMA(cqb[:, 0:NP], cum_scr[h:h + 1, 0:NP].partition_broadcast(128)),
              reads=[r_cscr], writes=[r_cqb], dmasem=d_cqb)
        S.add(SP, DMA(cqb[:, NP:NT], cum_scr[h:h + 1, NP + PAST:NP + PAST + NS].partition_broadcast(128)),
              reads=[r_cscr], writes=[r_cqb], dmasem=d_cqb)

    pj = [0]
    sbk = [0]
    pti = [0]
    obk = [0]

    nheads = H if upto >= 2 else 1
    load_w(0)
    load_cache(0)
    load_cqb(0)
    for h in range(nheads):
        pending_kt = []
        for grp in range(5):
            chunks = [(tc, TCH[tc]) for tc in range(grp * 4, min(grp * 4 + 4, 17))]
            s = grp % 2
            for j, (tc, (t0, tn)) in enumerate(chunks):
                bk = pj[0]; pj[0] ^= 1
                for c in range(16):
                    S.add(PE, MM(banks[bk][0:tn, 0:256], xT[:, c, t0:t0 + tn],
                                 wbuf[:, c, 128:384], c == 0, c == 15),
                          reads=[r_w[1]] + r_xT[tc], writes=[bankr[bk]])
                S.add(ACT, COPY(kv32[s][0:tn, j, :], banks[bk][0:tn, 0:256]),
                      reads=[bankr[bk]], writes=[r_kv32[s]])
            nj = len(chunks)
            tn = chunks[0][1][1]
            t0 = chunks[0][1][0]
            S.add(POOL, COPY(kb[s][0:tn, 0:nj, :], kv32[s][0:tn, 0:nj, 0:128]), reads=[r_kv32[s]], writes=[r_kb[s]])
            S.add(POOL, COPY(vb[0:tn, grp * 4:grp * 4 + nj, :], kv32[s][0:tn, 0:nj, 128:256]),
                  reads=[r_kv32[s]], writes=[r_vb[grp]])
            if grp < 4:
                S.add(SP, DMA(O["fk_p"][t0:t0 + 512, h * 128:(h + 1) * 128].rearrange("(c p) n -> p c n", p=128),
                              kv32[s][:, :, 0:128]), reads=[r_kv32[s]], dmasem=d_kvo[s])
                S.add(SP, DMA(O["fv_p"][t0:t0 + 512, h * 128:(h + 1) * 128].rearrange("(c p) n -> p c n", p=128),
                              kv32[s][:, :, 128:256]), reads=[r_kv32[s]], dmasem=d_kvo[s])
            else:
                S.add(SP, DMA(O["fk_s"][:, h * 128:(h + 1) * 128], kv32[s][0:NS, 0, 0:128]),
                      reads=[r_kv32[s]], dmasem=d_kvo[s])
                S.add(SP, DMA(O["fv_s"][:, h * 128:(h + 1) * 128], kv32[s][0:NS, 0, 128:256]),
                      reads=[r_kv32[s]], dmasem=d_kvo[s])
            def ktrans(grp=grp, s=s, nj=nj, tn=tn, t0=t0):
                bk = pj[0]; pj[0] ^= 1
                bv = banks[bk]
                for j in range(nj):
                    S.add(PE, TRM(bv[:, j * 128:j * 128 + tn], kb[s][0:tn, j, :], identb[0:tn, 0:tn]),
                          reads=[r_kb[s], r_const], writes=[bankr[bk]])
                S.add(ACT, COPY(kT[:, t0:t0 + nj * tn], bv[:, 0:nj * 128] if tn == 128 else bv[:, 0:tn]),
                      reads=[bankr[bk]], writes=[r_kT[grp]])
            if pending_kt:
                pending_kt.pop()()
            pending_kt.append(ktrans)
        for comp, dst, rr, slot in ((0, qT, r_qT, 0), (3, sg, r_sg, 2)):
            for tb, (t0, tn) in enumerate(TBLK):
                if tb == 1 and pending_kt:
                    pending_kt.pop()()
                bk = pj[0]; pj[0] ^= 1
                for c in range(16):
                    S.add(PE, MM(banks[bk][:, 0:tn], wbuf[:, c, comp * 128:(comp + 1) * 128], xT[:, c, t0:t0 + tn], c == 0, c == 15),
                          reads=[r_w[slot]] + xT_res(t0, tn), writes=[bankr[bk]])
                if comp == 0:
                    S.add(ACT, COPY(dst[:, t0:t0 + tn], banks[bk][:, 0:tn]), reads=[bankr[bk]], writes=[rr[tb]])
                else:
                    S.add(ACT, ACTF(dst[:, t0:t0 + tn], banks[bk][:, 0:tn], AF.Silu), reads=[bankr[bk]], writes=[rr[tb]])
        if h + 1 < nheads:
            load_w(h + 1)
        for half in range(2):
            bk = pj[0]; pj[0] ^= 1
            bv = banks[bk]
            for j in range(4):
                S.add(PE, TRM(bv[:, j * 128:(j + 1) * 128], kc_tm[:, half * 4 + j, :], identb[:, :]),
                      reads=[r_cache, r_const], writes=[bankr[bk]])
            S.add(ACT, COPY(kcT[:, half * 512:(half + 1) * 512], bv[:, 0:512]), reads=[bankr[bk]], writes=[r_kcT])

        tiles = []

        def attend(qcol0, qn, keys, tbidx):
            ob = obk[0]; obk[0] ^= 1
            nk = len(keys)
            for i, kk in enumerate(keys):
                tiles.append((qcol0, qn, tbidx, 4 + ob, 6 + ob, i == 0, i == nk - 1) + kk)

        for qb in range(4):
            keys = []
            for kc in range(4 * qb + 4):
                diag = kc >= 4 * qb
                n0 = (kc - 4 * qb) * 128 if diag else 0
                keys.append((kT[:, kc * 128:(kc + 1) * 128], 128, vb[:, kc, :], ncc[:, kc, h:h + 1], n0, diag,
                             [r_kT[kc // 4], r_vb[kc // 4]]))
            attend(qb * 512, 512, keys, qb)
        keys = []
        for kc in range(8):
            keys.append((kcT[:, kc * 128:(kc + 1) * 128], 128, vc_tm[:, kc, :], ncc[:, 16 + kc, h:h + 1], 0, False,
                         [r_kcT, r_cache]))
        keys.append((kT[:, NP:NT], NS, vb[0:NS, 16, :], ncc[0:NS, 24, h:h + 1], 0, True, [r_kT[4], r_vb[4]]))
        attend(NP, NS, keys, 4)

        LA = 3
        slots = {}

        def issue_S(i):
            (qcol0, qn, tbidx, Ob, Lb, first, last, kl, kn, vv, bias, n0, diag, rds) = tiles[i]
            sb_ = SRING[sbk[0] % 4]; sbk[0] += 1
            st = stk[0] % NST; stk[0] += 1
            p = pti[0] % NPT; pti[0] += 1
            slots[i] = p
            S.add(PE, MM(banks[sb_][0:kn, n0:qn], kl, qT[:, qcol0 + n0:qcol0 + qn], True, not diag),
                  reads=rds + [r_qT[tbidx]], writes=[bankr[sb_]])
            if diag:
                S.add(PE, MM(banks[sb_][0:kn, n0:n0 + kn], identb[0:kn, 0:kn], masknegb[0:kn, 0:kn], False, True),
                      reads=[r_const], writes=[bankr[sb_]])
            S.add(DVE, STT(stmp[st][0:kn, n0:qn], banks[sb_][0:kn, n0:qn], SCALE,
                           cqb[0:kn, qcol0 + n0:qcol0 + qn], ALU.mult, ALU.add),
                  reads=[bankr[sb_], r_cqb], writes=[r_st[st]])
            S.add(ACT, ACTF(pt[p][0:kn, n0:qn], stmp[st][0:kn, n0:qn], AF.Exp, bias=bias),
                  reads=[r_st[st], r_ncc], writes=[r_pt[p]])

        def issue_PV(i):
            (qcol0, qn, tbidx, Ob, Lb, first, last, kl, kn, vv, bias, n0, diag, rds) = tiles[i]
            p = slots.pop(i)
            S.add(PE, MM(banks[Ob][:, n0:qn], vv, pt[p][0:kn, n0:qn], first, last),
                  reads=[r_pt[p]] + rds, writes=[bankr[Ob]])
            S.add(PE, MM(banks[Lb][:, n0:qn], onesb[0:kn, :], pt[p][0:kn, n0:qn], first, last),
                  reads=[r_pt[p], r_const], writes=[bankr[Lb]])
            if last:
                S.add(ACT, ACTF(rl[:, 0:qn], banks[Lb][:, 0:qn], AF.Ln), reads=[bankr[Lb]], writes=[r_rl])
                S.add(ACT, ACTF(rl[:, 0:qn], rl[:, 0:qn], AF.Exp, scale=-1.0), reads=[r_rl], writes=[r_rl])
                S.add(DVE, TT(on[:, 0:qn], banks[Ob][:, 0:qn], rl[:, 0:qn], ALU.mult), reads=[bankr[Ob], r_rl], writes=[r_on])
                S.add(POOL, TT(aT[:, h, qcol0:qcol0 + qn], on[:, 0:qn], sg[:, qcol0:qcol0 + qn], ALU.mult),
                      reads=[r_on, r_sg[tbidx]], writes=[r_aT[h][tbidx]])

        nt_ = len(tiles)
        for i in range(min(LA, nt_)):
            issue_S(i)
        for i in range(nt_):
            if i + LA < nt_:
                issue_S(i + LA)
            issue_PV(i)
        if h + 1 < nheads:
            load_cache(h + 1)
            load_cqb(h + 1)

    if debug is not None and upto <= 2:
        r_d = Res("dbgt")
        dt_ = A.alloc("dbgt", [128, NT], F32, [r_d])
        d_dbg = S.dmasem("dbg")
        for h in range(nheads):
            S.add(DVE, COPY(dt_[:], aT[:, h, :]), reads=r_aT[h], writes=[r_d])
            S.add(SP, DMA(dbg[:, h, :], dt_[:]), reads=[r_d], dmasem=d_dbg)

    A.release(mkB)
    r_mscr = [[Res(f"mscr{j}_{tb}") for tb in range(5)] for j in range(16)]

    def combine(aT_, r_a, nchunks, w_out, gate, first, hooks=None, last=False):
        mkc = A.mark()
        r_wo = [Res("wo0"), Res("wo1")]
        wo = [A.alloc(f"wo{i}", [128, nchunks, 256], BF16, [r_wo[i]]) for i in range(2)]
        wm = [A.alloc(f"wm{i}", [128, 16, 256], BF16, [r_wo[i]]) for i in range(2)]
        d_wo = [S.dmasem(f"wo{gate}_0"), S.dmasem(f"wo{gate}_1")]
        r_gt = [Res("gt0"), Res("gt1")]
        gt = [A.alloc(f"gt{i}", [128, 512], F32, [r_gt[i]]) for i in range(2)]
        r_mt = [Res("mt0"), Res("mt1")]
        mt = [A.alloc(f"mt{i}", [128, 512], F32, [r_mt[i]]) for i in range(2)]
        d_mo = [S.dmasem(f"mo{gate}_0"), S.dmasem(f"mo{gate}_1")]
        r_pv = [Res("pv0"), Res("pv1")]
        pv = [A.alloc(f"pv{i}", [128, 512], F32, [r_pv[i]]) for i in range(2)]
        if last:
            r_mtb = [Res("mtb0"), Res("mtb1")]
            mtb = [A.alloc(f"mtb{i}", [128, 512], BF16, [r_mtb[i]]) for i in range(2)]
        d_pv = [S.dmasem(f"pv{gate}_0"), S.dmasem(f"pv{gate}_1")]

        def loadw(jg):
            s = jg % 2
            S.add(POOL, DMA(wo[s][:], w_out[:, jg * 256:(jg + 1) * 256].rearrange("(c p) n -> p c n", p=128)),
                  writes=[r_wo[s]], dmasem=d_wo[s])
            S.add(POOL, DMA(wm[s][:], I["w_merge"][:, gate * D + jg * 256:gate * D + (jg + 1) * 256]
                            .rearrange("(c p) n -> p c n", p=128)),
                  writes=[r_wo[s]], dmasem=d_wo[s])
        def load_pv(k):
            j_, tb_ = k // 5, k % 5
            t0_, tn_ = TBLK[tb_]
            S.add(SP, DMA(pv[k % 2][:, 0:tn_], m_scr[j_, :, t0_:t0_ + tn_]), reads=[r_mscr[j_][tb_]],
                  writes=[r_pv[k % 2]], dmasem=d_pv[k % 2])
        loadw(0)
        it = 0
        for jg in range(8):
            if jg + 1 < 8:
                loadw(jg + 1)
            if hooks and jg in hooks:
                hooks[jg]()
            s = jg % 2
            for jj in range(2):
                j = jg * 2 + jj
                for tb, (t0, tn) in enumerate(TBLK):
                    p2 = it % 2
                    it += 1
                    yb, gb = 2 * p2, 2 * p2 + 1
                    if not first:
                        if it == 1:
                            load_pv(0)
                        if it < 80:
                            load_pv(it)
                    for c in range(16):
                        S.add(PE, MM(banks[gb][:, 0:tn], wm[s][:, c, jj * 128:(jj + 1) * 128], xT[:, c, t0:t0 + tn],
                                     c == 0, c == 15), reads=[r_wo[s]] + xT_res(t0, tn), writes=[bankr[gb]])
                    for c in range(nchunks):
                        S.add(PE, MM(banks[yb][:, 0:tn], wo[s][:, c, jj * 128:(jj + 1) * 128], aT_[:, c, t0:t0 + tn],
                                     c == 0, c == nchunks - 1), reads=[r_wo[s], r_a(c, tb)], writes=[bankr[yb]])
                    S.add(ACT, ACTF(gt[p2][:, 0:tn], banks[gb][:, 0:tn], AF.Sigmoid, bias=bm[:, gate * 16 + j:gate * 16 + j + 1]),
                          reads=[bankr[gb], r_const], writes=[r_gt[p2]])
                    S.add(DVE, TT(mt[p2][:, 0:tn], banks[yb][:, 0:tn], gt[p2][:, 0:tn], ALU.mult),
                          reads=[bankr[yb], r_gt[p2]], writes=[r_mt[p2]])
                    if last:
                        S.add(POOL, TT(mtb[p2][:, 0:tn], mt[p2][:, 0:tn], pv[p2][:, 0:tn], ALU.add),
                              reads=[r_mt[p2], r_pv[p2]], writes=[r_mtb[p2]])
                        S.add(SP, DMA(m_scrb[j, :, t0:t0 + tn], mtb[p2][:, 0:tn]), reads=[r_mtb[p2]],
                              writes=[r_mscr[j][tb]], dmasem=d_mo[p2])
                        continue
                    if not first:
                        S.add(POOL, TT(mt[p2][:, 0:tn], mt[p2][:, 0:tn], pv[p2][:, 0:tn], ALU.add),
                              reads=[r_mt[p2], r_pv[p2]], writes=[r_mt[p2]])
                    S.add(SP, DMA(m_scr[j, :, t0:t0 + tn], mt[p2][:, 0:tn]), reads=[r_mt[p2]],
                          writes=[r_mscr[j][tb]], dmasem=d_mo[p2])
        A.release(mkc)

    chi = A.mark_hi()
    r_cw = [Res("cw0"), Res("cw1")]
    cwb = [A.alloc(f"cwb{i}", [128, 16, 512], BF16, [r_cw[i]], high=True) for i in range(2)]
    d_cw = [S.dmasem("cw0"), S.dmasem("cw1")]

    def load_cw(c):
        s = c % 2
        for comp, off in enumerate((OFF_CB, OFF_CC, OFF_CH, OFF_CG)):
            S.add(POOL, DMA(cwb[s][:, :, comp * 128:(comp + 1) * 128],
                            I["w_in"][:, off + c * 128:off + (c + 1) * 128].rearrange("(c p) n -> p c n", p=128)),
                  writes=[r_cw[s]], dmasem=d_cw[s])
    if upto >= 3:
        combine(aT, lambda c, tb: r_aT[c][tb], H, I["w_fox_out"], 0, True,
                hooks={7: (lambda: load_cw(0))} if upto >= 4 else None)
    A.release(mk2)

    if upto >= 4:
        mk3 = A.mark()
        r_aC = [[Res(f"aC{c}_{tb}") for tb in range(5)] for c in range(12)]
        aC = A.alloc("aC", [128, 12, NT], BF16, [r for l in r_aC for r in l])
        mk3b = A.mark()
        r_cs = Res("convsmall")
        cwT = A.alloc("cwT", [128, 3, 12], F32, [r_cs])
        cbT = A.alloc("cbT", [128, 12], F32, [r_cs])
        sconvT = A.alloc("sconvT", [128, 2, 12], F32, [r_cs])
        d_cs = S.dmasem("convsmall")
        for j in range(3):
            S.add(SP, DMA(cwT[:, j, :], I["conv_w"][j:j + 1, :].rearrange("o (c p) -> p (o c)", p=128),
                          allow_slow_non_contiguous=True), writes=[r_cs], dmasem=d_cs)
        S.add(SP, DMA(cbT[:], I["conv_b"].rearrange("(c p) o -> p (c o)", p=128), allow_slow_non_contiguous=True),
              writes=[r_cs], dmasem=d_cs)
        for j in range(2):
            S.add(SP, DMA(sconvT[:, j, :], I["sconv"][j:j + 1, :].rearrange("o (c p) -> p (o c)", p=128),
                          allow_slow_non_contiguous=True), writes=[r_cs], dmasem=d_cs)
        r_cg = [Res(f"cg{i}") for i in range(5)]
        cg = A.alloc("cg", [128, NT], F32, r_cg)
        r_u = [Res(f"u{i}") for i in range(5)]
        r_u0 = Res("u0")
        up = A.alloc("up", [128, 2 + NP], F32, r_u[0:4] + [r_u0])
        us = A.alloc("us", [128, 2 + NS], F32, [r_u[4]])
        r_bg = [Res(f"bg{i}") for i in range(5)]
        bg = A.alloc("bg", [128, NT], F32, r_bg)
        r_sc = [Res(f"sc{i}") for i in range(5)]
        scg = A.alloc("scg", [128, NT], F32, r_sc)
        r_tb = Res("tbuf")
        tbuf = A.alloc("tbuf", [128, NT], F32, [r_tb])
        S.add(POOL, lambda e: e.memset(up[:, 0:2], 0.0), writes=[r_u0])

        for c in range(12):
            s = c % 2
            for tb, (t0, tn) in enumerate(TBLK):
                if tb == 1 and c + 1 < 12:
                    load_cw(c + 1)
                def proj(comp):
                    bk = pj[0]; pj[0] ^= 1
                    for kc in range(16):
                        S.add(PE, MM(banks[bk][:, 0:tn], cwb[s][:, kc, comp * 128:(comp + 1) * 128], xT[:, kc, t0:t0 + tn],
                                     kc == 0, kc == 15), reads=[r_cw[s]] + xT_res(t0, tn), writes=[bankr[bk]])
                    return bk
                bk = proj(1)
                S.add(ACT, COPY(cg[:, t0:t0 + tn], banks[bk][:, 0:tn]), reads=[bankr[bk]], writes=[r_cg[tb]])
                bk = proj(2)
                udst = up[:, 2 + t0:2 + t0 + tn] if tb < 4 else us[:, 2:2 + NS]
                S.add(DVE, TT(udst, banks[bk][:, 0:tn], cg[:, t0:t0 + tn], ALU.mult),
                      reads=[bankr[bk], r_cg[tb]], writes=[r_u[tb]])
                bk = proj(0)
                S.add(ACT, COPY(bg[:, t0:t0 + tn], banks[bk][:, 0:tn]), reads=[bankr[bk]], writes=[r_bg[tb]])
                bk = proj(3)
                S.add(ACT, ACTF(scg[:, t0:t0 + tn], banks[bk][:, 0:tn], AF.Silu), reads=[bankr[bk]], writes=[r_sc[tb]])
            S.add(POOL, COPY(us[:, 0:2], sconvT[:, :, c]), reads=[r_cs], writes=[r_u[4]])
            for (ub, n, o0, rr) in ((up, NP, 0, r_u[0:4] + [r_u0]), (us, NS, NP, [r_u[4]])):
                S.add(DVE, TS(tbuf[:, o0:o0 + n], ub[:, 2:2 + n], cwT[:, 2, c:c + 1], ALU.mult, cbT[:, c:c + 1], ALU.add),
                      reads=rr + [r_cs], writes=[r_tb])
                S.add(DVE, STT(tbuf[:, o0:o0 + n], ub[:, 1:1 + n], cwT[:, 1, c:c + 1], tbuf[:, o0:o0 + n], ALU.mult, ALU.add),
                      reads=rr + [r_cs, r_tb], writes=[r_tb])
                S.add(DVE, STT(tbuf[:, o0:o0 + n], ub[:, 0:n], cwT[:, 0, c:c + 1], tbuf[:, o0:o0 + n], ALU.mult, ALU.add),
                      reads=rr + [r_cs, r_tb], writes=[r_tb])
            S.add(POOL, COPY(utail[:, :, c], up[:, NP:NP + 2]), reads=[r_u[3]], writes=[r_ut])
            S.add(POOL, COPY(ustail[:, :, c], us[:, NS:NS + 2]), reads=[r_u[4]], writes=[r_ut])
            S.add(POOL, TT(tbuf[:, :], tbuf[:, :], bg[:, :], ALU.mult), reads=[r_tb] + r_bg, writes=[r_tb])
            S.add(POOL, TT(aC[:, c, :], tbuf[:, :], scg[:, :], ALU.mult), reads=[r_tb] + r_sc, writes=r_aC[c])
        d_ut = S.dmasem("utail")
        for j in range(2):
            S.add(SP, DMA(O["cv_p"][j:j + 1, :].rearrange("o (c p) -> p (o c)", p=128), utail[:, j, :],
                          allow_slow_non_contiguous=True), reads=[r_ut], dmasem=d_ut)
            S.add(SP, DMA(O["cv_s"][j:j + 1, :].rearrange("o (c p) -> p (o c)", p=128), ustail[:, j, :],
                          allow_slow_non_contiguous=True), reads=[r_ut], dmasem=d_ut)
        A.release(mk3b)
        A.release_hi(chi)
        mhi = A.mark_hi()
        r_mw = [Res("mw0"), Res("mw1")]
        mwb = [A.alloc(f"mwb{i}", [128, 16, 512], BF16, [r_mw[i]], high=True) for i in range(2)]
        d_mw = [S.dmasem("mw0"), S.dmasem("mw1")]
        r_memb = Res("memb")
        memb = A.alloc("memb", [128, 2, D], BF16, [r_memb], high=True)
        d_memb = S.dmasem("memb")

        def load_mkv(nb):
            s = nb % 2
            S.add(POOL, DMA(mwb[s][:], I["w_mem_kv"][:, nb * 512:(nb + 1) * 512].rearrange("(c p) n -> p c n", p=128)),
                  writes=[r_mw[s]], dmasem=d_mw[s])

        def early_mem_a():
            S.add(POOL, DMA(memb[:].rearrange("p c (a b) -> p c a b", b=1024),
                            I["mem"].rearrange("(c p) (a b) -> p c a b", p=128, b=1024)),
                  writes=[r_memb], dmasem=d_memb)
            load_mkv(0)
        if upto >= 5:
            combine(aC, lambda c, tb: r_aC[c][tb], 12, I["w_conv_out"], 1, False,
                    hooks={5: early_mem_a, 6: lambda: load_mkv(1)})
        A.release(mk3)

    if upto >= 6:
        mk4 = A.mark()
        r_aM = [[Res(f"aM{c}_{tb}") for tb in range(5)] for c in range(8)]
        aM = A.alloc("aM", [128, 8, NT], BF16, [r for l in r_aM for r in l])
        mk4b = A.mark()
        r_mkb = [Res("mkb_p"), Res("mkb_s")]
        mkb = [A.alloc("mkb_p", [128, 2, 2 * MW], BF16, [r_mkb[0]]), A.alloc("mkb_s", [128, 2, 2 * MW], BF16, [r_mkb[1]])]
        d_cm = S.dmasem("cmem")
        r_mkT = [Res("mkT_p"), Res("mkT_s")]
        mkT = [A.alloc("mkT_p", [128, 8, MT], BF16, [r_mkT[0]]), A.alloc("mkT_s", [128, 8, MT], BF16, [r_mkT[1]])]
        mk4c = A.mark()
        r_memT = Res("memT")
        memT = A.alloc("memT", [128, 16, MT], BF16, [r_memT])
        r_m32 = [Res("m32_0"), Res("m32_1")]
        m32 = [A.alloc(f"m32_{i}", [128, 512], F32, [r_m32[i]]) for i in range(2)]
        d_m32 = [S.dmasem("m32_0"), S.dmasem("m32_1")]

        S.add(POOL, DMA(mkb[1][:, :, 0:MW], I["cmk"].rearrange("(c p) n -> p c n", p=128)), writes=[r_mkb[1]], dmasem=d_cm)
        S.add(POOL, DMA(mkb[1][:, :, MW:2 * MW], I["cmv"].rearrange("(c p) n -> p c n", p=128)), writes=[r_mkb[1]], dmasem=d_cm)
        for mc in range(2):
            for g in range(4):
                bk = pj[0]; pj[0] ^= 1
                bv = banks[bk]
                for j in range(4):
                    c = g * 4 + j
                    S.add(PE, TRM(bv[:, j * 128:(j + 1) * 128], memb[:, mc, c * 128:(c + 1) * 128], identb[:, :]),
                          reads=[r_memb, r_const], writes=[bankr[bk]])
                S.add(ACT, COPY(memT[:, g * 4:g * 4 + 4, mc * 128:(mc + 1) * 128],
                                bv[:, 0:512].rearrange("p (a b) -> p a b", b=128)), reads=[bankr[bk]], writes=[r_memT])

        def load_mq(h, s):
            S.add(POOL, DMA(mwb[s][:, :, 0:256], I["w_in"][:, OFF_MQ + h * 256:OFF_MQ + (h + 1) * 256]
                            .rearrange("(c p) n -> p c n", p=128)), writes=[r_mw[s]], dmasem=d_mw[s])
            S.add(POOL, DMA(mwb[s][:, :, 256:512], I["w_in"][:, OFF_MG + h * 256:OFF_MG + (h + 1) * 256]
                            .rearrange("(c p) n -> p c n", p=128)), writes=[r_mw[s]], dmasem=d_mw[s])
        it = 0
        for nb in range(4):
            if 1 <= nb + 1 < 4 and nb >= 1:
                load_mkv(nb + 1)
            elif nb == 3:
                load_mq(0, 0)
            s = nb % 2
            for mc in range(2):
                bk = pj[0]; pj[0] ^= 1
                p2 = it % 2; it += 1
                for c in range(16):
                    S.add(PE, MM(banks[bk][:, :], memT[:, c, mc * 128:(mc + 1) * 128], mwb[s][:, c, :], c == 0, c == 15),
                          reads=[r_memT, r_mw[s]], writes=[bankr[bk]])
                S.add(ACT, COPY(m32[p2][:], banks[bk][:, :]), reads=[bankr[bk]], writes=[r_m32[p2]])
                S.add(POOL, COPY(mkb[0][:, mc, nb * 512:(nb + 1) * 512], m32[p2][:]), reads=[r_m32[p2]], writes=[r_mkb[0]])
                dst = O["mk_p"] if nb < 2 else O["mv_p"]
                S.add(SP, DMA(dst[mc * 128:(mc + 1) * 128, (nb % 2) * 512:(nb % 2 + 1) * 512], m32[p2][:]),
                      reads=[r_m32[p2]], dmasem=d_m32[p2])
        load_mq(1, 1)
        for src in range(2):
            for hd in range(8):
                bk = pj[0]; pj[0] ^= 1
                bv = banks[bk]
                for mc in range(2):
                    S.add(PE, TRM(bv[:, mc * 128:(mc + 1) * 128], mkb[src][:, mc, hd * 128:(hd + 1) * 128], identb[:, :]),
                          reads=[r_mkb[src], r_const], writes=[bankr[bk]])
                S.add(ACT, COPY(mkT[src][:, hd, :], bv[:, 0:256]), reads=[bankr[bk]], writes=[r_mkT[src]])

        A.release(mk4c)
        r_mq = [Res(f"mq{i}") for i in range(5)]
        mqT = A.alloc("mqT", [128, 2, NT], BF16, r_mq)
        r_sgm = [Res(f"sgm{i}") for i in range(5)]
        sgm = A.alloc("sgm", [128, 2, NT], F32, r_sgm)
        r_mpt = [Res("mpt0"), Res("mpt1"), Res("mpt2"), Res("mpt3")]
        mpt = [A.alloc(f"mpt{i}", [128, 512], BF16, [r_mpt[i]]) for i in range(4)]
        r_mrl = Res("mrl")
        mrl = A.alloc("mrl", [128, 512], F32, [r_mrl])
        r_mon = [Res("mon0"), Res("mon1")]
        mon = [A.alloc(f"mon{i}", [128, 512], F32, [r_mon[i]]) for i in range(2)]

        def mproj(h, tb):
            s = h % 2
            t0, tn = TBLK[tb]
            for comp in range(4):
                bk = pj[0]; pj[0] ^= 1
                for c in range(16):
                    S.add(PE, MM(banks[bk][:, 0:tn], mwb[s][:, c, comp * 128:(comp + 1) * 128], xT[:, c, t0:t0 + tn],
                                 c == 0, c == 15), reads=[r_mw[s]] + xT_res(t0, tn), writes=[bankr[bk]])
                if comp < 2:
                    S.add(ACT, COPY(mqT[:, comp, t0:t0 + tn], banks[bk][:, 0:tn]), reads=[bankr[bk]], writes=[r_mq[tb]])
                else:
                    S.add(ACT, ACTF(sgm[:, comp - 2, t0:t0 + tn], banks[bk][:, 0:tn], AF.Silu),
                          reads=[bankr[bk]], writes=[r_sgm[tb]])

        units = [(h, tb) for h in range(4) for tb in range(5)]
        mproj(0, 0)
        for ui, (h, tb) in enumerate(units):
            t0, tn = TBLK[tb]
            src = 0 if tb < 4 else 1
            pts = []
            for mc in range(2):
                sb_ = 2 + mc
                p = pti[0] % 4; pti[0] += 1
                pts.append(p)
                for dc in range(2):
                    S.add(PE, MM(banks[sb_][:, 0:tn], mkT[src][:, h * 2 + dc, mc * 128:(mc + 1) * 128],
                                 mqT[:, dc, t0:t0 + tn], dc == 0, dc == 1),
                          reads=[r_mkT[src], r_mq[tb]], writes=[bankr[sb_]])
                S.add(ACT, ACTF(mpt[p][:, 0:tn], banks[sb_][:, 0:tn], AF.Exp, scale=MSCALE),
                      reads=[bankr[sb_]], writes=[r_mpt[p]])
            if ui + 1 < len(units):
                nh, ntb = units[ui + 1]
                mproj(nh, ntb)
                if ntb == 4 and nh + 2 < 4:
                    load_mq(nh + 2, nh % 2)
            for mc in range(2):
                S.add(PE, MM(banks[6][:, 0:tn], onesb[:, :], mpt[pts[mc]][:, 0:tn], mc == 0, mc == 1),
                      reads=[r_mpt[pts[mc]], r_const], writes=[bankr[6]])
            for dc in range(2):
                for mc in range(2):
                    S.add(PE, MM(banks[4 + dc][:, 0:tn],
                                 mkb[src][:, mc, MW + h * 256 + dc * 128:MW + h * 256 + (dc + 1) * 128],
                                 mpt[pts[mc]][:, 0:tn], mc == 0, mc == 1),
                          reads=[r_mpt[pts[mc]], r_mkb[src]], writes=[bankr[4 + dc]])
            S.add(DVE, lambda e, tn=tn: e.reciprocal(out=mrl[:, 0:tn], in_=banks[6][:, 0:tn]),
                  reads=[bankr[6]], writes=[r_mrl])
            for dc in range(2):
                S.add(DVE, TT(mon[dc][:, 0:tn], banks[4 + dc][:, 0:tn], mrl[:, 0:tn], ALU.mult),
                      reads=[bankr[4 + dc], r_mrl], writes=[r_mon[dc]])
                S.add(POOL, TT(aM[:, h * 2 + dc, t0:t0 + tn], mon[dc][:, 0:tn], sgm[:, dc, t0:t0 + tn], ALU.mult),
                      reads=[r_mon[dc], r_sgm[tb]], writes=[r_aM[h * 2 + dc][tb]])
        A.release(mk4b)
        A.release_hi(mhi)
        r_wob = Res("wob")
        wob = A.alloc("wob", [128, 16, D], BF16, [r_wob], high=True)
        d_wob = S.dmasem("wob")
        def wob_load(q4):
            return lambda: S.add(POOL, DMA(wob[:, :, q4 * 512:(q4 + 1) * 512],
                                           I["w_o"][:, q4 * 512:(q4 + 1) * 512].rearrange("(c p) n -> p c n", p=128)),
                                 writes=[r_wob], dmasem=d_wob)
        if upto >= 7:
            combine(aM, lambda c, tb: r_aM[c][tb], 8, I["w_mem_out"], 2, False,
                    hooks={1: wob_load(0), 3: wob_load(1), 5: wob_load(2), 7: wob_load(3)}, last=True)
        A.release(mk4)

    if upto >= 8:
        A.release(mkX)
        r_mT = [Res(f"mT{i}") for i in range(17)]
        mT = A.alloc("mT", [128, 16, NT], BF16, r_mT)
        d_mT = [S.dmasem(f"mT{i}") for i in range(5)]
        r_ln = Res("ln")
        lng = A.alloc("lng", [128, D], F32, [r_ln])
        lnb = A.alloc("lnb", [128, D], F32, [r_ln])
        d_ln = S.dmasem("ln")
        NR = 3
        r_x32 = [Res(f"x32_{i}") for i in range(NR)]
        x32 = [A.alloc(f"x32_{i}", [128, D], F32, [r_x32[i]]) for i in range(NR)]
        d_x32 = [S.dmasem(f"x32_{i}") for i in range(NR)]
        r_rr = [Res(f"rr{i}") for i in range(NR)]
        rr_ = [A.alloc(f"rr{i}", [128, D], F32, [r_rr[i]]) for i in range(NR)]
        d_yo = [S.dmasem(f"yo{i}") for i in range(NR)]
        r_stat = [Res(f"stat{i}") for i in range(NR)]
        stats = [A.alloc(f"stats{i}", [128, 4, 6], F32, [r_stat[i]]) for i in range(NR)]
        mv = [A.alloc(f"mv{i}", [128, 2], F32, [r_stat[i]]) for i in range(NR)]
        rstd = [A.alloc(f"rstd{i}", [128, 1], F32, [r_stat[i]]) for i in range(NR)]
        nmr = [A.alloc(f"nmr{i}", [128, 1], F32, [r_stat[i]]) for i in range(NR)]
        S.add(SP, DMA(lng[:], I["ln_g"].partition_broadcast(128)), writes=[r_ln], dmasem=d_ln)
        S.add(SP, DMA(lnb[:], I["ln_b"].partition_broadcast(128)), writes=[r_ln], dmasem=d_ln)
        for tb, (t0, tn) in enumerate(TBLK):
            for j4 in range(4):
                S.add(SP, DMA(mT[:, j4 * 4:j4 * 4 + 4, t0:t0 + tn],
                              m_scrb[j4 * 4:j4 * 4 + 4, :, t0:t0 + tn].rearrange("j p t -> p j t")),
                      reads=[r_mscr[j][tb] for j in range(j4 * 4, j4 * 4 + 4)],
                      writes=[r_mT[i] for i, (c0, cn) in enumerate(TCH) if t0 <= c0 < t0 + tn], dmasem=d_mT[tb])
        hb = [0]

        def load_x32(tc):
            t0, tn = TCH[tc]
            src = I["xp"][t0:t0 + tn, :] if tc < 16 else I["xs"][:, :]
            S.add(SP, DMA(x32[tc % NR][0:tn, :], src), writes=[r_x32[tc % NR]], dmasem=d_x32[tc % NR])
        for tc in range(NR - 1):
            load_x32(tc)
        for tc, (t0, tn) in enumerate(TCH):
            s = tc % NR
            if tc + NR - 1 < 17:
                load_x32(tc + NR - 1)
            for db in range(4):
                bk = hb[0] % 8; hb[0] += 1
                for c in range(16):
                    S.add(PE, MM(banks[bk][0:tn, :], mT[:, c, t0:t0 + tn], wob[:, c, db * 512:(db + 1) * 512],
                                 c == 0, c == 15), reads=[r_mT[tc], r_wob], writes=[bankr[bk]])
                S.add(DVE, STT(rr_[s][0:tn, db * 512:(db + 1) * 512], x32[s][0:tn, db * 512:(db + 1) * 512], DN_ALPHA,
                               banks[bk][0:tn, :], ALU.mult, ALU.add),
                      reads=[bankr[bk], r_x32[s]], writes=[r_rr[s]])
                S.add(DVE, lambda e, s=s, tn=tn, db=db: e.bn_stats(out=stats[s][0:tn, db, :], in_=rr_[s][0:tn, db * 512:(db + 1) * 512]),
                      reads=[r_rr[s]], writes=[r_stat[s]])
            S.add(DVE, lambda e, s=s, tn=tn: e.bn_aggr(out=mv[s][0:tn, :], in_=stats[s][0:tn, :, :].rearrange("p a b -> p (a b)")),
                  reads=[r_stat[s]], writes=[r_stat[s]])
            S.add(ACT, ACTF(rstd[s][0:tn, :], mv[s][0:tn, 1:2], AF.Sqrt, bias=epsb[0:tn, 0:1]),
                  reads=[r_stat[s], r_const], writes=[r_stat[s]])
            S.add(DVE, lambda e, s=s, tn=tn: e.reciprocal(out=rstd[s][0:tn, :], in_=rstd[s][0:tn, :]),
                  reads=[r_stat[s]], writes=[r_stat[s]])
            S.add(DVE, STT(nmr[s][0:tn, :], mv[s][0:tn, 0:1], -1.0, rstd[s][0:tn, :], ALU.mult, ALU.mult),
                  reads=[r_stat[s]], writes=[r_stat[s]])
            S.add(ACT, ACTF(rr_[s][0:tn, :], rr_[s][0:tn, :], AF.Identity, bias=nmr[s][0:tn, 0:1], scale=rstd[s][0:tn, 0:1]),
                  reads=[r_rr[s], r_stat[s]], writes=[r_rr[s]])
            S.add(DVE, TT(rr_[s][0:tn, :], rr_[s][0:tn, :], lng[0:tn, :], ALU.mult), reads=[r_rr[s], r_ln], writes=[r_rr[s]])
            S.add(POOL, TT(rr_[s][0:tn, :], rr_[s][0:tn, :], lnb[0:tn, :], ALU.add), reads=[r_rr[s], r_ln], writes=[r_rr[s]])
            dst = O["y_p"][t0:t0 + tn, :] if tc < 16 else O["y_s"][:, :]
            S.add(SP, DMA(dst, rr_[s][0:tn, :]), reads=[r_rr[s]], dmasem=d_yo[s])

    import contextlib
    es = contextlib.ExitStack()
    with nc.Block() as block:
        S.emit(block, es)
    return nc


def make_in_maps(inputs):
    f = lambda a: np.ascontiguousarray(np.asarray(a, dtype=np.float32))
    identb = np.eye(128, dtype=np.float32).astype(ml_dtypes.bfloat16)
    identf = np.eye(128, dtype=np.float32)
    onesb = np.ones((128, 128), dtype=np.float32).astype(ml_dtypes.bfloat16)
    kk = np.arange(128)[:, None]
    qq = np.arange(128)[None, :]
    maskneg = np.where(qq >= kk, 0.0, NEG).astype(np.float32)
    shared = {
        "w_in": f(inputs["w_in"][0]), "fox_bf": f(inputs["fox_bf"][0]).reshape(H, 1),
        "conv_w": f(inputs["conv_w"][0]), "conv_b": f(inputs["conv_b"][0]).reshape(FW, 1),
        "w_mem_kv": f(inputs["w_mem_kv"][0]), "w_fox_out": f(inputs["w_fox_out"][0]),
        "w_conv_out": f(inputs["w_conv_out"][0]), "w_mem_out": f(inputs["w_mem_out"][0]),
        "w_merge": f(inputs["w_merge"][0]), "b_merge": f(inputs["b_merge"][0]).reshape(3 * D, 1),
        "w_o": f(inputs["w_o"][0]), "ln_g": f(inputs["ln_g"][0]).reshape(1, D),
        "ln_b": f(inputs["ln_b"][0]).reshape(1, D),
        "identb": identb, "identf": identf, "onesb": onesb, "maskneg": maskneg,
        "masknegb": maskneg.astype(ml_dtypes.bfloat16),
    }
    maps = []
    for b in range(8):
        m = dict(shared)
        m["xp"] = f(inputs["x_prompt"][b])
        m["xs"] = f(inputs["x_sample"][b])
        m["mem"] = f(inputs["mem_prompt"][b])
        m["ck"] = f(inputs["cache_fox_k"][0, b]).reshape(PAST, FW)
        m["cv"] = f(inputs["cache_fox_v"][0, b]).reshape(PAST, FW)
        m["clf"] = f(inputs["cache_fox_logf"][0, b])
        m["sconv"] = f(inputs["state_conv"][0, b])
        m["cmk"] = f(inputs["cache_mem_k"][0, b]).reshape(MT, MW)
        m["cmv"] = f(inputs["cache_mem_v"][0, b]).reshape(MT, MW)
        maps.append(m)
    return maps


_NC_CACHE = {}


def kernel(**inputs):
    if "nc" not in _NC_CACHE:
        _NC_CACHE["nc"] = build()
    nc = _NC_CACHE["nc"]
    maps = make_in_maps(inputs)
    res = run_bass_kernel_spmd(nc, maps, core_ids=list(range(8)))
    R = res.results
    st = lambda n: np.stack([np.asarray(R[b][n], dtype=np.float32) for b in range(8)])
    return (
        st("y_p"), st("y_s"),
        st("fk_p").reshape(1, 8, NP, H, DH), st("fv_p").reshape(1, 8, NP, H, DH), st("fl_p").reshape(1, 8, NP, H),
        st("cv_p").reshape(1, 8, 2, FW), st("mk_p").reshape(1, 8, MT, 4, 256), st("mv_p").reshape(1, 8, MT, 4, 256),
        st("fk_s").reshape(1, 8, NS, H, DH), st("fv_s").reshape(1, 8, NS, H, DH), st("fl_s").reshape(1, 8, NS, H),
        st("cv_s").reshape(1, 8, 2, FW),
    )
```
